# Optimizing a Trainium2 kernel written in Bass

```python
import math
import jax
import jax.numpy as jnp
from jax import lax
import numpy as np

D_MODEL = 1024
BATCH = 8
SEQ = 2048
DEPTH = 4

CTX_LEN = 256
GRID_W = 64
HEAD_DIM = 64
D_MIX = D_MODEL
GROUP_W = D_MIX // 4
N_GROUP_HEADS = GROUP_W // HEAD_DIM
N_DIR = 2
NORM_EPS = 1e-6
N_MOD = 6
GDN_HEADS = N_GROUP_HEADS
GDN_CONV = 5
GDN_CHUNK = 64
SWA_Q_HEADS = N_GROUP_HEADS
SWA_KV_HEADS = 2
SWA_WINDOW = 128
SWA_BLOCK = 128
ROPE_THETA = 10000.0
GMLP_GROUPS = N_GROUP_HEADS
GMLP_CHUNK = 128
MLSTM_HEADS = N_GROUP_HEADS
MLSTM_CHUNK = 64
D_FF = 4 * D_MODEL

IN_SPLITS = (GROUP_W, GROUP_W, GROUP_W, GROUP_W, N_DIR * GDN_HEADS, N_DIR * GDN_HEADS,
             GROUP_W, SWA_KV_HEADS * HEAD_DIM, SWA_KV_HEADS * HEAD_DIM,
             GROUP_W, GROUP_W,
             GROUP_W, GROUP_W, GROUP_W, GROUP_W, N_DIR * MLSTM_HEADS, N_DIR * MLSTM_HEADS)
D_IN = sum(IN_SPLITS)

kernel_name = 'hybrid_parallel_groups_flow_block'


def rmsnorm(x, w):
    xf = x.astype(jnp.float32)
    y = xf * lax.rsqrt(jnp.mean(xf * xf, axis=-1, keepdims=True) + NORM_EPS)
    return (y * w.astype(jnp.float32)).astype(x.dtype)


def l2norm(x):
    xf = x.astype(jnp.float32)
    return xf * lax.rsqrt(jnp.sum(xf * xf, axis=-1, keepdims=True) + NORM_EPS)


def heads(t, n):
    return t.reshape(t.shape[:-1] + (n, t.shape[-1] // n))


def split_cols(z):
    out, start = [], 0
    for size in IN_SPLITS:
        out.append(z[..., start:start + size])
        start += size
    return out


def modulate(h, shift, scale):
    return h * (1 + scale) + shift


def dwconv_centred(x, w):
    k, ch = w.shape
    return lax.conv_general_dilated(x, w[:, None, :].astype(x.dtype), window_strides=(1,),
                                    padding=[(k // 2, k // 2)], dimension_numbers=('NWC', 'WIO', 'NWC'),
                                    feature_group_count=ch)


def to_chunks(t, chunk):
    bsz, length, h = t.shape[:3]
    t = t.astype(jnp.float32).reshape((bsz, length // chunk, chunk, h) + t.shape[3:])
    return jnp.moveaxis(t, (1, 3), (0, 2))


def from_chunks(t):
    t = jnp.moveaxis(t, (0, 2), (1, 3))
    return t.reshape((t.shape[0], t.shape[1] * t.shape[2]) + t.shape[3:])


def gdn_scan(q, k, v, g, beta, state):
    dk, dv = q.shape[-1], v.shape[-1]
    q = to_chunks(q, GDN_CHUNK) * dk ** -0.5
    k = to_chunks(k, GDN_CHUNK)
    v = to_chunks(v, GDN_CHUNK)
    g = to_chunks(g, GDN_CHUNK)
    beta = to_chunks(beta, GDN_CHUNK)
    idx = jnp.arange(GDN_CHUNK)
    incl = idx[:, None] >= idx[None, :]
    strict = idx[:, None] > idx[None, :]
    gcum = jnp.cumsum(g, axis=-1)
    decay = jnp.exp(jnp.where(incl, gcum[..., :, None] - gcum[..., None, :], -jnp.inf))
    kbeta = k * beta[..., None]
    a = jnp.where(strict, jnp.einsum('nbhik,nbhjk->nbhij', kbeta, k) * decay, 0.0)
    rhs = jnp.concatenate([v * beta[..., None], kbeta * jnp.exp(gcum)[..., None]], axis=-1)
    sol = lax.linalg.triangular_solve(a + jnp.eye(GDN_CHUNK, dtype=jnp.float32), rhs,
                                      left_side=True, lower=True)
    u, w = sol[..., :dv], sol[..., dv:]
    attn = jnp.einsum('nbhik,nbhjk->nbhij', q, k) * decay
    q_dec = q * jnp.exp(gcum)[..., None]
    k_dec = k * jnp.exp(gcum[..., -1:] - gcum)[..., None]
    g_last = jnp.exp(gcum[..., -1])

    def step(s, inp):
        u_c, w_c, attn_c, q_c, k_c, gl_c = inp
        v_new = u_c - jnp.einsum('bhck,bhkv->bhcv', w_c, s)
        o = jnp.einsum('bhck,bhkv->bhcv', q_c, s) + jnp.einsum('bhij,bhjv->bhiv', attn_c, v_new)
        s = s * gl_c[..., None, None] + jnp.einsum('bhck,bhcv->bhkv', k_c, v_new)
        return s, o

    state, o = lax.scan(step, state, (u, w, attn, q_dec, k_dec, g_last))
    return from_chunks(o), state


def mlstm_scan(q, k, v, ig, fg, state):
    dk = q.shape[-1]
    q = to_chunks(q, MLSTM_CHUNK) * dk ** -0.5
    k = to_chunks(k, MLSTM_CHUNK)
    v = to_chunks(v, MLSTM_CHUNK)
    ig = to_chunks(ig, MLSTM_CHUNK)
    logf = jax.nn.log_sigmoid(to_chunks(fg, MLSTM_CHUNK))
    idx = jnp.arange(MLSTM_CHUNK)
    incl = idx[:, None] >= idx[None, :]
    b = jnp.cumsum(logf, axis=-1)
    dmat = jnp.where(incl, b[..., :, None] - b[..., None, :] + ig[..., None, :], -jnp.inf)
    dmax = jnp.max(dmat, axis=-1)
    qk = jnp.einsum('nbhik,nbhjk->nbhij', q, k)
    b_last = b[..., -1]
    w_state = b[..., -1:] - b + ig
    w_state_max = jnp.max(w_state, axis=-1)

    def step(carry, inp):
        c_mem, n_vec, m = carry
        q_c, k_c, v_c, b_c, d_c, dmax_c, qk_c, ws_c, wsmax_c, bl_c = inp
        inter = b_c + m[..., None]
        m_t = jnp.maximum(inter, dmax_c)
        s = qk_c * jnp.exp(d_c - m_t[..., None])
        w_inter = jnp.exp(inter - m_t)
        num = (w_inter[..., None] * jnp.einsum('bhck,bhkv->bhcv', q_c, c_mem)
               + jnp.einsum('bhij,bhjv->bhiv', s, v_c))
        den = w_inter * jnp.einsum('bhck,bhk->bhc', q_c, n_vec) + jnp.sum(s, axis=-1)
        h = num / jnp.maximum(jnp.abs(den), jnp.exp(-m_t))[..., None]
        m_new = jnp.maximum(bl_c + m, wsmax_c)
        carry_decay = jnp.exp(bl_c + m - m_new)
        k_w = k_c * jnp.exp(ws_c - m_new[..., None])[..., None]
        c_mem = carry_decay[..., None, None] * c_mem + jnp.einsum('bhck,bhcv->bhkv', k_w, v_c)
        n_vec = carry_decay[..., None] * n_vec + jnp.sum(k_w, axis=-2)
        return (c_mem, n_vec, m_new), h

    state, h = lax.scan(step, state, (q, k, v, b, dmat, dmax, qk, w_state, w_state_max, b_last))
    return from_chunks(h), state


def bidirectional_prefixed(scan_fn, ctx_seq, ctx_gates, lat_seq, lat_gates, state0):
    out_x, out_c = 0.0, 0.0
    for d in range(N_DIR):
        flip = (lambda t: jnp.flip(t, axis=1)) if d == 1 else (lambda t: t)
        c_args = [flip(t) for t in ctx_seq] + [flip(gt[:, :, d]) for gt in ctx_gates]
        x_args = [flip(t) for t in lat_seq] + [flip(gt[:, :, d]) for gt in lat_gates]
        o_c, ctx_state = scan_fn(*c_args, state0)
        o_x, _ = scan_fn(*x_args, ctx_state)
        out_c = out_c + flip(o_c)
        out_x = out_x + flip(o_x)
    return out_x, out_c


def gdn_mixer(lat, cx, conv_w, a_log, dt_bias, norm_w):
    dtype = lat[0].dtype

    def branch(q, k, v, a, b):
        qkv = jax.nn.silu(dwconv_centred(jnp.concatenate([q, k, v], axis=-1), conv_w))
        q, k, v = jnp.split(qkv, 3, axis=-1)
        g = -jnp.exp(a_log.astype(jnp.float32)) * jax.nn.softplus(
            heads(a.astype(jnp.float32), N_DIR) + dt_bias.astype(jnp.float32))
        beta = jax.nn.sigmoid(heads(b.astype(jnp.float32), N_DIR))
        seq = (l2norm(heads(q, GDN_HEADS)), l2norm(heads(k, GDN_HEADS)), heads(v, GDN_HEADS))
        return seq, (g, beta)

    (sx, gx), (sc, gcx) = branch(*lat[:3], *lat[4:]), branch(*cx[:3], *cx[4:])
    state0 = jnp.zeros((lat[0].shape[0], GDN_HEADS, HEAD_DIM, HEAD_DIM), jnp.float32)
    ox, oc = bidirectional_prefixed(gdn_scan, sc, gcx, sx, gx, state0)

    def finish(o, z):
        o = rmsnorm(o, norm_w) * jax.nn.silu(heads(z.astype(jnp.float32), GDN_HEADS))
        return o.reshape(o.shape[:2] + (GROUP_W,)).astype(dtype)

    return finish(ox, lat[3]), finish(oc, cx[3])


def axial_rope(length, dtype):
    rows = length // GRID_W
    row = jnp.repeat(jnp.arange(rows), GRID_W).astype(jnp.float32)
    col = (jnp.arange(rows * GRID_W) % GRID_W).astype(jnp.float32)
    n_freq = HEAD_DIM // 4
    inv = jnp.power(ROPE_THETA, -jnp.arange(n_freq, dtype=jnp.float32) / n_freq)
    ang = jnp.concatenate([row[:, None] * inv, col[:, None] * inv], axis=-1)
    return jnp.cos(ang)[:, None, :].astype(dtype), jnp.sin(ang)[:, None, :].astype(dtype)


def apply_rope(t, cos, sin):
    half = t.shape[-1] // 2
    t1, t2 = t[..., :half], t[..., half:]
    return jnp.concatenate([t1 * cos - t2 * sin, t1 * sin + t2 * cos], axis=-1)


def sink_softmax(sink_logit, scores):
    sk = jnp.broadcast_to(sink_logit, scores.shape[:-1] + (1,))
    return jax.nn.softmax(jnp.concatenate([sk, scores], axis=-1), axis=-1)[..., 1:]


def banded_window_attention(q, k, v, kc, vc, sink):
    bsz, length, hq, dh = q.shape
    hkv = k.shape[2]
    grp = hq // hkv
    blk = SWA_BLOCK
    nb = length // blk
    scale = dh ** -0.5
    qb = q.reshape(bsz, nb, blk, hkv, grp, dh)

    def band(t):
        tp = jnp.pad(t, ((0, 0), (blk, blk), (0, 0), (0, 0))).reshape(bsz, nb + 2, blk, hkv, dh)
        return jnp.concatenate([tp[:, :-2], tp[:, 1:-1], tp[:, 2:]], axis=2)

    kb, vb = band(k), band(v)
    s_loc = jnp.einsum('bnqhgd,bnkhd->bnhgqk', qb, kb).astype(jnp.float32) * scale
    s_ctx = jnp.einsum('bnqhgd,bchd->bnhgqc', qb, kc).astype(jnp.float32) * scale
    qi = jnp.arange(blk)[:, None]
    kj = jnp.arange(3 * blk)[None, :]
    kpos = jnp.arange(nb)[:, None, None] * blk + kj[None] - blk
    mask = (jnp.abs(kj - blk - qi) <= SWA_WINDOW)[None] & (kpos >= 0) & (kpos < length)
    s_loc = jnp.where(mask[None, :, None, None], s_loc, -jnp.inf)
    sk = sink.astype(jnp.float32).reshape(hkv, grp)[None, None, :, :, None, None]
    p = sink_softmax(sk, jnp.concatenate([s_loc, s_ctx], axis=-1)).astype(v.dtype)
    p_loc, p_ctx = p[..., :3 * blk], p[..., 3 * blk:]
    o = (jnp.einsum('bnhgqk,bnkhd->bnqhgd', p_loc, vb)
         + jnp.einsum('bnhgqc,bchd->bnqhgd', p_ctx, vc))
    return o.reshape(bsz, length, hq * dh)


def context_attention(qc, kc, vc, sink):
    bsz, lc, hq, dh = qc.shape
    hkv = kc.shape[2]
    grp = hq // hkv
    q = qc.reshape(bsz, lc, hkv, grp, dh)
    s = jnp.einsum('bqhgd,bkhd->bhgqk', q, kc).astype(jnp.float32) * dh ** -0.5
    sk = sink.astype(jnp.float32).reshape(hkv, grp)[None, :, :, None, None]
    p = sink_softmax(sk, s).astype(vc.dtype)
    return jnp.einsum('bhgqk,bkhd->bqhgd', p, vc).reshape(bsz, lc, hq * dh)


def swa_mixer(lat, cx, sink):
    qx, kx, vx = lat
    qc, kc, vc = cx
    cos, sin = axial_rope(qx.shape[1], qx.dtype)
    qx = apply_rope(heads(qx, SWA_Q_HEADS), cos, sin)
    kx = apply_rope(heads(kx, SWA_KV_HEADS), cos, sin)
    kc, vc = heads(kc, SWA_KV_HEADS), heads(vc, SWA_KV_HEADS)
    o_x = banded_window_attention(qx, kx, heads(vx, SWA_KV_HEADS), kc, vc, sink)
    o_c = context_attention(heads(qc, SWA_Q_HEADS), kc, vc, sink)
    return o_x, o_c


def gmlp_mixer(u, v, w_s, b_s, norm_w):
    bsz, length, _ = u.shape
    n = length // GMLP_CHUNK
    u = jax.nn.gelu(u)
    v = rmsnorm(jax.nn.gelu(v), norm_w)
    vb = v.reshape(bsz, n, GMLP_CHUNK, GMLP_GROUPS, GROUP_W // GMLP_GROUPS)
    mixed = jnp.einsum('gpq,bnqgd->bnpgd', w_s, vb) + b_s.T[None, None, :, :, None]
    return u * mixed.reshape(bsz, length, GROUP_W)


def mlstm_mixer(lat, cx, ig_bias, fg_bias, norm_w):
    dtype = lat[0].dtype

    def branch(q, k, v, i, f):
        seq = (heads(q, MLSTM_HEADS), heads(k, MLSTM_HEADS), heads(v, MLSTM_HEADS))
        gates = (heads(i.astype(jnp.float32), N_DIR) + ig_bias.astype(jnp.float32),
                 heads(f.astype(jnp.float32), N_DIR) + fg_bias.astype(jnp.float32))
        return seq, gates

    (sx, gx), (sc, gcx) = branch(*lat[:3], *lat[4:]), branch(*cx[:3], *cx[4:])
    bsz = lat[0].shape[0]
    state0 = (jnp.zeros((bsz, MLSTM_HEADS, HEAD_DIM, HEAD_DIM), jnp.float32),
              jnp.zeros((bsz, MLSTM_HEADS, HEAD_DIM), jnp.float32),
              jnp.zeros((bsz, MLSTM_HEADS), jnp.float32))
    hx, hc = bidirectional_prefixed(mlstm_scan, sc, gcx, sx, gx, state0)

    def finish(h, o):
        h = rmsnorm(h, norm_w.reshape(MLSTM_HEADS, HEAD_DIM)).reshape(h.shape[:2] + (GROUP_W,))
        return (jax.nn.sigmoid(o.astype(jnp.float32)) * h).astype(dtype)

    return finish(hx, lat[3]), finish(hc, cx[3])


def token_mixing(hx, hc, w_in, gdn_conv_w, gdn_a_log, gdn_dt_bias, gdn_norm_w, swa_sink,
                 gmlp_w_s, gmlp_b_s, gmlp_norm_w, mlstm_ig_bias, mlstm_fg_bias, mlstm_norm_w):
    zx = split_cols(hx @ w_in)
    zc = split_cols(hc @ w_in)
    a_x, a_c = gdn_mixer(zx[0:6], zc[0:6], gdn_conv_w, gdn_a_log, gdn_dt_bias, gdn_norm_w)
    b_x, b_c = swa_mixer(zx[6:9], zc[6:9], swa_sink)
    c_x = gmlp_mixer(zx[9], zx[10], gmlp_w_s, gmlp_b_s, gmlp_norm_w)
    c_c = gmlp_mixer(zc[9], zc[10], gmlp_w_s, gmlp_b_s, gmlp_norm_w)
    d_x, d_c = mlstm_mixer(zx[11:17], zc[11:17], mlstm_ig_bias, mlstm_fg_bias, mlstm_norm_w)
    mix_x = jnp.concatenate([a_x, b_x, c_x, d_x], axis=-1)
    mix_c = jnp.concatenate([a_c, b_c, c_c, d_c], axis=-1)
    return mix_x, mix_c


def channel_mlp(h, w1, w2):
    return jnp.square(jax.nn.relu(h @ w1)) @ w2


def setup_inputs(seed: int = 0) -> dict:
    key = jax.random.key(seed)
    ks = iter(jax.random.split(key, 32))
    f32 = jnp.float32

    def nrm(shape, s):
        return jax.random.normal(next(ks), shape, f32) * s

    x = nrm((BATCH, SEQ, D_MODEL), 1.0)
    c = nrm((BATCH, D_MODEL), 1.0)
    ctx = nrm((BATCH, CTX_LEN, D_MODEL), 1.0)
    c_ctx = nrm((D_MODEL,), 1.0)
    ada_w = nrm((DEPTH, D_MODEL, N_MOD * D_MODEL), 0.5 * D_MODEL ** -0.5)
    ada_b = nrm((DEPTH, N_MOD * D_MODEL), 0.02)
    norm1_w = 1.0 + nrm((DEPTH, D_MODEL), 0.02)
    norm2_w = 1.0 + nrm((DEPTH, D_MODEL), 0.02)
    w_in = nrm((DEPTH, D_MODEL, D_IN), D_MODEL ** -0.5)
    w_out = nrm((DEPTH, D_MIX, D_MODEL), D_MIX ** -0.5)
    gdn_conv_w = nrm((DEPTH, GDN_CONV, 3 * GROUP_W), GDN_CONV ** -0.5)
    gdn_a_log = jnp.log(jax.random.uniform(next(ks), (DEPTH, N_DIR, GDN_HEADS), f32, 1.0, 16.0))
    dt = jnp.exp(jax.random.uniform(next(ks), (DEPTH, N_DIR, GDN_HEADS), f32,
                                    math.log(1e-3), math.log(1e-1)))
    gdn_dt_bias = dt + jnp.log(-jnp.expm1(-dt))
    gdn_norm_w = 1.0 + nrm((DEPTH, HEAD_DIM), 0.02)
    swa_sink = nrm((DEPTH, SWA_Q_HEADS), 1.0)
    gmlp_w_s = nrm((DEPTH, GMLP_GROUPS, GMLP_CHUNK, GMLP_CHUNK), GMLP_CHUNK ** -0.5)
    gmlp_b_s = 1.0 + nrm((DEPTH, GMLP_GROUPS, GMLP_CHUNK), 0.02)
    gmlp_norm_w = 1.0 + nrm((DEPTH, GROUP_W), 0.02)
    mlstm_ig_bias = nrm((DEPTH, N_DIR, MLSTM_HEADS), 0.1)
    mlstm_fg_bias = 3.0 + 3.0 * jax.random.uniform(next(ks), (DEPTH, N_DIR, MLSTM_HEADS), f32)
    mlstm_norm_w = 1.0 + nrm((DEPTH, GROUP_W), 0.02)
    mlp_w1 = nrm((DEPTH, D_MODEL, D_FF), D_MODEL ** -0.5)
    mlp_w2 = nrm((DEPTH, D_FF, D_MODEL), D_FF ** -0.5)
    final_norm_w = 1.0 + nrm((D_MODEL,), 0.02)
    return {'x': x, 'c': c, 'ctx': ctx, 'c_ctx': c_ctx, 'ada_w': ada_w, 'ada_b': ada_b,
            'norm1_w': norm1_w, 'norm2_w': norm2_w, 'w_in': w_in, 'w_out': w_out,
            'gdn_conv_w': gdn_conv_w, 'gdn_a_log': gdn_a_log, 'gdn_dt_bias': gdn_dt_bias,
            'gdn_norm_w': gdn_norm_w, 'swa_sink': swa_sink, 'gmlp_w_s': gmlp_w_s, 'gmlp_b_s': gmlp_b_s,
            'gmlp_norm_w': gmlp_norm_w, 'mlstm_ig_bias': mlstm_ig_bias, 'mlstm_fg_bias': mlstm_fg_bias,
            'mlstm_norm_w': mlstm_norm_w, 'mlp_w1': mlp_w1, 'mlp_w2': mlp_w2, 'final_norm_w': final_norm_w}


def reference(x, c, ctx, c_ctx, ada_w, ada_b, norm1_w, norm2_w, w_in, w_out, gdn_conv_w, gdn_a_log,
              gdn_dt_bias, gdn_norm_w, swa_sink, gmlp_w_s, gmlp_b_s, gmlp_norm_w, mlstm_ig_bias,
              mlstm_fg_bias, mlstm_norm_w, mlp_w1, mlp_w2, final_norm_w):
    for l in range(DEPTH):
        last = l == DEPTH - 1
        mod_x = jnp.split((jax.nn.silu(c) @ ada_w[l] + ada_b[l])[:, None, :], N_MOD, axis=-1)
        mod_c = jnp.split((jax.nn.silu(c_ctx) @ ada_w[l] + ada_b[l])[None, None, :], N_MOD, axis=-1)
        hx = modulate(rmsnorm(x, norm1_w[l]), mod_x[0], mod_x[1])
        hc = modulate(rmsnorm(ctx, norm1_w[l]), mod_c[0], mod_c[1])
        mix_x, mix_c = token_mixing(hx, hc, w_in[l], gdn_conv_w[l], gdn_a_log[l], gdn_dt_bias[l],
                                    gdn_norm_w[l], swa_sink[l], gmlp_w_s[l], gmlp_b_s[l], gmlp_norm_w[l],
                                    mlstm_ig_bias[l], mlstm_fg_bias[l], mlstm_norm_w[l])
        x = x + mod_x[2] * (mix_x @ w_out[l])
        x = x + mod_x[5] * channel_mlp(modulate(rmsnorm(x, norm2_w[l]), mod_x[3], mod_x[4]),
                                       mlp_w1[l], mlp_w2[l])
        if not last:
            ctx = ctx + mod_c[2] * (mix_c @ w_out[l])
            ctx = ctx + mod_c[5] * channel_mlp(modulate(rmsnorm(ctx, norm2_w[l]), mod_c[3], mod_c[4]),
                                               mlp_w1[l], mlp_w2[l])
    return rmsnorm(x, final_norm_w)
```

```python
import math
import numpy as np
from contextlib import ExitStack
import concourse.bass as bass
import concourse.mybir as mybir
from concourse.bass_utils import run_bass_kernel_spmd

F32 = mybir.dt.float32
BF16 = mybir.dt.bfloat16
AF = mybir.ActivationFunctionType
ALU = mybir.AluOpType
AX = mybir.AxisListType

NEG = -30000.0


class Res:
    __slots__ = ("w", "rs", "excl")

    def __init__(self, excl=False):
        self.w = None
        self.rs = []
        self.excl = excl


class V:
    __slots__ = ("ap", "res")

    def __init__(self, ap, res):
        self.ap = ap
        self.res = res

    def __getitem__(self, k):
        return V(self.ap[k], self.res)

    def bc(self, shape):
        return V(self.ap.to_broadcast(list(shape)), self.res)

    def unsq(self, ax):
        return V(self.ap.unsqueeze(ax), self.res)

    def bitcast(self, dt):
        return V(self.ap.bitcast(dt), self.res)

    def rr(self, pat, **kw):
        return V(self.ap.rearrange(pat, **kw), self.res)

    def pbc(self, n):
        return V(self.ap.partition_broadcast(n), self.res)

    @property
    def shape(self):
        return tuple(self.ap.shape)


class BassBackend:
    LIM = 20000
    NSLOT = 12

    def __init__(self, nc, es):
        self.nc = nc
        self.es = es
        self.eng = {"pe": nc.tensor, "act": nc.scalar, "dve": nc.vector, "pool": nc.gpsimd, "sp": nc.sync}
        names = list(self.eng)
        self.sems = {e: [] for e in names}
        self.seq = {e: 0 for e in names}
        self.seen = {e: {f: 0 for f in names} for e in names}
        self.hist = {e: [] for e in names}
        self.dq = {}
        for q in ("sp", "pool", "act"):
            self.dq[q] = {"sems": [es.enter_context(nc.semaphore("d%s%d" % (q, i))) for i in range(self.NSLOT)],
                          "cnt": [0] * self.NSLOT, "next": 0}
        self.dseen = {e: {} for e in names}
        self.n_ins = 0
        self._uid = 0

    def sb(self, name, shape, dtype=F32, scope=None):
        self._uid += 1
        t = (scope or self.es).enter_context(self.nc.sbuf_tensor("%s_%d" % (name, self._uid), list(shape), dtype))
        return V(t[:] if len(shape) == 2 else t[tuple([slice(None)] * len(shape))], (Res(),))

    def ps(self, name, shape, dtype=F32):
        t = self.es.enter_context(self.nc.psum_tensor(name, list(shape), dtype))
        return V(t[tuple([slice(None)] * len(shape))], (Res(excl=True),))

    def dram(self, name, shape, dtype, kind):
        t = self.nc.dram_tensor(name, list(shape), dtype, kind=kind)
        return V(t.ap(), ())

    def _sem(self, e, sq):
        i = (sq - 1) // self.LIM
        while len(self.sems[e]) <= i:
            self.sems[e].append(self.es.enter_context(self.nc.semaphore("s%s%d" % (e, len(self.sems[e])))))
        return self.sems[e][i], (sq - 1) % self.LIM + 1

    def _merge(self, e, snap):
        se = self.seen[e]
        for f, v in snap[0].items():
            if v > se[f]:
                se[f] = v
        de = self.dseen[e]
        for k, v in snap[1].items():
            if v > de.get(k, 0):
                de[k] = v

    def _wait_tok(self, e, tok, skip_same):
        if tok[0] == "e":
            _, f, sq = tok
            if f == e and skip_same:
                return
            if self.seen[e][f] >= sq:
                return
            sem, val = self._sem(f, sq)
            self.eng[e].wait_ge(sem, val)
            self.seen[e][f] = sq
            self._merge(e, self.hist[f][sq - 1])
        else:
            _, q, slot, cnt, snap = tok
            if self.dseen[e].get((q, slot), 0) >= cnt:
                return
            self.eng[e].wait_ge(self.dq[q]["sems"][slot], cnt)
            self.dseen[e][(q, slot)] = cnt
            self._merge(e, snap)

    def _sync(self, e, reads, writes, skip_same=False):
        for r in reads:
            for res in r.res:
                if res.w is not None:
                    self._wait_tok(e, res.w, skip_same)
                if res.excl:
                    for t in res.rs:
                        if t[1] != e:
                            self._wait_tok(e, t, skip_same)
        for w in writes:
            for res in w.res:
                if res.w is not None:
                    self._wait_tok(e, res.w, skip_same)
                for t in res.rs:
                    self._wait_tok(e, t, skip_same)

    def _snap(self, e):
        return (dict(self.seen[e]), dict(self.dseen[e]))

    def _mark(self, tok, e, reads, writes):
        for r in reads:
            for res in r.res:
                res.rs = [t for t in res.rs if not (t[0] == "e" and t[1] == e)] + [tok]
        for w in writes:
            for res in w.res:
                res.w = tok
                res.rs = []

    def _commit(self, e, ins, reads, writes):
        self.seq[e] += 1
        sq = self.seq[e]
        sem, val = self._sem(e, sq)
        ins.then_inc(sem, 1)
        self.hist[e].append(self._snap(e))
        self._mark(("e", e, sq), e, reads, writes)
        self.n_ins += 1

    def dma(self, out, in_, q="sp"):
        e = q
        self._sync(e, [in_], [out])
        d = self.dq[q]
        slot = d["next"]
        d["next"] = (slot + 1) % self.NSLOT
        if d["cnt"][slot] > self.dseen[e].get((q, slot), 0):
            self.eng[e].wait_ge(d["sems"][slot], d["cnt"][slot])
            self.dseen[e][(q, slot)] = d["cnt"][slot]
        ins = self.eng[e].dma_start(out=out.ap, in_=in_.ap)
        d["cnt"][slot] += 16
        ins.then_inc(d["sems"][slot], 16)
        tok = ("d", q, slot, d["cnt"][slot], self._snap(e))
        self._mark(tok, e, [in_], [out])
        self.n_ins += 1

    def mm(self, out, lhsT, rhs, start=True, stop=True):
        self._sync("pe", [lhsT, rhs], [out], skip_same=True)
        ins = self.nc.tensor.matmul(out.ap, lhsT=lhsT.ap, rhs=rhs.ap, start=start, stop=stop)
        self._commit("pe", ins, [lhsT, rhs], [out])

    def tr(self, out, in_, ident):
        self._sync("pe", [in_, ident], [out], skip_same=True)
        ins = self.nc.tensor.transpose(out.ap, in_.ap, ident.ap)
        self._commit("pe", ins, [in_, ident], [out])

    def act(self, out, in_, func, bias=None, scale=1.0, accum=None):
        rd = [in_] + [x for x in (bias, scale) if isinstance(x, V)]
        wr = [out] + ([accum] if accum is not None else [])
        self._sync("act", rd, wr)
        kw = {}
        if bias is not None:
            kw["bias"] = bias.ap if isinstance(bias, V) else float(bias)
        if accum is not None:
            kw["accum_out"] = accum.ap
        ins = self.nc.scalar.activation(out=out.ap, in_=in_.ap, func=func,
                                        scale=(scale.ap if isinstance(scale, V) else float(scale)), **kw)
        self._commit("act", ins, rd, wr)

    def tt(self, out, a, b, op, eng="dve"):
        self._sync(eng, [a, b], [out])
        ins = self.eng[eng].tensor_tensor(out=out.ap, in0=a.ap, in1=b.ap, op=op)
        self._commit(eng, ins, [a, b], [out])

    def ts(self, out, a, s1, op0, s2=None, op1=None, eng="dve"):
        rd = [a] + [x for x in (s1, s2) if isinstance(x, V)]
        self._sync(eng, rd, [out])
        f = lambda x: x.ap if isinstance(x, V) else (None if x is None else float(x))
        if op1 is None:
            ins = self.eng[eng].tensor_scalar(out=out.ap, in0=a.ap, scalar1=f(s1), scalar2=None, op0=op0)
        else:
            ins = self.eng[eng].tensor_scalar(out=out.ap, in0=a.ap, scalar1=f(s1), scalar2=f(s2), op0=op0, op1=op1)
        self._commit(eng, ins, rd, [out])

    def stt(self, out, a, s, b, op0, op1, eng="dve"):
        rd = [a, b] + ([s] if isinstance(s, V) else [])
        self._sync(eng, rd, [out])
        ins = self.eng[eng].scalar_tensor_tensor(out=out.ap, in0=a.ap, scalar=(s.ap if isinstance(s, V) else float(s)),
                                                 in1=b.ap, op0=op0, op1=op1)
        self._commit(eng, ins, rd, [out])

    def red(self, out, in_, op, eng="dve"):
        self._sync(eng, [in_], [out])
        ins = self.eng[eng].tensor_reduce(out=out.ap, in_=in_.ap, axis=AX.X, op=op)
        self._commit(eng, ins, [in_], [out])

    def copy(self, out, in_, eng="dve"):
        if eng == "act":
            return self.act(out, in_, AF.Copy)
        self._sync(eng, [in_], [out])
        ins = self.eng[eng].tensor_copy(out=out.ap, in_=in_.ap)
        self._commit(eng, ins, [in_], [out])

    def memset(self, out, val, eng="dve"):
        self._sync(eng, [], [out])
        ins = self.eng[eng].memset(out.ap, float(val))
        self._commit(eng, ins, [], [out])

    def recip(self, out, in_):
        self._sync("dve", [in_], [out])
        ins = self.nc.vector.reciprocal(out=out.ap, in_=in_.ap)
        self._commit("dve", ins, [in_], [out])

    def finish(self):
        for q, d in self.dq.items():
            for slot in range(self.NSLOT):
                if d["cnt"][slot] > self.dseen["sp"].get((q, slot), 0):
                    self.nc.sync.wait_ge(d["sems"][slot], d["cnt"][slot])


    def barrier(self):
        names = list(self.eng)
        toks = [("e", f, self.seq[f]) for f in names if self.seq[f] > 0]
        for q, d in self.dq.items():
            for slot in range(self.NSLOT):
                if d["cnt"][slot] > 0:
                    toks.append(("d", q, slot, d["cnt"][slot], ({}, {})))
        for e in ("pe", "act", "dve", "pool", "sp"):
            for t in toks:
                self._wait_tok(e, t, False)


def make_ring(tiles):
    ctr = [0]

    def nxt():
        ctr[0] += 1
        return tiles[ctr[0] % len(tiles)]
    return nxt


def lockstep(factories, width):
    it = iter(factories)
    active = []
    free = list(range(width))
    exhausted = False
    while True:
        while free and not exhausted:
            f = next(it, None)
            if f is None:
                exhausted = True
                break
            slot = free.pop(0)
            active.append((slot, f(slot)))
        if not active:
            break
        for item in list(active):
            try:
                next(item[1])
            except StopIteration:
                active.remove(item)
                free.append(item[0])


D = 1024
NT = 18
TOK = 2304
NCTX = 256
CTS = [(0, 256), (256, 768), (768, 1280), (1280, 1792), (1792, 2304)]
C_ID, C_ONES, C_BLK, C_CUM0, C_CUM1, C_IND0, C_IND1, C_S63, C_S127, C_S0, C_S64, C_MN0, C_MN1, C_OFFD, C_RMT, C_BAND = range(16)
NCONST = 18
PF_L = 94
PB_ALOG, PB_DTB, PB_GNW, PB_SINK, PB_MNW, PB_IGB, PB_FGB, PB_LNW = 0, 8, 16, 80, 84, 340, 348, 356
PB_L = 612


def make_consts():
    c = np.zeros((NCONST, 128, 128), np.float32)
    k = np.arange(128)[:, None]
    i = np.arange(128)[None, :]
    same = (k // 64) == (i // 64)
    c[C_ID] = (k == i)
    c[C_ONES] = 1.0
    c[C_BLK] = same
    c[C_CUM0] = same & (k <= i)
    c[C_CUM1] = same & (k >= i)
    c[C_IND0] = (k < 64) & (i >= 0)
    c[C_IND1] = (k >= 64) & (i >= 0)
    c[C_S63] = (k == 63) & (i >= 0)
    c[C_S127] = (k == 127) & (i >= 0)
    c[C_S0] = (k == 0) & (i >= 0)
    c[C_S64] = (k == 64) & (i >= 0)
    c[C_MN0] = np.where(same & (k <= i), 0.0, NEG)
    c[C_MN1] = np.where(same & (k >= i), 0.0, NEG)
    c[C_OFFD] = (k != i)
    rmt = np.zeros((128, 128), np.float32)
    for hb in (0, 64):
        for d in range(32):
            rmt[hb + d + 32, hb + d] = -1.0
            rmt[hb + d, hb + d + 32] = 1.0
    c[C_RMT] = rmt
    qi = np.arange(128)[:, None]
    kj = np.arange(384)[None, :]
    band = np.where(np.abs(kj - 128 - qi) <= 128, 0.0, NEG).astype(np.float32)
    for t in range(3):
        c[C_BAND + t] = band[:, t * 128:(t + 1) * 128]
    return np.ascontiguousarray(c.transpose(1, 0, 2).reshape(128, NCONST * 128))


def make_rope():
    rows = 2048 // 64
    row = np.repeat(np.arange(rows), 64).astype(np.float32)
    col = (np.arange(2048) % 64).astype(np.float32)
    inv = np.power(np.float32(10000.0), -np.arange(16, dtype=np.float32) / 16).astype(np.float32)
    ang = np.concatenate([row[:, None] * inv, col[:, None] * inv], axis=-1).astype(np.float32)
    cos = np.cos(ang).astype(np.float32).T
    sin = np.sin(ang).astype(np.float32).T
    cc = np.concatenate([cos, cos, cos, cos], axis=0)
    ss = np.concatenate([sin, sin, sin, sin], axis=0)
    return np.ascontiguousarray(cc), np.ascontiguousarray(ss)


def build_model(B, io, L=4, mixers=("gmlp", "swa", "mlstm", "gdn"), do_mlp=True, new_scope=ExitStack):
    MUL, ADD, SUB, MAX, MIN = ALU.mult, ALU.add, ALU.subtract, ALU.max, ALU.min
    cst = B.sb("cst", [128, NCONST, 128])
    B.dma(cst, io["consts"].rr("p (n c) -> p n c", c=128))
    CM = lambda i: cst[:, i, :]
    ident, ones = CM(C_ID), CM(C_ONES)
    identb = B.sb("identb", [128, 128], BF16)
    B.copy(identb, ident)
    xT = B.sb("xT", [128, 8, TOK])
    xres = [[Res() for _ in CTS] for _ in range(8)]

    def X(c, ci):
        a, b = CTS[ci]
        v = xT[:, c, a:b]
        v.res = (xres[c][ci],)
        return v

    for c in range(8):
        for ci in range(5):
            a, b = CTS[ci]
            if ci == 0:
                B.dma(X(c, 0), io["ctxT"][c * 128:(c + 1) * 128, :])
            else:
                B.dma(X(c, ci), io["xT"][c * 128:(c + 1) * 128, a - 256:b - 256])
    pfm = B.sb("pfm", [128, L_DEPTH * PF_L + 8])
    B.dma(pfm, io["pfm"])
    epsT = B.sb("epsT", [128, 1])
    B.memset(epsT, 1e-6)
    oneT = B.sb("oneT", [128, 1])
    B.memset(oneT, 1.0)
    psA = [B.ps("psA%d" % i, [128, 512]) for i in range(4)]
    psB = [B.ps("psB%d" % i, [128, 512]) for i in range(4)]
    pctr = [0, 0]

    def PA():
        pctr[0] += 1
        return psA[pctr[0] % 4]

    def PB():
        pctr[1] += 1
        return psB[pctr[1] % 4]

    mod = B.sb("mod", [128, L, 48, 2])
    ns1 = B.sb("ns1", [128, L, 8, 2])
    ns2 = B.sb("ns2", [128, L, 8, 2])
    with new_scope() as sc0:
        scv = B.sb("scv", [128, 8, 2], scope=sc0)
        B.dma(scv, io["cvT"])
        B.act(scv, scv, AF.Silu)
        awb = [B.sb("awb", [128, 8, 768], scope=sc0) for _ in range(2)]
        for l in range(L):
            pm = PA()
            for s in range(8):
                aw = awb[s % 2]
                B.dma(aw, io["ada_w"][l].rr("(k p) n -> p k n", p=128)[:, :, s * 768:(s + 1) * 768])
                for nn in range(6):
                    n = s * 6 + nn
                    for k in range(8):
                        B.mm(pm[:, n * 2:n * 2 + 2], aw[:, k, nn * 128:(nn + 1) * 128], scv[:, k, :],
                             start=(k == 0), stop=(k == 7))
            adab = pfm[:, l * PF_L + 16:l * PF_L + 64]
            B.tt(mod[:, l], pm[:, 0:96].rr("p (a b) -> p a b", b=2), adab.unsq(2).bc([128, 48, 2]), ADD)
            for (ns, so, no) in ((ns1, 8, 0), (ns2, 32, 8)):
                B.ts(ns[:, l], mod[:, l, so:so + 8, :], 1.0, ADD)
                B.tt(ns[:, l], ns[:, l], pfm[:, l * PF_L + no:l * PF_L + no + 8].unsq(2).bc([128, 8, 2]), MUL)
        B.barrier()

    def normmod(ci, ns, sh, hout, S):
        a, b = CTS[ci]
        w = b - a
        s = 1 if ci == 0 else 0
        pss = PA()
        for c in range(8):
            sq = S["sq"][c % 2]
            B.act(sq[:, :w], X(c, ci), AF.Square)
            B.mm(pss[:, :w], ones, sq[:, :w], start=(c == 0), stop=(c == 7))
        rstd = S["rstd"]
        B.act(rstd[:, :w], pss[:, :w], AF.Sqrt, scale=1.0 / D, bias=epsT)
        B.recip(rstd[:, :w], rstd[:, :w])
        for c in range(8):
            tmp = S["tmp"][c % 2]
            B.tt(tmp[:, :w], X(c, ci), rstd[:, :w], MUL)
            B.act(hout[:, c, :w], tmp[:, :w], AF.Identity, scale=ns[:, c, s:s + 1], bias=sh[:, c, s:s + 1])

    def norm_scratch(scope):
        return {"sq": [B.sb("sq", [128, 512], scope=scope) for _ in range(2)],
                "tmp": [B.sb("tmpn", [128, 512], scope=scope) for _ in range(2)],
                "rstd": B.sb("rstd", [128, 512], scope=scope)}

    def load_w(dst, src3, q="pool"):
        for k in range(8):
            B.dma(dst[:, k, :], src3[:, k, :], q=q)

    def residual(l, mixT, wo, kc, ci_list, gate_off):
        for ci in ci_list:
            a, b = CTS[ci]
            w = b - a
            s = 1 if ci == 0 else 0
            for n in range(8):
                pp = PB()
                for k in range(kc):
                    B.mm(pp[:, :w], wo[:, k, n * 128:(n + 1) * 128], mixT[:, k, a:b], start=(k == 0), stop=(k == kc - 1))
                B.stt(X(n, ci), pp[:, :w], mod[:, l, gate_off + n, s:s + 1], X(n, ci), MUL, ADD)

    ctx_needed = lambda l: l < L - 1
    hres = [Res() for _ in CTS]

    def HD(ci):
        a, b = CTS[ci]
        v = io["hd"][:, :, a:b]
        v.res = (hres[ci],)
        return v

    for l in range(L):
        cis = [0, 1, 2, 3, 4] if ctx_needed(l) else [1, 2, 3, 4]
        sh1 = mod[:, l, 0:8, :]
        sh2 = mod[:, l, 24:32, :]
        with new_scope() as sc:
            S = norm_scratch(sc)
            hb = [B.sb("hb", [128, 8, 512], BF16, scope=sc) for _ in range(2)]
            for ci in range(5):
                a, b = CTS[ci]
                normmod(ci, ns1[:, l], sh1, hb[ci % 2], S)
                B.dma(HD(ci), hb[ci % 2][:, :, :b - a])
            B.barrier()

        def loadh(ci, h):
            a, b = CTS[ci]
            B.dma(h[:, :, :b - a], HD(ci))

        for mi, mx in enumerate(("gmlp", "swa", "mlstm", "gdn")):
            if mx not in mixers:
                continue
            with new_scope() as sc:
                S = {}
                S["sc"] = sc
                S["cst"] = cst
                if mx in ("gmlp", "swa"):
                    S["h"] = [B.sb("hct", [128, 8, 512], BF16, scope=sc) for _ in range(2)]
                S["wout"] = B.sb("wout", [128, 2, 1024], BF16, scope=sc)
                S["pbc"] = B.sb("pbc", [128, PB_L], scope=sc)
                B.dma(S["pbc"], io["pbc"][l].pbc(128))
                grp = {"gdn": 0, "swa": 1, "gmlp": 2, "mlstm": 3}[mx]
                for k in range(2):
                    B.dma(S["wout"][:, k, :], io["w_out"][l, grp * 256 + k * 128:grp * 256 + (k + 1) * 128, :], q="pool")
                env = dict(B=B, io=io, l=l, S=S, CM=CM, PA=PA, PB=PB, normmod=normmod, ns1=ns1[:, l], sh1=sh1,
                           load_w=load_w, loadh=loadh, new_scope=new_scope, psA=psA, psB=psB, ident=ident, identb=identb, ones=ones, pfm=pfm, epsT=epsT, oneT=oneT)
                {"gmlp": mixer_gmlp, "swa": mixer_swa, "mlstm": mixer_mlstm, "gdn": mixer_gdn}[mx](env)
                residual(l, S["mixT"], S["wout"], 2, cis, 16)
                B.barrier()
        if do_mlp:
            with new_scope() as sc:
                hT = B.sb("hT", [128, 8, TOK], BF16, scope=sc)
                with new_scope() as scn:
                    S = norm_scratch(scn)
                    for ci in cis:
                        a, b = CTS[ci]
                        normmod(ci, ns2[:, l], sh2, hT[:, :, a:b], S)
                    B.barrier()
                w1s = [B.sb("w1", [128, 8, 1024], BF16, scope=sc) for _ in range(2)]
                w2s = [B.sb("w2", [128, 8, 1024], BF16, scope=sc) for _ in range(2)]
                hid = [B.sb("hid", [128, 8, 512], BF16, scope=sc) for _ in range(2)]
                rl = [B.sb("rl", [128, 512], scope=sc) for _ in range(2)]

                def ldq(qq):
                    load_w(w1s[qq % 2], io["mlp_w1"][l].rr("(k p) n -> p k n", p=128)[:, :, qq * 1024:(qq + 1) * 1024])
                    load_w(w2s[qq % 2], io["mlp_w2"][l, qq * 1024:(qq + 1) * 1024, :].rr("(k p) n -> p k n", p=128))

                ldq(0)
                for qq in range(4):
                    if qq < 3:
                        ldq(qq + 1)
                    w1, w2 = w1s[qq % 2], w2s[qq % 2]
                    for ci in cis:
                        a, b = CTS[ci]
                        w = b - a
                        s = 1 if ci == 0 else 0
                        hd = hid[ci % 2]
                        for f in range(8):
                            pp = PA()
                            for k in range(8):
                                B.mm(pp[:, :w], w1[:, k, f * 128:(f + 1) * 128], hT[:, k, a:b], start=(k == 0), stop=(k == 7))
                            r = rl[f % 2]
                            B.act(r[:, :w], pp[:, :w], AF.Relu)
                            B.tt(hd[:, f, :w], r[:, :w], r[:, :w], MUL, eng="pool")
                        for n in range(8):
                            pp = PB()
                            for f in range(8):
                                B.mm(pp[:, :w], w2[:, f, n * 128:(n + 1) * 128], hd[:, f, :w], start=(f == 0), stop=(f == 7))
                            B.stt(X(n, ci), pp[:, :w], mod[:, l, 40 + n, s:s + 1], X(n, ci), MUL, ADD)
                B.barrier()

    with new_scope() as sc:
        S = norm_scratch(sc)
        fo = [B.sb("fo", [128, 512], scope=sc) for _ in range(2)]
        fw = pfm[:, L_DEPTH * PF_L:L_DEPTH * PF_L + 8]
        for ci in range(1, 5):
            a, b = CTS[ci]
            w = b - a
            pss = PA()
            for c in range(8):
                sq = S["sq"][c % 2]
                B.act(sq[:, :w], X(c, ci), AF.Square)
                B.mm(pss[:, :w], ones, sq[:, :w], start=(c == 0), stop=(c == 7))
            rstd = S["rstd"]
            B.act(rstd[:, :w], pss[:, :w], AF.Sqrt, scale=1.0 / D, bias=epsT)
            B.recip(rstd[:, :w], rstd[:, :w])
            for c in range(8):
                o = fo[c % 2]
                B.stt(o[:, :w], X(c, ci), fw[:, c:c + 1], rstd[:, :w], MUL, MUL)
                B.dma(io["outT"][c * 128:(c + 1) * 128, a - 256:b - 256], o[:, :w])
    B.finish()


def mixer_gmlp(E):
    B, io, l, S, PA, PB = E["B"], E["io"], E["l"], E["S"], E["PA"], E["PB"]
    sc = S["sc"]
    MUL, ADD = ALU.mult, ALU.add
    win = B.sb("win", [128, 8, 512], BF16, scope=sc)
    E["load_w"](win, io["w_in"][l].rr("(k p) n -> p k n", p=128)[:, :, 1552:2064])
    wsT = B.sb("wsT", [128, 4, 128], BF16, scope=sc)
    B.dma(wsT, io["gmlp_wsT"][l].rr("g q p -> q g p"), q="pool")
    bs = B.sb("bs", [128, 2, 128], scope=sc)
    B.dma(bs, io["gmlp_bsr"][l])
    nw = S["pbc"][:, PB_MNW:PB_MNW + 256]
    uT = B.sb("uT", [128, 2, 512], scope=sc)
    vg = B.sb("vg", [128, 256], scope=sc)
    junk = B.sb("junk", [128, 256], scope=sc)
    vn = B.sb("vn", [128, 256], BF16, scope=sc)
    ssq = B.sb("ssq", [128, 1], scope=sc)
    tmpm = B.sb("tmpm", [128, 128], scope=sc)
    mixT = S["mixT"] = B.sb("mixT", [128, 2, TOK], BF16, scope=sc)
    E["loadh"](0, S["h"][0])
    for ci in range(5):
        a, b = CTS[ci]
        w = b - a
        h = S["h"][ci % 2]
        if ci < 4:
            E["loadh"](ci + 1, S["h"][(ci + 1) % 2])
        for c in range(2):
            pp = PA()
            for k in range(8):
                B.mm(pp[:, :w], win[:, k, c * 128:(c + 1) * 128], h[:, k, :w], start=(k == 0), stop=(k == 7))
            B.act(uT[:, c, :w], pp[:, :w], AF.Gelu_apprx_tanh)
        for ti in range(w // 128):
            t0 = ti * 128
            pv = PA()
            for k in range(8):
                B.mm(pv[:, :256], h[:, k, t0:t0 + 128], win[:, k, 256:512], start=(k == 0), stop=(k == 7))
            B.act(vg, pv[:, :256], AF.Gelu_apprx_tanh)
            B.memset(ssq, 0.0)
            B.act(junk, vg, AF.Square, accum=ssq)
            B.act(ssq, ssq, AF.Sqrt, scale=1.0 / 256, bias=E["epsT"])
            B.recip(ssq, ssq)
            B.stt(vn, vg, ssq, nw, MUL, MUL)
            for pc in range(2):
                pm_ = PB()
                for gi in range(2):
                    g = pc * 2 + gi
                    B.mm(pm_[gi * 64:(gi + 1) * 64, :128], vn[:, g * 64:(g + 1) * 64], wsT[:, g, :])
                B.tt(tmpm, pm_[:, :128], bs[:, pc, :], ADD)
                B.tt(mixT[:, pc, a + t0:a + t0 + 128], tmpm, uT[:, pc, t0:t0 + 128], MUL, eng="pool")
    if "dbg" in io:
        io["dbg"][(l, "gmlp")] = np.array(mixT.a, dtype=np.float32)


def mixer_swa(E):
    B, io, l, S, PA, PB, CM = E["B"], E["io"], E["l"], E["S"], E["PA"], E["PB"], E["CM"]
    sc = S["sc"]
    MUL, ADD, MAX = ALU.mult, ALU.add, ALU.max
    ident = E["ident"]
    win = B.sb("win", [128, 8, 512], BF16, scope=sc)
    E["load_w"](win, io["w_in"][l].rr("(k p) n -> p k n", p=128)[:, :, 1040:1552])
    qT = B.sb("qT", [128, 2, TOK], BF16, scope=sc)
    kT = B.sb("kT", [128, TOK], BF16, scope=sc)
    vtok = B.sb("vtok", [128, NT, 128], BF16, scope=sc)
    rc_l = [B.sb("rc", [128, 512], scope=sc) for _ in range(2)]
    rs_l = [B.sb("rs", [128, 512], scope=sc) for _ in range(2)]
    raw_l = [B.sb("raw", [128, 512], scope=sc) for _ in range(2)]
    t1_l = [B.sb("t1", [128, 512], scope=sc) for _ in range(2)]
    t2_l = [B.sb("t2", [128, 512], scope=sc) for _ in range(2)]
    fctr = 0
    sink = S["pbc"][:, PB_SINK:PB_SINK + 4]
    mixT = S["mixT"] = B.sb("mixT", [128, 2, TOK], BF16, scope=sc)
    E["loadh"](0, S["h"][0])
    for ci in range(5):
        a, b = CTS[ci]
        w = b - a
        h = S["h"][ci % 2]
        if ci < 4:
            E["loadh"](ci + 1, S["h"][(ci + 1) % 2])
        rc, rs_ = rc_l[ci % 2], rs_l[ci % 2]
        if ci > 0:
            B.dma(rc, io["ropec"][:, a - 256:b - 256])
            B.dma(rs_, io["ropes"][:, a - 256:b - 256])
        for r in range(3):
            raw, t1, t2 = raw_l[fctr % 2], t1_l[fctr % 2], t2_l[fctr % 2]
            fctr += 1
            pp = PA()
            if r < 2:
                for j, hq in enumerate((r, r + 2)):
                    for k in range(8):
                        B.mm(pp[j * 64:(j + 1) * 64, :w], win[:, k, hq * 64:(hq + 1) * 64], h[:, k, :w],
                             start=(k == 0), stop=(k == 7))
            else:
                for k in range(8):
                    B.mm(pp[:, :w], win[:, k, 256:384], h[:, k, :w], start=(k == 0), stop=(k == 7))
            dst = qT[:, r, a:b] if r < 2 else kT[:, a:b]
            if ci == 0:
                B.act(dst, pp[:, :w], AF.Copy)
            else:
                B.act(raw[:, :w], pp[:, :w], AF.Copy)
                pr = PB()
                B.mm(pr[:, :w], CM(C_RMT), raw[:, :w])
                B.tt(t1[:, :w], raw[:, :w], rc[:, :w], MUL)
                B.tt(t2[:, :w], pr[:, :w], rs_[:, :w], MUL)
                B.tt(dst, t1[:, :w], t2[:, :w], ADD, eng="pool")
        for ti in range(w // 128):
            t0 = ti * 128
            pv = PA()
            for k in range(8):
                B.mm(pv[:, :128], h[:, k, t0:t0 + 128], win[:, k, 384:512], start=(k == 0), stop=(k == 7))
            B.act(vtok[:, (a + t0) // 128, :], pv[:, :128], AF.Copy)
    if "dbg" in io:
        io["dbg"]["qT"] = np.array(qT.a, dtype=np.float32)
        io["dbg"]["kT"] = np.array(kT.a, dtype=np.float32)
    bandm = E["B"].sb("bandm", [128, 384], scope=sc)
    B.copy(bandm.rr("p (a b) -> p a b", b=128), E["S"]["cst"][:, C_BAND:C_BAND + 3, :], eng="pool")
    NBUF = 4
    s_l = [B.sb("s", [128, 640], scope=sc) for _ in range(NBUF)]
    p_l = [B.sb("p", [128, 640], scope=sc) for _ in range(NBUF)]
    pT_l = [B.sb("pT", [128, 640], BF16, scope=sc) for _ in range(NBUF)]
    st_l = [B.sb("st", [128, 8], scope=sc) for _ in range(NBUF)]
    rings = [make_ring(E["psA"][0:2]), make_ring(E["psA"][2:4]), make_ring(E["psB"][0:2]), make_ring(E["psB"][2:4])]

    def unit_gen(slot, qb, hq):
        PS = rings[slot]
        s, p, pT, st = s_l[slot], p_l[slot], pT_l[slot], st_l[slot]
        mxv, negm, rsum, es, den = (st[:, i:i + 1] for i in range(5))
        q0 = qb * 128
        if qb >= 2:
            n = qb - 2
            lo, hi = max(n - 1, 0), min(n + 1, 15)
            kl0, kl1 = 256 + lo * 128, 256 + (hi + 1) * 128
            wl = kl1 - kl0
            moff = (lo - (n - 1)) * 128
        else:
            wl = 0
        W = wl + 256
        nb = W // 128
        hk = hq // 2
        r = hq % 2
        base = hk * 64
        qv = qT[base:base + 64, r, q0:q0 + 128]
        if wl:
            pa = PS()
            B.mm(pa[:, :wl], qv, kT[base:base + 64, kl0:kl1])
        pb = PS()
        B.mm(pb[:, :256], qv, kT[base:base + 64, 0:256])
        yield
        if wl:
            B.stt(s[:, :wl], pa[:, :wl], 0.125, bandm[:, moff:moff + wl], MUL, ADD)
        B.act(s[:, wl:W], pb[:, :256], AF.Copy, scale=0.125)
        yield
        B.red(mxv, s[:, :W], MAX)
        yield
        B.ts(negm, mxv, sink[:, hq:hq + 1], MAX, -1.0, MUL)
        B.memset(rsum, 0.0)
        yield
        B.act(p[:, :W], s[:, :W], AF.Exp, bias=negm, accum=rsum)
        B.act(es, sink[:, hq:hq + 1], AF.Exp, bias=negm)
        yield
        B.tt(den, rsum, es, ADD)
        yield
        B.recip(den, den)
        yield
        B.ts(p[:, :W], p[:, :W], den, MUL)
        yield
        pt1, pt2 = PS(), PS()
        for j_ in range(nb):
            B.tr((pt1 if j_ < 4 else pt2)[:, (j_ % 4) * 128:(j_ % 4 + 1) * 128], p[:, j_ * 128:(j_ + 1) * 128], ident)
        yield
        B.act(pT[:, 0:min(nb, 4) * 128], pt1[:, 0:min(nb, 4) * 128], AF.Copy)
        if nb > 4:
            B.copy(pT[:, 512:640], pt2[:, 0:128])
        yield
        po = PS()
        ob = (hq % 2) * 64
        for j_ in range(nb):
            kt = (kl0 // 128 + j_) if j_ < wl // 128 else (j_ - wl // 128)
            B.mm(po[ob:ob + 64, :128], vtok[:, kt, hk * 64:(hk + 1) * 64], pT[:, j_ * 128:(j_ + 1) * 128],
                 start=(j_ == 0), stop=(j_ == nb - 1))
        yield
        B.act(mixT[ob:ob + 64, hq // 2, q0:q0 + 128], po[ob:ob + 64, :128], AF.Copy)

    lockstep([(lambda slot, qb=qb, hq=hq: unit_gen(slot, qb, hq)) for qb in range(NT) for hq in range(4)], NBUF)
    if "dbg" in io:
        io["dbg"][(l, "swa")] = np.array(mixT.a, dtype=np.float32)


def mixer_mlstm(E):
    B, io, l, S, PA, PB, CM = E["B"], E["io"], E["l"], E["S"], E["PA"], E["PB"], E["CM"]
    sc = S["sc"]
    MUL, ADD, SUB, MAX, MIN = ALU.mult, ALU.add, ALU.subtract, ALU.max, ALU.min
    ident, ones = E["ident"], E["ones"]
    new_scope = E["new_scope"]
    mixT = S["mixT"] = B.sb("mixT", [128, 2, TOK], BF16, scope=sc)
    pbc = S["pbc"]
    QT = B.sb("QT", [128, 2, TOK], BF16, scope=sc)
    KT = B.sb("KT", [128, 2, TOK], BF16, scope=sc)
    Ktok = B.sb("Ktok", [128, NT, 4, 64], BF16, scope=sc)
    V1 = B.sb("V1", [128, NT, 4, 66], BF16, scope=sc)
    Og = B.sb("Og", [128, NT, 256], BF16, scope=sc)
    nbT = B.sb("nbT", [128, NT, 8], scope=sc)
    colT = B.sb("colT", [128, NT, 8], scope=sc)
    cmT = B.sb("cmT", [128, NT, 8], scope=sc)
    nblT = B.sb("nblT", [128, NT, 2, 8], scope=sc)
    cmlT = B.sb("cmlT", [128, NT, 2, 8], scope=sc)
    mnew = B.sb("mnew", [128, 36, 8], scope=sc)
    carry = B.sb("carry", [128, 36, 8], scope=sc)
    zer = B.sb("zer", [128, 8], scope=sc)
    B.memset(zer, 0.0)
    B.memset(V1, 1.0)
    g8 = [B.sb("g8", [128, 8], scope=sc) for _ in range(6)]
    rd_l = [B.sb("rd", [128, 8, 128], scope=sc) for _ in range(2)]
    big_l = [B.sb("big", [128, 4, 128], scope=sc) for _ in range(2)]
    with new_scope() as sc1:
        win = B.sb("win", [128, 8, 1040], BF16, scope=sc1)
        E["load_w"](win, io["w_in"][l].rr("(k p) n -> p k n", p=128)[:, :, 2064:3104])
        hs2 = [B.sb("hct", [128, 8, 512], BF16, scope=sc1) for _ in range(2)]
        E["loadh"](0, hs2[0])
        for ci in range(5):
            a, b = CTS[ci]
            w = b - a
            h = hs2[ci % 2]
            if ci < 4:
                E["loadh"](ci + 1, hs2[(ci + 1) % 2])
            for r in range(4):
                pp = PA()
                for k in range(8):
                    B.mm(pp[:, :w], win[:, k, r * 128:(r + 1) * 128], h[:, k, :w], start=(k == 0), stop=(k == 7))
                if r < 2:
                    B.act(QT[:, r, a:b], pp[:, :w], AF.Copy, scale=0.125)
                else:
                    B.act(KT[:, r - 2, a:b], pp[:, :w], AF.Copy)
            for ti in range(w // 128):
                t0 = ti * 128
                t = (a + t0) // 128
                p1, p2 = PA(), PA()
                for k in range(8):
                    B.mm(p1[:, :512], h[:, k, t0:t0 + 128], win[:, k, 256:768], start=(k == 0), stop=(k == 7))
                for k in range(8):
                    B.mm(p2[:, :272], h[:, k, t0:t0 + 128], win[:, k, 768:1040], start=(k == 0), stop=(k == 7))
                B.act(Ktok[:, t], p1[:, 0:256].rr("p (h d) -> p h d", d=64), AF.Copy)
                B.act(V1[:, t, :, 0:64], p1[:, 256:512].rr("p (h d) -> p h d", d=64), AF.Copy)
                B.act(Og[:, t, :], p2[:, 0:256], AF.Sigmoid)
                ig, fx, lf = g8[(t % 2) * 3], g8[(t % 2) * 3 + 1], g8[(t % 2) * 3 + 2]
                rd = rd_l[t % 2]
                B.tt(ig, p2[:, 256:264], pbc[:, PB_IGB:PB_IGB + 8], ADD)
                B.tt(fx, p2[:, 264:272], pbc[:, PB_FGB:PB_FGB + 8], ADD)
                B.act(fx, fx, AF.Exp, scale=-1.0)
                B.act(lf, fx, AF.Ln, bias=E["oneT"])
                pg = PB()
                B.mm(pg[:, 0:4], CM(C_CUM0), lf[:, 0:4])
                B.mm(pg[:, 4:8], CM(C_CUM1), lf[:, 4:8])
                B.mm(pg[:, 8:16], CM(C_IND0), lf)
                B.mm(pg[:, 16:24], CM(C_IND1), lf)
                B.copy(nbT[:, t, :], pg[:, 0:8])
                B.copy(nblT[:, t], pg[:, 8:24].rr("p (c n) -> p c n", n=8))
                B.tt(colT[:, t, :], ig, pg[:, 0:8], ADD)
                B.tt(rd, colT[:, t, :].unsq(2).bc([128, 8, 128]), ident.unsq(1).bc([128, 8, 128]), MUL)
                for d in range(2):
                    big = big_l[d]
                    pc = PA()
                    B.mm(pc[:, :512], ones, rd[:, d * 4:(d + 1) * 4, :])
                    B.tt(big, pc[:, :512].rr("p (h j) -> p h j", j=128),
                         CM(C_MN1 if d == 0 else C_MN0).unsq(1).bc([128, 4, 128]), ADD)
                    B.red(cmT[:, t, d * 4:(d + 1) * 4], big, MAX)
                pl = PB()
                for cl in range(2):
                    for d in range(2):
                        sel = (C_S63, C_S127)[cl] if d == 0 else (C_S0, C_S64)[cl]
                        B.mm(pl[:, cl * 8 + d * 4:cl * 8 + d * 4 + 4], CM(sel), cmT[:, t, d * 4:(d + 1) * 4])
                B.copy(cmlT[:, t], pl[:, 0:16].rr("p (c n) -> p c n", n=8))
        B.barrier()
    acc = B.sb("acc", [128, NT, 4, 64], scope=sc)
    tmp4 = B.sb("tmp4", [128, 4], scope=sc)
    mst = {}
    for d in range(2):
        d4 = slice(d * 4, d * 4 + 4)
        order = list(range(36)) if d == 0 else [3, 2, 1, 0] + list(range(35, 3, -1))
        prev = zer[:, d4]
        for c in order:
            t, cl = divmod(c, 2)
            mst[(c, d)] = prev
            B.tt(tmp4, prev, cmlT[:, t, cl, d4], MAX)
            B.tt(mnew[:, c, d4], tmp4, nblT[:, t, cl, d4], SUB)
            B.tt(tmp4, prev, nblT[:, t, cl, d4], SUB)
            B.tt(tmp4, tmp4, mnew[:, c, d4], SUB)
            B.act(carry[:, c, d4], tmp4, AF.Exp)
            prev = mnew[:, c, d4]
    def dirbufs():
        o = {}
        o["St32"] = B.sb("St32", [128, 2, 66], scope=sc)
        o["Stb"] = B.sb("Stb", [128, 2, 66], BF16, scope=sc)
        for nm in ("mstk", "mnwk", "nblk", "inter", "mt", "rowt", "wint", "emt", "kwf", "dd"):
            o[nm] = B.sb(nm, [128, 4], scope=sc)
        o["ET"] = B.sb("ET", [128, 4, 128], scope=sc)
        o["SmT"] = B.sb("SmT", [128, 4, 128], BF16, scope=sc)
        o["kw"] = B.sb("kw", [128, 4, 64], BF16, scope=sc)
        o["t1"] = B.sb("t1", [128, 4, 66], scope=sc)
        o["nd"] = B.sb("nd", [128, 4, 66], scope=sc)
        o["ho"] = B.sb("ho", [128, 4, 64], scope=sc)
        o["rdd"] = B.sb("rdd", [128, 4, 128], scope=sc)
        return o

    DB = [dirbufs(), dirbufs()]
    B.memset(acc, 0.0)
    for d in range(2):
        B.memset(DB[d]["St32"], 0.0)
        B.memset(DB[d]["Stb"], 0.0)

    rings = [make_ring(E["psA"]), make_ring(E["psB"])]

    def tile_step(d, t):
        o = DB[d]
        PS = rings[d]
        St32, Stb, ET, SmT, kw, t1, nd, ho = (o[k] for k in ("St32", "Stb", "ET", "SmT", "kw", "t1", "nd", "ho"))
        mstk, mnwk, nblk, inter, mt, rowt, wint, emt, kwf, dd = (o[k] for k in (
            "mstk", "mnwk", "nblk", "inter", "mt", "rowt", "wint", "emt", "kwf", "dd"))
        d4 = slice(d * 4, d * 4 + 4)
        mneg = CM(C_MN0 if d == 0 else C_MN1)
        tok0 = t * 128
        for cl in range(2):
            c = t * 2 + cl
            hs_ = slice(cl * 64, cl * 64 + 64)
            B.copy(mstk[hs_], mst[(c, d)][hs_])
            B.copy(mnwk[hs_], mnew[hs_, c, d4])
            B.copy(nblk[hs_], nblT[hs_, t, cl, d4])
        nb = nbT[:, t, d4]
        colt = colT[:, t, d4]
        B.tt(inter, mstk, nb, SUB)
        B.tt(mt, cmT[:, t, d4], nb, SUB)
        B.tt(mt, mt, inter, MAX)
        B.stt(rowt, nb, -1.0, mt, MUL, SUB)
        B.tt(wint, inter, mt, SUB)
        B.act(wint, wint, AF.Exp)
        B.act(emt, mt, AF.Exp, scale=-1.0)
        B.tt(kwf, colt, nblk, SUB)
        B.tt(kwf, kwf, mnwk, SUB)
        B.act(kwf, kwf, AF.Exp)
        rdd = o["rdd"]
        B.tt(rdd, rowt.unsq(2).bc([128, 4, 128]), ident.unsq(1).bc([128, 4, 128]), MUL)
        yield
        pe = PS()
        B.mm(pe[:, :512], ones, rdd)
        for hh in range(4):
            B.stt(ET[:, hh, :], pe[:, hh * 128:(hh + 1) * 128], colT[:, t, d * 4 + hh:d * 4 + hh + 1], mneg, ADD, MIN)
        yield
        B.act(ET, ET, AF.Exp)
        yield
        pkp = (PS(), PS())
        for hh in range(4):
            hb = (hh % 2) * 64
            B.mm(pkp[hh % 2][:, hh * 128:(hh + 1) * 128], KT[hb:hb + 64, hh // 2, tok0:tok0 + 128],
                 QT[hb:hb + 64, hh // 2, tok0:tok0 + 128])
        yield
        for par in range(2):
            B.tt(SmT[:, par::2, :], pkp[par][:, :512].rr("p (h j) -> p h j", j=128)[:, par::2, :], ET[:, par::2, :], MUL)
        yield
        pi = PS()
        for hh in range(4):
            B.mm(pi[:, hh * 66:hh * 66 + 65], SmT[:, hh, :], V1[:, t, hh, 0:65])
        yield
        B.tt(kw, Ktok[:, t], kwf.unsq(2).bc([128, 4, 64]), MUL)
        yield
        pinp = (PS(), PS())
        pu = PS()
        for cl in ((0, 1) if d == 0 else (1, 0)):
            c = t * 2 + cl
            cb = cl * 64
            for hh in range(4):
                hb, hp = (hh % 2) * 64, hh // 2
                B.mm(pinp[hh % 2][cb:cb + 64, hh * 66:hh * 66 + 65], QT[hb:hb + 64, hp, tok0 + cb:tok0 + cb + 64],
                     Stb[hb:hb + 64, hp, 0:65])
            yield
            for hh in range(4):
                hb, hp = (hh % 2) * 64, hh // 2
                B.mm(pu[hb:hb + 64, hp * 66:hp * 66 + 65], kw[cb:cb + 64, hh, :], V1[cb:cb + 64, t, hh, 0:65])
            yield
            for hh in range(4):
                hb, hp = (hh % 2) * 64, hh // 2
                B.stt(St32[hb:hb + 64, hp, 0:65], St32[hb:hb + 64, hp, 0:65], carry[hb:hb + 64, c, d * 4 + hh:d * 4 + hh + 1],
                      pu[hb:hb + 64, hp * 66:hp * 66 + 65], MUL, ADD)
            B.act(Stb, St32, AF.Copy)
        yield
        for par in range(2):
            B.tt(t1[:, par::2, 0:65], pinp[par][:, 0:264].rr("p (h e) -> p h e", e=66)[:, par::2, 0:65],
                 wint[:, par::2].unsq(2).bc([128, 2, 65]), MUL)
        B.tt(nd[:, :, 0:65], t1[:, :, 0:65], pi[:, 0:264].rr("p (h e) -> p h e", e=66)[:, :, 0:65], ADD)
        B.stt(dd, nd[:, :, 64], -1.0, nd[:, :, 64], MUL, MAX)
        B.tt(dd, dd, emt, MAX)
        B.recip(dd, dd)
        B.tt(ho, nd[:, :, 0:64], dd.unsq(2).bc([128, 4, 64]), MUL)
        B.tt(acc[:, t], acc[:, t], ho, ADD, eng="pool")

    order = [list(range(NT)), [1, 0] + list(range(NT - 1, 1, -1))]
    def dir_gen(d):
        for t in order[d]:
            yield from tile_step(d, t)

    lockstep([lambda slot: dir_gen(0), lambda slot: dir_gen(1)], 2)
    ho = DB[0]["ho"]
    dd = DB[0]["dd"]
    lnw = pbc[:, PB_LNW:PB_LNW + 256].rr("p (h d) -> p h d", d=64)
    hn = B.sb("hn", [128, 4, 64], scope=sc)
    for t in range(NT):
        B.tt(ho, acc[:, t], acc[:, t], MUL)
        B.red(dd, ho, ADD)
        B.act(dd, dd, AF.Sqrt, scale=1.0 / 64, bias=E["epsT"])
        B.recip(dd, dd)
        B.tt(hn, acc[:, t], dd.unsq(2).bc([128, 4, 64]), MUL)
        B.tt(hn, hn, lnw, MUL)
        B.tt(hn, hn, Og[:, t, :].rr("p (h d) -> p h d", d=64), MUL)
        for pc_ in range(2):
            pt = PB()
            B.tr(pt[:, :128], hn[:, pc_ * 2:pc_ * 2 + 2, :].rr("p h d -> p (h d)"), ident)
            B.act(mixT[:, pc_, t * 128:(t + 1) * 128], pt[:, :128], AF.Copy)
    if "dbg" in io:
        io["dbg"][(l, "mlstm")] = np.array(mixT.a, dtype=np.float32)


def mixer_gdn(E):
    B, io, l, S, PA, PB, CM = E["B"], E["io"], E["l"], E["S"], E["PA"], E["PB"], E["CM"]
    sc = S["sc"]
    MUL, ADD, SUB, MAX, MIN = ALU.mult, ALU.add, ALU.subtract, ALU.max, ALU.min
    ident, ones, pfm = E["ident"], E["ones"], E["pfm"]
    new_scope = E["new_scope"]
    pbc = S["pbc"]
    QT = B.sb("QT", [128, 2, TOK], BF16, scope=sc)
    KT = B.sb("KT", [128, 2, TOK], BF16, scope=sc)
    Ktok = B.sb("Ktok", [128, NT, 4, 64], BF16, scope=sc)
    Vtok = B.sb("Vtok", [128, NT, 4, 64], BF16, scope=sc)
    Zg = B.sb("Zg", [128, NT, 256], BF16, scope=sc)
    ngcT = B.sb("ngcT", [128, NT, 8], scope=sc)
    betaT = B.sb("betaT", [128, NT, 8], scope=sc)
    nglT = B.sb("nglT", [128, NT, 2, 8], scope=sc)
    glT = B.sb("glT", [128, NT, 2, 8], scope=sc)
    g8 = [B.sb("g8", [128, 8], scope=sc) for _ in range(3)]
    eal = B.sb("eal", [128, 8], scope=sc)
    B.act(eal, pbc[:, PB_ALOG:PB_ALOG + 8], AF.Exp)
    with new_scope() as sc1:
        win = B.sb("win", [128, 8, 1040], BF16, scope=sc1)
        E["load_w"](win, io["w_in"][l].rr("(k p) n -> p k n", p=128)[:, :, 0:1040])
        hh_ = [B.sb("hct", [128, 8, 512], BF16, scope=sc1) for _ in range(2)]
        raw = B.sb("raw", [128, 2312], scope=sc1)
        cacc = B.sb("cacc", [128, TOK], scope=sc1)
        sq = B.sb("sq", [128, 512], scope=sc1)
        rin = B.sb("rin", [128, 512], scope=sc1)
        B.memset(raw, 0.0)
        hc = 0
        for r in range(6):
            for ci in range(5):
                a, b = CTS[ci]
                w = b - a
                h = hh_[hc % 2]
                hc += 1
                E["loadh"](ci, h)
                pp = PA()
                for k in range(8):
                    B.mm(pp[:, :w], win[:, k, r * 128:(r + 1) * 128], h[:, k, :w], start=(k == 0), stop=(k == 7))
                off = 2 if ci == 0 else 262 + (a - 256)
                B.act(raw[:, off:off + w], pp[:, :w], AF.Copy)
            cw = lambda tap: pfm[:, l * PF_L + 64 + tap * 6 + r:l * PF_L + 64 + tap * 6 + r + 1]
            for (o0, t0, n) in ((0, 0, 256), (260, 256, 2048)):
                B.ts(cacc[:, t0:t0 + n], raw[:, o0:o0 + n], cw(0), MUL)
                for tap in range(1, 5):
                    B.stt(cacc[:, t0:t0 + n], raw[:, o0 + tap:o0 + tap + n], cw(tap), cacc[:, t0:t0 + n], MUL, ADD)
            B.act(cacc, cacc, AF.Silu)
            if r < 4:
                for ci in range(5):
                    a, b = CTS[ci]
                    w = b - a
                    B.act(sq[:, :w], cacc[:, a:b], AF.Square)
                    pq_ = PB()
                    B.mm(pq_[:, :w], CM(C_BLK), sq[:, :w])
                    B.act(rin[:, :w], pq_[:, :w], AF.Sqrt, bias=E["epsT"])
                    B.recip(rin[:, :w], rin[:, :w])
                    dst = (QT if r < 2 else KT)[:, r % 2, a:b]
                    B.stt(dst, cacc[:, a:b], 0.125 if r < 2 else 1.0, rin[:, :w], MUL, MUL)
                    if r >= 2:
                        B.tt(sq[:, :w], cacc[:, a:b], rin[:, :w], MUL)
                        for ti in range(w // 128):
                            pt = PB()
                            B.tr(pt[:, :128], sq[:, ti * 128:(ti + 1) * 128], ident)
                            B.act(Ktok[:, (a + ti * 128) // 128, (r - 2) * 2:(r - 2) * 2 + 2, :],
                                  pt[:, :128].rr("p (h d) -> p h d", d=64), AF.Copy)
            else:
                for t in range(NT):
                    pt = PB()
                    B.tr(pt[:, :128], cacc[:, t * 128:(t + 1) * 128], ident)
                    B.act(Vtok[:, t, (r - 4) * 2:(r - 4) * 2 + 2, :], pt[:, :128].rr("p (h d) -> p h d", d=64), AF.Copy)
        for ci in range(5):
            a, b = CTS[ci]
            w = b - a
            h = hh_[hc % 2]
            hc += 1
            E["loadh"](ci, h)
            for ti in range(w // 128):
                t0 = ti * 128
                t = (a + t0) // 128
                p2 = PA()
                for k in range(8):
                    B.mm(p2[:, :272], h[:, k, t0:t0 + 128], win[:, k, 768:1040], start=(k == 0), stop=(k == 7))
                B.act(Zg[:, t, :], p2[:, 0:256], AF.Silu)
                xa, ng = g8[0], g8[1]
                B.tt(xa, p2[:, 256:264], pbc[:, PB_DTB:PB_DTB + 8], ADD)
                B.act(xa, xa, AF.Exp)
                B.act(xa, xa, AF.Ln, bias=E["oneT"])
                B.tt(ng, xa, eal, MUL)
                B.act(betaT[:, t, :], p2[:, 264:272], AF.Sigmoid)
                pg = PB()
                B.mm(pg[:, 0:4], CM(C_CUM0), ng[:, 0:4])
                B.mm(pg[:, 4:8], CM(C_CUM1), ng[:, 4:8])
                B.mm(pg[:, 8:16], CM(C_IND0), ng)
                B.mm(pg[:, 16:24], CM(C_IND1), ng)
                B.copy(ngcT[:, t, :], pg[:, 0:8])
                B.copy(nglT[:, t], pg[:, 8:24].rr("p (c n) -> p c n", n=8))
        B.barrier()
    B.act(glT, nglT, AF.Exp, scale=-1.0)
    acc = B.sb("acc", [128, NT, 4, 64], scope=sc)
    B.memset(acc, 0.0)
    H4 = lambda p: p[:, :512].rr("p (h j) -> p h j", j=128)
    with new_scope() as sc2:
        def dirbufs():
            o = {}
            o["S32"] = B.sb("S32", [128, 2, 64], scope=sc2)
            o["Sb"] = B.sb("Sb", [128, 2, 64], BF16, scope=sc2)
            for nm in ("nglk", "rowt", "egc", "negegc", "kdf", "negb"):
                o[nm] = B.sb(nm, [128, 4], scope=sc2)
            o["ET"] = B.sb("ET", [128, 4, 128], scope=sc2)
            o["Xs"] = [B.sb("X", [128, 4, 128], scope=sc2) for _ in range(2)]
            o["XTs"] = [B.sb("XT", [128, 4, 128], scope=sc2) for _ in range(2)]
            o["Ps"] = [B.sb("P", [128, 4, 128], scope=sc2) for _ in range(2)]
            o["attnT"] = B.sb("attnT", [128, 4, 128], BF16, scope=sc2)
            o["kdec"] = B.sb("kdec", [128, 4, 64], BF16, scope=sc2)
            o["Rp"] = B.sb("Rp", [128, 4, 64], scope=sc2)
            o["vn"] = B.sb("vn", [128, 4, 64], BF16, scope=sc2)
            o["t1"] = B.sb("t1", [128, 4, 64], scope=sc2)
            o["ho"] = B.sb("ho", [128, 4, 64], scope=sc2)
            return o

        DB = [dirbufs(), dirbufs()]
        for d in range(2):
            B.memset(DB[d]["S32"], 0.0)
            B.memset(DB[d]["Sb"], 0.0)

        rings = [make_ring(E["psA"]), make_ring(E["psB"])]

        def tile_step(d, t):
            o = DB[d]
            PS = rings[d]
            S32, Sb, ET, Xs, XTs, Ps, attnT, kdec, Rp, vn, t1, ho = (o[k] for k in (
                "S32", "Sb", "ET", "Xs", "XTs", "Ps", "attnT", "kdec", "Rp", "vn", "t1", "ho"))
            nglk, rowt, egc, negegc, kdf, negb = (o[k] for k in ("nglk", "rowt", "egc", "negegc", "kdf", "negb"))
            d4 = slice(d * 4, d * 4 + 4)
            mneg = CM(C_MN0 if d == 0 else C_MN1)
            tok0 = t * 128
            ngc = ngcT[:, t, d4]
            for cl in range(2):
                hs_ = slice(cl * 64, cl * 64 + 64)
                B.copy(nglk[hs_], nglT[hs_, t, cl, d4])
            B.ts(rowt, ngc, -1.0, MUL)
            B.act(egc, ngc, AF.Exp, scale=-1.0)
            B.ts(negegc, egc, -1.0, MUL)
            B.tt(kdf, ngc, nglk, SUB)
            B.act(kdf, kdf, AF.Exp)
            B.ts(negb, betaT[:, t, d4], -1.0, MUL)
            rd = Ps[0]
            tmpA = Xs[1]
            B.tt(rd, rowt.unsq(2).bc([128, 4, 128]), ident.unsq(1).bc([128, 4, 128]), MUL)
            yield
            pe = PS()
            B.mm(pe[:, :512], ones, rd)
            for hh in range(4):
                B.stt(ET[:, hh, :], pe[:, hh * 128:(hh + 1) * 128], ngcT[:, t, d * 4 + hh:d * 4 + hh + 1], mneg, ADD, MIN)
            yield
            B.act(ET, ET, AF.Exp)
            yield
            pkk, pkq = (PS(), PS()), (PS(), PS())
            for hh in range(4):
                hb, hp = (hh % 2) * 64, hh // 2
                kt_ = KT[hb:hb + 64, hp, tok0:tok0 + 128]
                B.mm(pkk[hh % 2][:, hh * 128:(hh + 1) * 128], kt_, kt_)
                B.mm(pkq[hh % 2][:, hh * 128:(hh + 1) * 128], kt_, QT[hb:hb + 64, hp, tok0:tok0 + 128])
            yield
            for par in range(2):
                B.tt(attnT[:, par::2, :], H4(pkq[par])[:, par::2, :], ET[:, par::2, :], MUL)
                B.tt(tmpA[:, par::2, :], H4(pkk[par])[:, par::2, :], ET[:, par::2, :], MUL)
            yield
            X, XT, P = Xs[0], XTs[0], Ps[0]
            for hh in range(4):
                B.stt(X[:, hh, :], tmpA[:, hh, :], negb[:, hh:hh + 1], CM(C_OFFD), MUL, MUL)
            yield
            ptr = PS()
            for hh in range(4):
                B.tr(ptr[:, hh * 128:(hh + 1) * 128], X[:, hh, :], ident)
            yield
            B.act(XT, H4(ptr), AF.Copy)
            B.tt(P, X, ident.unsq(1).bc([128, 4, 128]), ADD)
            for lev in range(5):
                X2, XT2, P2 = Xs[(lev + 1) % 2], XTs[(lev + 1) % 2], Ps[(lev + 1) % 2]
                yield
                pXT = PS()
                for hh in range(4):
                    B.mm(pXT[:, hh * 128:(hh + 1) * 128], X[:, hh, :], XT[:, hh, :])
                yield
                B.act(XT2, H4(pXT), AF.Copy)
                if lev < 4:
                    pX = PS()
                    for hh in range(4):
                        B.mm(pX[:, hh * 128:(hh + 1) * 128], XT[:, hh, :], X[:, hh, :])
                    B.act(X2, H4(pX), AF.Copy)
                yield
                pP = PS()
                for hh in range(4):
                    B.mm(pP[:, hh * 128:(hh + 1) * 128], XT2[:, hh, :], P[:, hh, :])
                yield
                B.tt(P2, H4(pP), P, ADD)
                X, XT, P = X2, XT2, P2
            yield
            B.tt(kdec, Ktok[:, t], kdf.unsq(2).bc([128, 4, 64]), MUL)
            for cl in ((0, 1) if d == 0 else (1, 0)):
                cb = cl * 64
                cs = slice(cb, cb + 64)
                yield
                pks = (PS(), PS())
                for hh in range(4):
                    hb, hp = (hh % 2) * 64, hh // 2
                    B.mm(pks[hh % 2][cs, hh * 64:(hh + 1) * 64], KT[hb:hb + 64, hp, tok0 + cb:tok0 + cb + 64], Sb[hb:hb + 64, hp, :])
                yield
                for hh in range(4):
                    B.stt(Rp[cs, hh, :], pks[hh % 2][cs, hh * 64:(hh + 1) * 64], negegc[cs, hh:hh + 1], Vtok[cs, t, hh, :], MUL, ADD)
                yield
                pv = PS()
                for hh in range(4):
                    B.mm(pv[cs, hh * 64:(hh + 1) * 64], P[cs, hh, cb:cb + 64], Rp[cs, hh, :])
                yield
                for hh in range(4):
                    B.act(vn[cs, hh, :], pv[cs, hh * 64:(hh + 1) * 64], AF.Copy, scale=betaT[cs, t, d * 4 + hh:d * 4 + hh + 1])
                yield
                pq, pa = (PS(), PS()), PS()
                for hh in range(4):
                    hb, hp = (hh % 2) * 64, hh // 2
                    B.mm(pq[hh % 2][cs, hh * 64:(hh + 1) * 64], QT[hb:hb + 64, hp, tok0 + cb:tok0 + cb + 64], Sb[hb:hb + 64, hp, :])
                    B.mm(pa[cs, hh * 64:(hh + 1) * 64], attnT[cs, hh, cb:cb + 64], vn[cs, hh, :])
                yield
                for par in range(2):
                    B.tt(t1[cs, par::2, :], pq[par][cs, 0:256].rr("p (h e) -> p h e", e=64)[:, par::2, :],
                         egc[cs, par::2].unsq(2).bc([64, 2, 64]), MUL)
                B.tt(ho[cs], t1[cs], pa[cs, 0:256].rr("p (h e) -> p h e", e=64), ADD)
                B.tt(acc[cs, t], acc[cs, t], ho[cs], ADD, eng="pool")
                yield
                pu = PS()
                for hh in range(4):
                    hb, hp = (hh % 2) * 64, hh // 2
                    B.mm(pu[hb:hb + 64, hp * 64:(hp + 1) * 64], kdec[cs, hh, :], vn[cs, hh, :])
                yield
                for hh in range(4):
                    hb, hp = (hh % 2) * 64, hh // 2
                    B.stt(S32[hb:hb + 64, hp, :], S32[hb:hb + 64, hp, :], glT[hb:hb + 64, t, cl, d * 4 + hh:d * 4 + hh + 1],
                          pu[hb:hb + 64, hp * 64:(hp + 1) * 64], MUL, ADD)
                B.act(Sb, S32, AF.Copy)

        order = [list(range(NT)), [1, 0] + list(range(NT - 1, 1, -1))]
        def dir_gen(d):
            for t in order[d]:
                yield from tile_step(d, t)

        lockstep([lambda slot: dir_gen(0), lambda slot: dir_gen(1)], 2)
        B.barrier()
    mixT = S["mixT"] = B.sb("mixT", [128, 2, TOK], BF16, scope=sc)
    ho = B.sb("ho", [128, 4, 64], scope=sc)
    dd = B.sb("dd", [128, 4], scope=sc)
    gnw = pbc[:, PB_GNW:PB_GNW + 64].unsq(1).bc([128, 4, 64])
    hn = B.sb("hn", [128, 4, 64], scope=sc)
    for t in range(NT):
        B.tt(ho, acc[:, t], acc[:, t], MUL)
        B.red(dd, ho, ADD)
        B.act(dd, dd, AF.Sqrt, scale=1.0 / 64, bias=E["epsT"])
        B.recip(dd, dd)
        B.tt(hn, acc[:, t], dd.unsq(2).bc([128, 4, 64]), MUL)
        B.tt(hn, hn, gnw, MUL)
        B.tt(hn, hn, Zg[:, t, :].rr("p (h d) -> p h d", d=64), MUL)
        for pc_ in range(2):
            pt = PB()
            B.tr(pt[:, :128], hn[:, pc_ * 2:pc_ * 2 + 2, :].rr("p h d -> p (h d)"), ident)
            B.act(mixT[:, pc_, t * 128:(t + 1) * 128], pt[:, :128], AF.Copy)
    if "dbg" in io:
        io["dbg"][(l, "gdn")] = np.array(mixT.a, dtype=np.float32)


L_DEPTH = 4
IO_SPECS = [
    ("xT", [1024, 2048]), ("ctxT", [1024, 256]), ("cvT", [128, 8, 2]), ("pfm", [128, L_DEPTH * PF_L + 8]),
    ("pbc", [L_DEPTH, PB_L]), ("consts", [128, NCONST * 128]), ("ropec", [128, 2048]), ("ropes", [128, 2048]),
    ("gmlp_wsT", [L_DEPTH, 4, 128, 128]), ("gmlp_bsr", [L_DEPTH, 128, 2, 128]),
    ("ada_w", [L_DEPTH, 1024, 6144]), ("w_in", [L_DEPTH, 1024, 3104]), ("w_out", [L_DEPTH, 1024, 1024]),
    ("mlp_w1", [L_DEPTH, 1024, 4096]), ("mlp_w2", [L_DEPTH, 4096, 1024]),
]


def prep_shared(inp):
    f = lambda a: np.ascontiguousarray(np.asarray(a, dtype=np.float32))
    L = L_DEPTH
    pfm = np.zeros((128, L * PF_L + 8), np.float32)
    pbc = np.zeros((L, PB_L), np.float32)
    fm = lambda v: f(v).reshape(-1, 128).T
    for l in range(L):
        o = l * PF_L
        pfm[:, o:o + 8] = fm(inp["norm1_w"][l])
        pfm[:, o + 8:o + 16] = fm(inp["norm2_w"][l])
        pfm[:, o + 16:o + 64] = fm(inp["ada_b"][l])
        pfm[:, o + 64:o + 94] = f(inp["gdn_conv_w"][l]).reshape(5, 6, 128).transpose(2, 0, 1).reshape(128, 30)
        pbc[l] = np.concatenate([f(inp["gdn_a_log"][l]).ravel(), f(inp["gdn_dt_bias"][l]).ravel(),
                                 f(inp["gdn_norm_w"][l]).ravel(), f(inp["swa_sink"][l]).ravel(),
                                 f(inp["gmlp_norm_w"][l]).ravel(), f(inp["mlstm_ig_bias"][l]).ravel(),
                                 f(inp["mlstm_fg_bias"][l]).ravel(), f(inp["mlstm_norm_w"][l]).ravel()])
    pfm[:, L * PF_L:] = fm(inp["final_norm_w"])
    bs = f(inp["gmlp_b_s"])
    bsr = np.repeat(bs.reshape(L, 2, 2, 1, 128), 64, axis=3)
    bsr = bsr.transpose(0, 2, 3, 1, 4).reshape(L, 128, 2, 128)
    rc, rs = make_rope()
    return {"pfm": pfm, "pbc": pbc, "consts": make_consts(), "ropec": rc, "ropes": rs,
            "gmlp_wsT": f(np.asarray(inp["gmlp_w_s"]).transpose(0, 1, 3, 2)), "gmlp_bsr": f(bsr),
            "ada_w": f(inp["ada_w"]), "w_in": f(inp["w_in"]), "w_out": f(inp["w_out"]),
            "mlp_w1": f(inp["mlp_w1"]), "mlp_w2": f(inp["mlp_w2"])}


def prep_core(inp, b):
    f = lambda a: np.ascontiguousarray(np.asarray(a, dtype=np.float32))
    cv = np.stack([np.asarray(inp["c"])[b], np.asarray(inp["c_ctx"])], axis=-1)
    return {"xT": f(np.asarray(inp["x"])[b].T), "ctxT": f(np.asarray(inp["ctx"])[b].T),
            "cvT": f(cv.reshape(8, 128, 2).transpose(1, 0, 2))}


def build_nc(**kw):
    nc = bass.Bass("TRN2", target_bir_lowering=False)
    with ExitStack() as es:
        B = BassBackend(nc, es)
        io = {}
        for name, shape in IO_SPECS:
            io[name] = B.dram(name, shape, F32, "ExternalInput")
        io["outT"] = B.dram("outT", [1024, 2048], F32, "ExternalOutput")
        io["hd"] = B.dram("hd_scratch", [128, 8, TOK], BF16, "Internal")
        build_model(B, io, **kw)
        print("instructions:", B.n_ins, {e: B.seq[e] for e in B.seq}, flush=True)
    return nc


def kernel(**inputs):
    shared = prep_shared(inputs)
    nc = build_nc()
    in_maps = []
    for b in range(8):
        m = dict(shared)
        m.update(prep_core(inputs, b))
        in_maps.append(m)
    res = run_bass_kernel_spmd(nc, in_maps, core_ids=list(range(8)))
    out = np.stack([np.asarray(r["outT"]).T for r in res.results], axis=0)
    return np.ascontiguousarray(out.astype(np.float32))
```

```python
import math
import numpy as np
from contextlib import ExitStack
import concourse.bass as bass
import concourse.mybir as mybir
from concourse.bass_utils import run_bass_kernel_spmd

F32 = mybir.dt.float32
BF16 = mybir.dt.bfloat16
AF = mybir.ActivationFunctionType
ALU = mybir.AluOpType
AX = mybir.AxisListType

NEG = -30000.0


class Res:
    __slots__ = ("w", "rs", "excl")

    def __init__(self, excl=False):
        self.w = None
        self.rs = []
        self.excl = excl


class V:
    __slots__ = ("ap", "res")

    def __init__(self, ap, res):
        self.ap = ap
        self.res = res

    def __getitem__(self, k):
        return V(self.ap[k], self.res)

    def bc(self, shape):
        return V(self.ap.to_broadcast(list(shape)), self.res)

    def unsq(self, ax):
        return V(self.ap.unsqueeze(ax), self.res)

    def bitcast(self, dt):
        return V(self.ap.bitcast(dt), self.res)

    def rr(self, pat, **kw):
        return V(self.ap.rearrange(pat, **kw), self.res)

    def pbc(self, n):
        return V(self.ap.partition_broadcast(n), self.res)

    @property
    def shape(self):
        return tuple(self.ap.shape)


class BassBackend:
    LIM = 20000
    NSLOT = 12

    def __init__(self, nc, es):
        self.nc = nc
        self.es = es
        self.eng = {"pe": nc.tensor, "act": nc.scalar, "dve": nc.vector, "pool": nc.gpsimd, "sp": nc.sync}
        names = list(self.eng)
        self.sems = {e: [] for e in names}
        self.seq = {e: 0 for e in names}
        self.seen = {e: {f: 0 for f in names} for e in names}
        self.hist = {e: [] for e in names}
        self.dq = {}
        for q in ("sp", "pool", "act"):
            self.dq[q] = {"sems": [es.enter_context(nc.semaphore("d%s%d" % (q, i))) for i in range(self.NSLOT)],
                          "cnt": [0] * self.NSLOT, "next": 0}
        self.dseen = {e: {} for e in names}
        self.n_ins = 0
        self._uid = 0

    def sb(self, name, shape, dtype=F32, scope=None):
        self._uid += 1
        t = (scope or self.es).enter_context(self.nc.sbuf_tensor("%s_%d" % (name, self._uid), list(shape), dtype))
        return V(t[:] if len(shape) == 2 else t[tuple([slice(None)] * len(shape))], (Res(),))

    def ps(self, name, shape, dtype=F32):
        t = self.es.enter_context(self.nc.psum_tensor(name, list(shape), dtype))
        return V(t[tuple([slice(None)] * len(shape))], (Res(excl=True),))

    def dram(self, name, shape, dtype, kind):
        t = self.nc.dram_tensor(name, list(shape), dtype, kind=kind)
        return V(t.ap(), ())

    def _sem(self, e, sq):
        i = (sq - 1) // self.LIM
        while len(self.sems[e]) <= i:
            self.sems[e].append(self.es.enter_context(self.nc.semaphore("s%s%d" % (e, len(self.sems[e])))))
        return self.sems[e][i], (sq - 1) % self.LIM + 1

    def _merge(self, e, snap):
        se = self.seen[e]
        for f, v in snap[0].items():
            if v > se[f]:
                se[f] = v
        de = self.dseen[e]
        for k, v in snap[1].items():
            if v > de.get(k, 0):
                de[k] = v

    def _wait_tok(self, e, tok, skip_same):
        if tok[0] == "e":
            _, f, sq = tok
            if f == e and skip_same:
                return
            if self.seen[e][f] >= sq:
                return
            sem, val = self._sem(f, sq)
            self.eng[e].wait_ge(sem, val)
            self.seen[e][f] = sq
            self._merge(e, self.hist[f][sq - 1])
        else:
            _, q, slot, cnt, snap = tok
            if self.dseen[e].get((q, slot), 0) >= cnt:
                return
            self.eng[e].wait_ge(self.dq[q]["sems"][slot], cnt)
            self.dseen[e][(q, slot)] = cnt
            self._merge(e, snap)

    def _sync(self, e, reads, writes, skip_same=False):
        for r in reads:
            for res in r.res:
                if res.w is not None:
                    self._wait_tok(e, res.w, skip_same)
                if res.excl:
                    for t in res.rs:
                        if t[1] != e:
                            self._wait_tok(e, t, skip_same)
        for w in writes:
            for res in w.res:
                if res.w is not None:
                    self._wait_tok(e, res.w, skip_same)
                for t in res.rs:
                    self._wait_tok(e, t, skip_same)

    def _snap(self, e):
        return (dict(self.seen[e]), dict(self.dseen[e]))

    def _mark(self, tok, e, reads, writes):
        for r in reads:
            for res in r.res:
                res.rs = [t for t in res.rs if not (t[0] == "e" and t[1] == e)] + [tok]
        for w in writes:
            for res in w.res:
                res.w = tok
                res.rs = []

    def _commit(self, e, ins, reads, writes):
        self.seq[e] += 1
        sq = self.seq[e]
        sem, val = self._sem(e, sq)
        ins.then_inc(sem, 1)
        self.hist[e].append(self._snap(e))
        self._mark(("e", e, sq), e, reads, writes)
        self.n_ins += 1

    def dma(self, out, in_, q="sp"):
        e = q
        self._sync(e, [in_], [out])
        d = self.dq[q]
        slot = d["next"]
        d["next"] = (slot + 1) % self.NSLOT
        if d["cnt"][slot] > self.dseen[e].get((q, slot), 0):
            self.eng[e].wait_ge(d["sems"][slot], d["cnt"][slot])
            self.dseen[e][(q, slot)] = d["cnt"][slot]
        ins = self.eng[e].dma_start(out=out.ap, in_=in_.ap)
        d["cnt"][slot] += 16
        ins.then_inc(d["sems"][slot], 16)
        tok = ("d", q, slot, d["cnt"][slot], self._snap(e))
        self._mark(tok, e, [in_], [out])
        self.n_ins += 1

    def mm(self, out, lhsT, rhs, start=True, stop=True):
        self._sync("pe", [lhsT, rhs], [out], skip_same=True)
        ins = self.nc.tensor.matmul(out.ap, lhsT=lhsT.ap, rhs=rhs.ap, start=start, stop=stop)
        self._commit("pe", ins, [lhsT, rhs], [out])

    def tr(self, out, in_, ident):
        self._sync("pe", [in_, ident], [out], skip_same=True)
        ins = self.nc.tensor.transpose(out.ap, in_.ap, ident.ap)
        self._commit("pe", ins, [in_, ident], [out])

    def act(self, out, in_, func, bias=None, scale=1.0, accum=None):
        rd = [in_] + [x for x in (bias, scale) if isinstance(x, V)]
        wr = [out] + ([accum] if accum is not None else [])
        self._sync("act", rd, wr)
        kw = {}
        if bias is not None:
            kw["bias"] = bias.ap if isinstance(bias, V) else float(bias)
        if accum is not None:
            kw["accum_out"] = accum.ap
        ins = self.nc.scalar.activation(out=out.ap, in_=in_.ap, func=func,
                                        scale=(scale.ap if isinstance(scale, V) else float(scale)), **kw)
        self._commit("act", ins, rd, wr)

    def tt(self, out, a, b, op, eng="dve"):
        self._sync(eng, [a, b], [out])
        ins = self.eng[eng].tensor_tensor(out=out.ap, in0=a.ap, in1=b.ap, op=op)
        self._commit(eng, ins, [a, b], [out])

    def ts(self, out, a, s1, op0, s2=None, op1=None, eng="dve"):
        rd = [a] + [x for x in (s1, s2) if isinstance(x, V)]
        self._sync(eng, rd, [out])
        f = lambda x: x.ap if isinstance(x, V) else (None if x is None else float(x))
        if op1 is None:
            ins = self.eng[eng].tensor_scalar(out=out.ap, in0=a.ap, scalar1=f(s1), scalar2=None, op0=op0)
        else:
            ins = self.eng[eng].tensor_scalar(out=out.ap, in0=a.ap, scalar1=f(s1), scalar2=f(s2), op0=op0, op1=op1)
        self._commit(eng, ins, rd, [out])

    def stt(self, out, a, s, b, op0, op1, eng="dve"):
        rd = [a, b] + ([s] if isinstance(s, V) else [])
        self._sync(eng, rd, [out])
        ins = self.eng[eng].scalar_tensor_tensor(out=out.ap, in0=a.ap, scalar=(s.ap if isinstance(s, V) else float(s)),
                                                 in1=b.ap, op0=op0, op1=op1)
        self._commit(eng, ins, rd, [out])

    def red(self, out, in_, op, eng="dve"):
        self._sync(eng, [in_], [out])
        ins = self.eng[eng].tensor_reduce(out=out.ap, in_=in_.ap, axis=AX.X, op=op)
        self._commit(eng, ins, [in_], [out])

    def copy(self, out, in_, eng="dve"):
        if eng == "act":
            return self.act(out, in_, AF.Copy)
        self._sync(eng, [in_], [out])
        ins = self.eng[eng].tensor_copy(out=out.ap, in_=in_.ap)
        self._commit(eng, ins, [in_], [out])

    def memset(self, out, val, eng="dve"):
        self._sync(eng, [], [out])
        ins = self.eng[eng].memset(out.ap, float(val))
        self._commit(eng, ins, [], [out])

    def recip(self, out, in_):
        self._sync("dve", [in_], [out])
        ins = self.nc.vector.reciprocal(out=out.ap, in_=in_.ap)
        self._commit("dve", ins, [in_], [out])

    def finish(self):
        for q, d in self.dq.items():
            for slot in range(self.NSLOT):
                if d["cnt"][slot] > self.dseen["sp"].get((q, slot), 0):
                    self.nc.sync.wait_ge(d["sems"][slot], d["cnt"][slot])


    def barrier(self):
        names = list(self.eng)
        toks = [("e", f, self.seq[f]) for f in names if self.seq[f] > 0]
        for q, d in self.dq.items():
            for slot in range(self.NSLOT):
                if d["cnt"][slot] > 0:
                    toks.append(("d", q, slot, d["cnt"][slot], ({}, {})))
        for e in ("pe", "act", "dve", "pool", "sp"):
            for t in toks:
                self._wait_tok(e, t, False)


def make_ring(tiles):
    ctr = [0]

    def nxt():
        ctr[0] += 1
        return tiles[ctr[0] % len(tiles)]
    return nxt


def lockstep(factories, width):
    it = iter(factories)
    active = []
    free = list(range(width))
    exhausted = False
    while True:
        while free and not exhausted:
            f = next(it, None)
            if f is None:
                exhausted = True
                break
            slot = free.pop(0)
            active.append((slot, f(slot)))
        if not active:
            break
        for item in list(active):
            try:
                next(item[1])
            except StopIteration:
                active.remove(item)
                free.append(item[0])


D = 1024
NT = 18
TOK = 2304
NCTX = 256
CTS = [(0, 256), (256, 768), (768, 1280), (1280, 1792), (1792, 2304)]
C_ID, C_ONES, C_BLK, C_CUM0, C_CUM1, C_IND0, C_IND1, C_S63, C_S127, C_S0, C_S64, C_MN0, C_MN1, C_OFFD, C_RMT, C_BAND = range(16)
NCONST = 18
PF_L = 94
PB_ALOG, PB_DTB, PB_GNW, PB_SINK, PB_MNW, PB_IGB, PB_FGB, PB_LNW = 0, 8, 16, 80, 84, 340, 348, 356
PB_L = 612


def make_consts():
    c = np.zeros((NCONST, 128, 128), np.float32)
    k = np.arange(128)[:, None]
    i = np.arange(128)[None, :]
    same = (k // 64) == (i // 64)
    c[C_ID] = (k == i)
    c[C_ONES] = 1.0
    c[C_BLK] = same
    c[C_CUM0] = same & (k <= i)
    c[C_CUM1] = same & (k >= i)
    c[C_IND0] = (k < 64) & (i >= 0)
    c[C_IND1] = (k >= 64) & (i >= 0)
    c[C_S63] = (k == 63) & (i >= 0)
    c[C_S127] = (k == 127) & (i >= 0)
    c[C_S0] = (k == 0) & (i >= 0)
    c[C_S64] = (k == 64) & (i >= 0)
    c[C_MN0] = np.where(same & (k <= i), 0.0, NEG)
    c[C_MN1] = np.where(same & (k >= i), 0.0, NEG)
    c[C_OFFD] = (k != i)
    rmt = np.zeros((128, 128), np.float32)
    for hb in (0, 64):
        for d in range(32):
            rmt[hb + d + 32, hb + d] = -1.0
            rmt[hb + d, hb + d + 32] = 1.0
    c[C_RMT] = rmt
    qi = np.arange(128)[:, None]
    kj = np.arange(384)[None, :]
    band = np.where(np.abs(kj - 128 - qi) <= 128, 0.0, NEG).astype(np.float32)
    for t in range(3):
        c[C_BAND + t] = band[:, t * 128:(t + 1) * 128]
    return np.ascontiguousarray(c.transpose(1, 0, 2).reshape(128, NCONST * 128))


def make_rope():
    rows = 2048 // 64
    row = np.repeat(np.arange(rows), 64).astype(np.float32)
    col = (np.arange(2048) % 64).astype(np.float32)
    inv = np.power(np.float32(10000.0), -np.arange(16, dtype=np.float32) / 16).astype(np.float32)
    ang = np.concatenate([row[:, None] * inv, col[:, None] * inv], axis=-1).astype(np.float32)
    cos = np.cos(ang).astype(np.float32).T
    sin = np.sin(ang).astype(np.float32).T
    cc = np.concatenate([cos, cos, cos, cos], axis=0)
    ss = np.concatenate([sin, sin, sin, sin], axis=0)
    return np.ascontiguousarray(cc), np.ascontiguousarray(ss)


def build_model(B, io, L=4, mixers=("gmlp", "swa", "mlstm", "gdn"), do_mlp=True, new_scope=ExitStack):
    MUL, ADD, SUB, MAX, MIN = ALU.mult, ALU.add, ALU.subtract, ALU.max, ALU.min
    cst = B.sb("cst", [128, NCONST, 128])
    B.dma(cst, io["consts"].rr("p (n c) -> p n c", c=128))
    CM = lambda i: cst[:, i, :]
    ident, ones = CM(C_ID), CM(C_ONES)
    identb = B.sb("identb", [128, 128], BF16)
    B.copy(identb, ident)
    xT = B.sb("xT", [128, 8, TOK])
    xres = [[Res() for _ in CTS] for _ in range(8)]

    def X(c, ci):
        a, b = CTS[ci]
        v = xT[:, c, a:b]
        v.res = (xres[c][ci],)
        return v

    for c in range(8):
        for ci in range(5):
            a, b = CTS[ci]
            if ci == 0:
                B.dma(X(c, 0), io["ctxT"][c * 128:(c + 1) * 128, :])
            else:
                B.dma(X(c, ci), io["xT"][c * 128:(c + 1) * 128, a - 256:b - 256])
    pfm = B.sb("pfm", [128, L_DEPTH * PF_L + 8])
    B.dma(pfm, io["pfm"])
    epsT = B.sb("epsT", [128, 1])
    B.memset(epsT, 1e-6)
    oneT = B.sb("oneT", [128, 1])
    B.memset(oneT, 1.0)
    psA = [B.ps("psA%d" % i, [128, 512]) for i in range(4)]
    psB = [B.ps("psB%d" % i, [128, 512]) for i in range(4)]
    pctr = [0, 0]

    def PA():
        pctr[0] += 1
        return psA[pctr[0] % 4]

    def PB():
        pctr[1] += 1
        return psB[pctr[1] % 4]

    mod = B.sb("mod", [128, L, 48, 2])
    ns1 = B.sb("ns1", [128, L, 8, 2])
    ns2 = B.sb("ns2", [128, L, 8, 2])
    with new_scope() as sc0:
        scv = B.sb("scv", [128, 8, 2], scope=sc0)
        B.dma(scv, io["cvT"])
        B.act(scv, scv, AF.Silu)
        awb = [B.sb("awb", [128, 8, 768], scope=sc0) for _ in range(2)]
        for l in range(L):
            pm = PA()
            for s in range(8):
                aw = awb[s % 2]
                B.dma(aw, io["ada_w"][l].rr("(k p) n -> p k n", p=128)[:, :, s * 768:(s + 1) * 768])
                for nn in range(6):
                    n = s * 6 + nn
                    for k in range(8):
                        B.mm(pm[:, n * 2:n * 2 + 2], aw[:, k, nn * 128:(nn + 1) * 128], scv[:, k, :],
                             start=(k == 0), stop=(k == 7))
            adab = pfm[:, l * PF_L + 16:l * PF_L + 64]
            B.tt(mod[:, l], pm[:, 0:96].rr("p (a b) -> p a b", b=2), adab.unsq(2).bc([128, 48, 2]), ADD)
            for (ns, so, no) in ((ns1, 8, 0), (ns2, 32, 8)):
                B.ts(ns[:, l], mod[:, l, so:so + 8, :], 1.0, ADD)
                B.tt(ns[:, l], ns[:, l], pfm[:, l * PF_L + no:l * PF_L + no + 8].unsq(2).bc([128, 8, 2]), MUL)
        B.barrier()

    def normmod(ci, ns, sh, hout, S):
        a, b = CTS[ci]
        w = b - a
        s = 1 if ci == 0 else 0
        pss = PA()
        for c in range(8):
            sq = S["sq"][c % 2]
            B.act(sq[:, :w], X(c, ci), AF.Square)
            B.mm(pss[:, :w], ones, sq[:, :w], start=(c == 0), stop=(c == 7))
        rstd = S["rstd"]
        B.act(rstd[:, :w], pss[:, :w], AF.Sqrt, scale=1.0 / D, bias=epsT)
        B.recip(rstd[:, :w], rstd[:, :w])
        for c in range(8):
            tmp = S["tmp"][c % 2]
            B.tt(tmp[:, :w], X(c, ci), rstd[:, :w], MUL)
            B.act(hout[:, c, :w], tmp[:, :w], AF.Identity, scale=ns[:, c, s:s + 1], bias=sh[:, c, s:s + 1])

    def norm_scratch(scope):
        return {"sq": [B.sb("sq", [128, 512], scope=scope) for _ in range(2)],
                "tmp": [B.sb("tmpn", [128, 512], scope=scope) for _ in range(2)],
                "rstd": B.sb("rstd", [128, 512], scope=scope)}

    def load_w(dst, src3, q="pool"):
        for k in range(8):
            B.dma(dst[:, k, :], src3[:, k, :], q=q)

    def residual(l, mixT, wo, kc, ci_list, gate_off):
        for ci in ci_list:
            a, b = CTS[ci]
            w = b - a
            s = 1 if ci == 0 else 0
            for n in range(8):
                pp = PB()
                for k in range(kc):
                    B.mm(pp[:, :w], wo[:, k, n * 128:(n + 1) * 128], mixT[:, k, a:b], start=(k == 0), stop=(k == kc - 1))
                B.stt(X(n, ci), pp[:, :w], mod[:, l, gate_off + n, s:s + 1], X(n, ci), MUL, ADD)

    ctx_needed = lambda l: l < L - 1
    hres = [Res() for _ in CTS]

    def HD(ci):
        a, b = CTS[ci]
        v = io["hd"][:, :, a:b]
        v.res = (hres[ci],)
        return v

    for l in range(L):
        cis = [0, 1, 2, 3, 4] if ctx_needed(l) else [1, 2, 3, 4]
        sh1 = mod[:, l, 0:8, :]
        sh2 = mod[:, l, 24:32, :]
        with new_scope() as sc:
            S = norm_scratch(sc)
            hb = [B.sb("hb", [128, 8, 512], BF16, scope=sc) for _ in range(2)]
            for ci in range(5):
                a, b = CTS[ci]
                normmod(ci, ns1[:, l], sh1, hb[ci % 2], S)
                B.dma(HD(ci), hb[ci % 2][:, :, :b - a])
            B.barrier()

        def loadh(ci, h):
            a, b = CTS[ci]
            B.dma(h[:, :, :b - a], HD(ci))

        for mi, mx in enumerate(("gmlp", "swa", "mlstm", "gdn")):
            if mx not in mixers:
                continue
            with new_scope() as sc:
                S = {}
                S["sc"] = sc
                S["cst"] = cst
                if mx in ("gmlp", "swa"):
                    S["h"] = [B.sb("hct", [128, 8, 512], BF16, scope=sc) for _ in range(2)]
                S["wout"] = B.sb("wout", [128, 2, 1024], BF16, scope=sc)
                S["pbc"] = B.sb("pbc", [128, PB_L], scope=sc)
                B.dma(S["pbc"], io["pbc"][l].pbc(128))
                grp = {"gdn": 0, "swa": 1, "gmlp": 2, "mlstm": 3}[mx]
                for k in range(2):
                    B.dma(S["wout"][:, k, :], io["w_out"][l, grp * 256 + k * 128:grp * 256 + (k + 1) * 128, :], q="pool")
                env = dict(B=B, io=io, l=l, S=S, CM=CM, PA=PA, PB=PB, normmod=normmod, ns1=ns1[:, l], sh1=sh1,
                           load_w=load_w, loadh=loadh, new_scope=new_scope, psA=psA, psB=psB, ident=ident, identb=identb, ones=ones, pfm=pfm, epsT=epsT, oneT=oneT)
                {"gmlp": mixer_gmlp, "swa": mixer_swa, "mlstm": mixer_mlstm, "gdn": mixer_gdn}[mx](env)
                residual(l, S["mixT"], S["wout"], 2, cis, 16)
                B.barrier()
        if do_mlp:
            with new_scope() as sc:
                hT = B.sb("hT", [128, 8, TOK], BF16, scope=sc)
                with new_scope() as scn:
                    S = norm_scratch(scn)
                    for ci in cis:
                        a, b = CTS[ci]
                        normmod(ci, ns2[:, l], sh2, hT[:, :, a:b], S)
                    B.barrier()
                w1s = [B.sb("w1", [128, 8, 1024], BF16, scope=sc) for _ in range(2)]
                w2s = [B.sb("w2", [128, 8, 1024], BF16, scope=sc) for _ in range(2)]
                hid = [B.sb("hid", [128, 8, 512], BF16, scope=sc) for _ in range(2)]
                rl = [B.sb("rl", [128, 512], scope=sc) for _ in range(2)]

                def ldq(qq):
                    load_w(w1s[qq % 2], io["mlp_w1"][l].rr("(k p) n -> p k n", p=128)[:, :, qq * 1024:(qq + 1) * 1024])
                    load_w(w2s[qq % 2], io["mlp_w2"][l, qq * 1024:(qq + 1) * 1024, :].rr("(k p) n -> p k n", p=128))

                ldq(0)
                for qq in range(4):
                    if qq < 3:
                        ldq(qq + 1)
                    w1, w2 = w1s[qq % 2], w2s[qq % 2]
                    for ci in cis:
                        a, b = CTS[ci]
                        w = b - a
                        s = 1 if ci == 0 else 0
                        hd = hid[ci % 2]
                        for f in range(8):
                            pp = PA()
                            for k in range(8):
                                B.mm(pp[:, :w], w1[:, k, f * 128:(f + 1) * 128], hT[:, k, a:b], start=(k == 0), stop=(k == 7))
                            r = rl[f % 2]
                            B.act(r[:, :w], pp[:, :w], AF.Relu)
                            B.tt(hd[:, f, :w], r[:, :w], r[:, :w], MUL, eng="pool")
                        for n in range(8):
                            pp = PB()
                            for f in range(8):
                                B.mm(pp[:, :w], w2[:, f, n * 128:(n + 1) * 128], hd[:, f, :w], start=(f == 0), stop=(f == 7))
                            B.stt(X(n, ci), pp[:, :w], mod[:, l, 40 + n, s:s + 1], X(n, ci), MUL, ADD)
                B.barrier()

    with new_scope() as sc:
        S = norm_scratch(sc)
        fo = [B.sb("fo", [128, 512], scope=sc) for _ in range(2)]
        fw = pfm[:, L_DEPTH * PF_L:L_DEPTH * PF_L + 8]
        for ci in range(1, 5):
            a, b = CTS[ci]
            w = b - a
            pss = PA()
            for c in range(8):
                sq = S["sq"][c % 2]
                B.act(sq[:, :w], X(c, ci), AF.Square)
                B.mm(pss[:, :w], ones, sq[:, :w], start=(c == 0), stop=(c == 7))
            rstd = S["rstd"]
            B.act(rstd[:, :w], pss[:, :w], AF.Sqrt, scale=1.0 / D, bias=epsT)
            B.recip(rstd[:, :w], rstd[:, :w])
            for c in range(8):
                o = fo[c % 2]
                B.stt(o[:, :w], X(c, ci), fw[:, c:c + 1], rstd[:, :w], MUL, MUL)
                B.dma(io["outT"][c * 128:(c + 1) * 128, a - 256:b - 256], o[:, :w])
    B.finish()


def mixer_gmlp(E):
    B, io, l, S, PA, PB = E["B"], E["io"], E["l"], E["S"], E["PA"], E["PB"]
    sc = S["sc"]
    MUL, ADD = ALU.mult, ALU.add
    win = B.sb("win", [128, 8, 512], BF16, scope=sc)
    E["load_w"](win, io["w_in"][l].rr("(k p) n -> p k n", p=128)[:, :, 1552:2064])
    wsT = B.sb("wsT", [128, 4, 128], BF16, scope=sc)
    B.dma(wsT, io["gmlp_wsT"][l].rr("g q p -> q g p"), q="pool")
    bs = B.sb("bs", [128, 2, 128], scope=sc)
    B.dma(bs, io["gmlp_bsr"][l])
    nw = S["pbc"][:, PB_MNW:PB_MNW + 256]
    uT = B.sb("uT", [128, 2, 512], scope=sc)
    vg = B.sb("vg", [128, 256], scope=sc)
    junk = B.sb("junk", [128, 256], scope=sc)
    vn = B.sb("vn", [128, 256], BF16, scope=sc)
    ssq = B.sb("ssq", [128, 1], scope=sc)
    tmpm = B.sb("tmpm", [128, 128], scope=sc)
    mixT = S["mixT"] = B.sb("mixT", [128, 2, TOK], BF16, scope=sc)
    E["loadh"](0, S["h"][0])
    for ci in range(5):
        a, b = CTS[ci]
        w = b - a
        h = S["h"][ci % 2]
        if ci < 4:
            E["loadh"](ci + 1, S["h"][(ci + 1) % 2])
        for c in range(2):
            pp = PA()
            for k in range(8):
                B.mm(pp[:, :w], win[:, k, c * 128:(c + 1) * 128], h[:, k, :w], start=(k == 0), stop=(k == 7))
            B.act(uT[:, c, :w], pp[:, :w], AF.Gelu_apprx_tanh)
        for ti in range(w // 128):
            t0 = ti * 128
            pv = PA()
            for k in range(8):
                B.mm(pv[:, :256], h[:, k, t0:t0 + 128], win[:, k, 256:512], start=(k == 0), stop=(k == 7))
            B.act(vg, pv[:, :256], AF.Gelu_apprx_tanh)
            B.memset(ssq, 0.0)
            B.act(junk, vg, AF.Square, accum=ssq)
            B.act(ssq, ssq, AF.Sqrt, scale=1.0 / 256, bias=E["epsT"])
            B.recip(ssq, ssq)
            B.stt(vn, vg, ssq, nw, MUL, MUL)
            for pc in range(2):
                pm_ = PB()
                for gi in range(2):
                    g = pc * 2 + gi
                    B.mm(pm_[gi * 64:(gi + 1) * 64, :128], vn[:, g * 64:(g + 1) * 64], wsT[:, g, :])
                B.tt(tmpm, pm_[:, :128], bs[:, pc, :], ADD)
                B.tt(mixT[:, pc, a + t0:a + t0 + 128], tmpm, uT[:, pc, t0:t0 + 128], MUL, eng="pool")
    if "dbg" in io:
        io["dbg"][(l, "gmlp")] = np.array(mixT.a, dtype=np.float32)


def mixer_swa(E):
    B, io, l, S, PA, PB, CM = E["B"], E["io"], E["l"], E["S"], E["PA"], E["PB"], E["CM"]
    sc = S["sc"]
    MUL, ADD, MAX = ALU.mult, ALU.add, ALU.max
    ident = E["ident"]
    win = B.sb("win", [128, 8, 512], BF16, scope=sc)
    E["load_w"](win, io["w_in"][l].rr("(k p) n -> p k n", p=128)[:, :, 1040:1552])
    qT = B.sb("qT", [128, 2, TOK], BF16, scope=sc)
    kT = B.sb("kT", [128, TOK], BF16, scope=sc)
    vtok = B.sb("vtok", [128, NT, 128], BF16, scope=sc)
    rc_l = [B.sb("rc", [128, 512], scope=sc) for _ in range(2)]
    rs_l = [B.sb("rs", [128, 512], scope=sc) for _ in range(2)]
    raw_l = [B.sb("raw", [128, 512], scope=sc) for _ in range(2)]
    t1_l = [B.sb("t1", [128, 512], scope=sc) for _ in range(2)]
    t2_l = [B.sb("t2", [128, 512], scope=sc) for _ in range(2)]
    fctr = 0
    sink = S["pbc"][:, PB_SINK:PB_SINK + 4]
    mixT = S["mixT"] = B.sb("mixT", [128, 2, TOK], BF16, scope=sc)
    E["loadh"](0, S["h"][0])
    for ci in range(5):
        a, b = CTS[ci]
        w = b - a
        h = S["h"][ci % 2]
        if ci < 4:
            E["loadh"](ci + 1, S["h"][(ci + 1) % 2])
        rc, rs_ = rc_l[ci % 2], rs_l[ci % 2]
        if ci > 0:
            B.dma(rc, io["ropec"][:, a - 256:b - 256])
            B.dma(rs_, io["ropes"][:, a - 256:b - 256])
        for r in range(3):
            raw, t1, t2 = raw_l[fctr % 2], t1_l[fctr % 2], t2_l[fctr % 2]
            fctr += 1
            pp = PA()
            if r < 2:
                for j, hq in enumerate((r, r + 2)):
                    for k in range(8):
                        B.mm(pp[j * 64:(j + 1) * 64, :w], win[:, k, hq * 64:(hq + 1) * 64], h[:, k, :w],
                             start=(k == 0), stop=(k == 7))
            else:
                for k in range(8):
                    B.mm(pp[:, :w], win[:, k, 256:384], h[:, k, :w], start=(k == 0), stop=(k == 7))
            dst = qT[:, r, a:b] if r < 2 else kT[:, a:b]
            if ci == 0:
                B.act(dst, pp[:, :w], AF.Copy)
            else:
                B.act(raw[:, :w], pp[:, :w], AF.Copy)
                pr = PB()
                B.mm(pr[:, :w], CM(C_RMT), raw[:, :w])
                B.tt(t1[:, :w], raw[:, :w], rc[:, :w], MUL)
                B.tt(t2[:, :w], pr[:, :w], rs_[:, :w], MUL)
                B.tt(dst, t1[:, :w], t2[:, :w], ADD, eng="pool")
        for ti in range(w // 128):
            t0 = ti * 128
            pv = PA()
            for k in range(8):
                B.mm(pv[:, :128], h[:, k, t0:t0 + 128], win[:, k, 384:512], start=(k == 0), stop=(k == 7))
            B.act(vtok[:, (a + t0) // 128, :], pv[:, :128], AF.Copy)
    if "dbg" in io:
        io["dbg"]["qT"] = np.array(qT.a, dtype=np.float32)
        io["dbg"]["kT"] = np.array(kT.a, dtype=np.float32)
    bandm = E["B"].sb("bandm", [128, 384], scope=sc)
    B.copy(bandm.rr("p (a b) -> p a b", b=128), E["S"]["cst"][:, C_BAND:C_BAND + 3, :], eng="pool")
    NBUF = 4
    s_l = [B.sb("s", [128, 640], scope=sc) for _ in range(NBUF)]
    p_l = [B.sb("p", [128, 640], scope=sc) for _ in range(NBUF)]
    pT_l = [B.sb("pT", [128, 640], BF16, scope=sc) for _ in range(NBUF)]
    st_l = [B.sb("st", [128, 8], scope=sc) for _ in range(NBUF)]
    rings = [make_ring(E["psA"][0:2]), make_ring(E["psA"][2:4]), make_ring(E["psB"][0:2]), make_ring(E["psB"][2:4])]

    def unit_gen(slot, qb, hq):
        PS = rings[slot]
        s, p, pT, st = s_l[slot], p_l[slot], pT_l[slot], st_l[slot]
        mxv, negm, rsum, es, den = (st[:, i:i + 1] for i in range(5))
        q0 = qb * 128
        if qb >= 2:
            n = qb - 2
            lo, hi = max(n - 1, 0), min(n + 1, 15)
            kl0, kl1 = 256 + lo * 128, 256 + (hi + 1) * 128
            wl = kl1 - kl0
            moff = (lo - (n - 1)) * 128
        else:
            wl = 0
        W = wl + 256
        nb = W // 128
        hk = hq // 2
        r = hq % 2
        base = hk * 64
        qv = qT[base:base + 64, r, q0:q0 + 128]
        if wl:
            pa = PS()
            B.mm(pa[:, :wl], qv, kT[base:base + 64, kl0:kl1])
        pb = PS()
        B.mm(pb[:, :256], qv, kT[base:base + 64, 0:256])
        yield
        if wl:
            B.stt(s[:, :wl], pa[:, :wl], 0.125, bandm[:, moff:moff + wl], MUL, ADD)
        B.act(s[:, wl:W], pb[:, :256], AF.Copy, scale=0.125)
        yield
        B.red(mxv, s[:, :W], MAX)
        yield
        B.ts(negm, mxv, sink[:, hq:hq + 1], MAX, -1.0, MUL)
        B.memset(rsum, 0.0)
        yield
        B.act(p[:, :W], s[:, :W], AF.Exp, bias=negm, accum=rsum)
        B.act(es, sink[:, hq:hq + 1], AF.Exp, bias=negm)
        yield
        B.tt(den, rsum, es, ADD)
        yield
        B.recip(den, den)
        yield
        B.ts(p[:, :W], p[:, :W], den, MUL)
        yield
        pt1, pt2 = PS(), PS()
        for j_ in range(nb):
            B.tr((pt1 if j_ < 4 else pt2)[:, (j_ % 4) * 128:(j_ % 4 + 1) * 128], p[:, j_ * 128:(j_ + 1) * 128], ident)
        yield
        B.act(pT[:, 0:min(nb, 4) * 128], pt1[:, 0:min(nb, 4) * 128], AF.Copy)
        if nb > 4:
            B.copy(pT[:, 512:640], pt2[:, 0:128])
        yield
        po = PS()
        ob = (hq % 2) * 64
        for j_ in range(nb):
            kt = (kl0 // 128 + j_) if j_ < wl // 128 else (j_ - wl // 128)
            B.mm(po[ob:ob + 64, :128], vtok[:, kt, hk * 64:(hk + 1) * 64], pT[:, j_ * 128:(j_ + 1) * 128],
                 start=(j_ == 0), stop=(j_ == nb - 1))
        yield
        B.act(mixT[ob:ob + 64, hq // 2, q0:q0 + 128], po[ob:ob + 64, :128], AF.Copy)

    lockstep([(lambda slot, qb=qb, hq=hq: unit_gen(slot, qb, hq)) for qb in range(NT) for hq in range(4)], NBUF)
    if "dbg" in io:
        io["dbg"][(l, "swa")] = np.array(mixT.a, dtype=np.float32)


def mixer_mlstm(E):
    B, io, l, S, PA, PB, CM = E["B"], E["io"], E["l"], E["S"], E["PA"], E["PB"], E["CM"]
    sc = S["sc"]
    MUL, ADD, SUB, MAX, MIN = ALU.mult, ALU.add, ALU.subtract, ALU.max, ALU.min
    ident, ones = E["ident"], E["ones"]
    new_scope = E["new_scope"]
    mixT = S["mixT"] = B.sb("mixT", [128, 2, TOK], BF16, scope=sc)
    pbc = S["pbc"]
    QT = B.sb("QT", [128, 2, TOK], BF16, scope=sc)
    KT = B.sb("KT", [128, 2, TOK], BF16, scope=sc)
    Ktok = B.sb("Ktok", [128, NT, 4, 64], BF16, scope=sc)
    V1 = B.sb("V1", [128, NT, 4, 66], BF16, scope=sc)
    Og = B.sb("Og", [128, NT, 256], BF16, scope=sc)
    nbT = B.sb("nbT", [128, NT, 8], scope=sc)
    colT = B.sb("colT", [128, NT, 8], scope=sc)
    cmT = B.sb("cmT", [128, NT, 8], scope=sc)
    nblT = B.sb("nblT", [128, NT, 2, 8], scope=sc)
    cmlT = B.sb("cmlT", [128, NT, 2, 8], scope=sc)
    mnew = B.sb("mnew", [128, 36, 8], scope=sc)
    carry = B.sb("carry", [128, 36, 8], scope=sc)
    zer = B.sb("zer", [128, 8], scope=sc)
    B.memset(zer, 0.0)
    B.memset(V1, 1.0)
    g8 = [B.sb("g8", [128, 8], scope=sc) for _ in range(6)]
    rd_l = [B.sb("rd", [128, 8, 128], scope=sc) for _ in range(2)]
    big_l = [B.sb("big", [128, 4, 128], scope=sc) for _ in range(2)]
    with new_scope() as sc1:
        win = B.sb("win", [128, 8, 1040], BF16, scope=sc1)
        E["load_w"](win, io["w_in"][l].rr("(k p) n -> p k n", p=128)[:, :, 2064:3104])
        rings1 = [make_ring(E["psA"]), make_ring(E["psB"])]
        hs2 = [B.sb("hct", [128, 8, 512], BF16, scope=sc1) for _ in range(2)]
        E["loadh"](0, hs2[0])
        for ci in range(5):
            a, b = CTS[ci]
            w = b - a
            h = hs2[ci % 2]
            if ci < 4:
                E["loadh"](ci + 1, hs2[(ci + 1) % 2])
            for r in range(4):
                pp = PA()
                for k in range(8):
                    B.mm(pp[:, :w], win[:, k, r * 128:(r + 1) * 128], h[:, k, :w], start=(k == 0), stop=(k == 7))
                if r < 2:
                    B.act(QT[:, r, a:b], pp[:, :w], AF.Copy, scale=0.125)
                else:
                    B.act(KT[:, r - 2, a:b], pp[:, :w], AF.Copy)
            def tile_gen(slot, ti, h=h, a=a):
                PS = rings1[slot]
                t0 = ti * 128
                t = (a + t0) // 128
                p1, p2 = PS(), PS()
                for k in range(8):
                    B.mm(p1[:, :512], h[:, k, t0:t0 + 128], win[:, k, 256:768], start=(k == 0), stop=(k == 7))
                for k in range(8):
                    B.mm(p2[:, :272], h[:, k, t0:t0 + 128], win[:, k, 768:1040], start=(k == 0), stop=(k == 7))
                yield
                B.act(Ktok[:, t], p1[:, 0:256].rr("p (h d) -> p h d", d=64), AF.Copy)
                B.act(V1[:, t, :, 0:64], p1[:, 256:512].rr("p (h d) -> p h d", d=64), AF.Copy)
                B.act(Og[:, t, :], p2[:, 0:256], AF.Sigmoid)
                ig, fx, lf = g8[slot * 3], g8[slot * 3 + 1], g8[slot * 3 + 2]
                rd = rd_l[slot]
                B.tt(ig, p2[:, 256:264], pbc[:, PB_IGB:PB_IGB + 8], ADD)
                B.tt(fx, p2[:, 264:272], pbc[:, PB_FGB:PB_FGB + 8], ADD)
                yield
                B.act(fx, fx, AF.Exp, scale=-1.0)
                yield
                B.act(lf, fx, AF.Ln, bias=E["oneT"])
                yield
                pg = PS()
                B.mm(pg[:, 0:4], CM(C_CUM0), lf[:, 0:4])
                B.mm(pg[:, 4:8], CM(C_CUM1), lf[:, 4:8])
                B.mm(pg[:, 8:16], CM(C_IND0), lf)
                B.mm(pg[:, 16:24], CM(C_IND1), lf)
                yield
                B.copy(nbT[:, t, :], pg[:, 0:8])
                B.copy(nblT[:, t], pg[:, 8:24].rr("p (c n) -> p c n", n=8))
                B.tt(colT[:, t, :], ig, pg[:, 0:8], ADD)
                yield
                B.tt(rd, colT[:, t, :].unsq(2).bc([128, 8, 128]), ident.unsq(1).bc([128, 8, 128]), MUL)
                yield
                for d in range(2):
                    big = big_l[slot]
                    pc = PS()
                    B.mm(pc[:, :512], ones, rd[:, d * 4:(d + 1) * 4, :])
                    B.tt(big, pc[:, :512].rr("p (h j) -> p h j", j=128),
                         CM(C_MN1 if d == 0 else C_MN0).unsq(1).bc([128, 4, 128]), ADD)
                    B.red(cmT[:, t, d * 4:(d + 1) * 4], big, MAX)
                yield
                pl = PS()
                for cl in range(2):
                    for d in range(2):
                        sel = (C_S63, C_S127)[cl] if d == 0 else (C_S0, C_S64)[cl]
                        B.mm(pl[:, cl * 8 + d * 4:cl * 8 + d * 4 + 4], CM(sel), cmT[:, t, d * 4:(d + 1) * 4])
                yield
                B.copy(cmlT[:, t], pl[:, 0:16].rr("p (c n) -> p c n", n=8))
            lockstep([(lambda slot, ti=ti: tile_gen(slot, ti)) for ti in range(w // 128)], 2)
        B.barrier()
    acc = B.sb("acc", [128, NT, 4, 64], scope=sc)
    tmp4 = B.sb("tmp4", [128, 4], scope=sc)
    mst = {}
    for d in range(2):
        d4 = slice(d * 4, d * 4 + 4)
        order = list(range(36)) if d == 0 else [3, 2, 1, 0] + list(range(35, 3, -1))
        prev = zer[:, d4]
        for c in order:
            t, cl = divmod(c, 2)
            mst[(c, d)] = prev
            B.tt(tmp4, prev, cmlT[:, t, cl, d4], MAX)
            B.tt(mnew[:, c, d4], tmp4, nblT[:, t, cl, d4], SUB)
            B.tt(tmp4, prev, nblT[:, t, cl, d4], SUB)
            B.tt(tmp4, tmp4, mnew[:, c, d4], SUB)
            B.act(carry[:, c, d4], tmp4, AF.Exp)
            prev = mnew[:, c, d4]
    def dirbufs():
        o = {}
        o["St32"] = B.sb("St32", [128, 2, 66], scope=sc)
        o["Stb"] = B.sb("Stb", [128, 2, 66], BF16, scope=sc)
        for nm in ("mstk", "mnwk", "nblk", "inter", "mt", "rowt", "wint", "emt", "kwf", "dd"):
            o[nm] = B.sb(nm, [128, 4], scope=sc)
        o["ET"] = B.sb("ET", [128, 4, 128], scope=sc)
        o["SmT"] = B.sb("SmT", [128, 4, 128], BF16, scope=sc)
        o["kw"] = B.sb("kw", [128, 4, 64], BF16, scope=sc)
        o["t1"] = B.sb("t1", [128, 4, 66], scope=sc)
        o["nd"] = B.sb("nd", [128, 4, 66], scope=sc)
        o["ho"] = B.sb("ho", [128, 4, 64], scope=sc)
        o["rdd"] = B.sb("rdd", [128, 4, 128], scope=sc)
        return o

    DB = [dirbufs(), dirbufs()]
    B.memset(acc, 0.0)
    for d in range(2):
        B.memset(DB[d]["St32"], 0.0)
        B.memset(DB[d]["Stb"], 0.0)

    rings = [make_ring(E["psA"]), make_ring(E["psB"])]

    def tile_step(d, t):
        o = DB[d]
        PS = rings[d]
        St32, Stb, ET, SmT, kw, t1, nd, ho = (o[k] for k in ("St32", "Stb", "ET", "SmT", "kw", "t1", "nd", "ho"))
        mstk, mnwk, nblk, inter, mt, rowt, wint, emt, kwf, dd = (o[k] for k in (
            "mstk", "mnwk", "nblk", "inter", "mt", "rowt", "wint", "emt", "kwf", "dd"))
        d4 = slice(d * 4, d * 4 + 4)
        mneg = CM(C_MN0 if d == 0 else C_MN1)
        tok0 = t * 128
        for cl in range(2):
            c = t * 2 + cl
            hs_ = slice(cl * 64, cl * 64 + 64)
            B.copy(mstk[hs_], mst[(c, d)][hs_])
            B.copy(mnwk[hs_], mnew[hs_, c, d4])
            B.copy(nblk[hs_], nblT[hs_, t, cl, d4])
        nb = nbT[:, t, d4]
        colt = colT[:, t, d4]
        B.tt(inter, mstk, nb, SUB)
        B.tt(mt, cmT[:, t, d4], nb, SUB)
        B.tt(mt, mt, inter, MAX)
        B.stt(rowt, nb, -1.0, mt, MUL, SUB)
        B.tt(wint, inter, mt, SUB)
        B.act(wint, wint, AF.Exp)
        B.act(emt, mt, AF.Exp, scale=-1.0)
        B.tt(kwf, colt, nblk, SUB)
        B.tt(kwf, kwf, mnwk, SUB)
        B.act(kwf, kwf, AF.Exp)
        rdd = o["rdd"]
        B.tt(rdd, rowt.unsq(2).bc([128, 4, 128]), ident.unsq(1).bc([128, 4, 128]), MUL)
        yield
        pe = PS()
        B.mm(pe[:, :512], ones, rdd)
        for hh in range(4):
            B.stt(ET[:, hh, :], pe[:, hh * 128:(hh + 1) * 128], colT[:, t, d * 4 + hh:d * 4 + hh + 1], mneg, ADD, MIN)
        yield
        B.act(ET, ET, AF.Exp)
        yield
        pkp = (PS(), PS())
        for hh in range(4):
            hb = (hh % 2) * 64
            B.mm(pkp[hh % 2][:, hh * 128:(hh + 1) * 128], KT[hb:hb + 64, hh // 2, tok0:tok0 + 128],
                 QT[hb:hb + 64, hh // 2, tok0:tok0 + 128])
        yield
        for par in range(2):
            B.tt(SmT[:, par::2, :], pkp[par][:, :512].rr("p (h j) -> p h j", j=128)[:, par::2, :], ET[:, par::2, :], MUL)
        yield
        pi = PS()
        for hh in range(4):
            B.mm(pi[:, hh * 66:hh * 66 + 65], SmT[:, hh, :], V1[:, t, hh, 0:65])
        yield
        B.tt(kw, Ktok[:, t], kwf.unsq(2).bc([128, 4, 64]), MUL)
        yield
        pinp = (PS(), PS())
        pu = PS()
        for cl in ((0, 1) if d == 0 else (1, 0)):
            c = t * 2 + cl
            cb = cl * 64
            for hh in range(4):
                hb, hp = (hh % 2) * 64, hh // 2
                B.mm(pinp[hh % 2][cb:cb + 64, hh * 66:hh * 66 + 65], QT[hb:hb + 64, hp, tok0 + cb:tok0 + cb + 64],
                     Stb[hb:hb + 64, hp, 0:65])
            yield
            for hh in range(4):
                hb, hp = (hh % 2) * 64, hh // 2
                B.mm(pu[hb:hb + 64, hp * 66:hp * 66 + 65], kw[cb:cb + 64, hh, :], V1[cb:cb + 64, t, hh, 0:65])
            yield
            for hh in range(4):
                hb, hp = (hh % 2) * 64, hh // 2
                B.stt(St32[hb:hb + 64, hp, 0:65], St32[hb:hb + 64, hp, 0:65], carry[hb:hb + 64, c, d * 4 + hh:d * 4 + hh + 1],
                      pu[hb:hb + 64, hp * 66:hp * 66 + 65], MUL, ADD)
            B.act(Stb, St32, AF.Copy)
        yield
        for par in range(2):
            B.tt(t1[:, par::2, 0:65], pinp[par][:, 0:264].rr("p (h e) -> p h e", e=66)[:, par::2, 0:65],
                 wint[:, par::2].unsq(2).bc([128, 2, 65]), MUL)
        B.tt(nd[:, :, 0:65], t1[:, :, 0:65], pi[:, 0:264].rr("p (h e) -> p h e", e=66)[:, :, 0:65], ADD)
        B.stt(dd, nd[:, :, 64], -1.0, nd[:, :, 64], MUL, MAX)
        B.tt(dd, dd, emt, MAX)
        B.recip(dd, dd)
        B.tt(ho, nd[:, :, 0:64], dd.unsq(2).bc([128, 4, 64]), MUL)
        B.tt(acc[:, t], acc[:, t], ho, ADD, eng="pool")

    order = [list(range(NT)), [1, 0] + list(range(NT - 1, 1, -1))]
    def dir_gen(d):
        for t in order[d]:
            yield from tile_step(d, t)

    lockstep([lambda slot: dir_gen(0), lambda slot: dir_gen(1)], 2)
    ho = DB[0]["ho"]
    dd = DB[0]["dd"]
    lnw = pbc[:, PB_LNW:PB_LNW + 256].rr("p (h d) -> p h d", d=64)
    hn = B.sb("hn", [128, 4, 64], scope=sc)
    for t in range(NT):
        B.tt(ho, acc[:, t], acc[:, t], MUL)
        B.red(dd, ho, ADD)
        B.act(dd, dd, AF.Sqrt, scale=1.0 / 64, bias=E["epsT"])
        B.recip(dd, dd)
        B.tt(hn, acc[:, t], dd.unsq(2).bc([128, 4, 64]), MUL)
        B.tt(hn, hn, lnw, MUL)
        B.tt(hn, hn, Og[:, t, :].rr("p (h d) -> p h d", d=64), MUL)
        for pc_ in range(2):
            pt = PB()
            B.tr(pt[:, :128], hn[:, pc_ * 2:pc_ * 2 + 2, :].rr("p h d -> p (h d)"), ident)
            B.act(mixT[:, pc_, t * 128:(t + 1) * 128], pt[:, :128], AF.Copy)
    if "dbg" in io:
        io["dbg"][(l, "mlstm")] = np.array(mixT.a, dtype=np.float32)


def mixer_gdn(E):
    B, io, l, S, PA, PB, CM = E["B"], E["io"], E["l"], E["S"], E["PA"], E["PB"], E["CM"]
    sc = S["sc"]
    MUL, ADD, SUB, MAX, MIN = ALU.mult, ALU.add, ALU.subtract, ALU.max, ALU.min
    ident, ones, pfm = E["ident"], E["ones"], E["pfm"]
    new_scope = E["new_scope"]
    pbc = S["pbc"]
    QT = B.sb("QT", [128, 2, TOK], BF16, scope=sc)
    KT = B.sb("KT", [128, 2, TOK], BF16, scope=sc)
    Ktok = B.sb("Ktok", [128, NT, 4, 64], BF16, scope=sc)
    Vtok = B.sb("Vtok", [128, NT, 4, 64], BF16, scope=sc)
    Zg = B.sb("Zg", [128, NT, 256], BF16, scope=sc)
    ngcT = B.sb("ngcT", [128, NT, 8], scope=sc)
    betaT = B.sb("betaT", [128, NT, 8], scope=sc)
    nglT = B.sb("nglT", [128, NT, 2, 8], scope=sc)
    glT = B.sb("glT", [128, NT, 2, 8], scope=sc)
    g8s = [[B.sb("g8", [128, 8], scope=sc) for _ in range(2)] for _ in range(2)]
    rings1 = [make_ring(E["psA"]), make_ring(E["psB"])]
    eal = B.sb("eal", [128, 8], scope=sc)
    B.act(eal, pbc[:, PB_ALOG:PB_ALOG + 8], AF.Exp)
    with new_scope() as sc1:
        win = B.sb("win", [128, 8, 1040], BF16, scope=sc1)
        E["load_w"](win, io["w_in"][l].rr("(k p) n -> p k n", p=128)[:, :, 0:1040])
        hh_ = [B.sb("hct", [128, 8, 512], BF16, scope=sc1) for _ in range(2)]
        raw = B.sb("raw", [128, 2312], scope=sc1)
        cacc = B.sb("cacc", [128, TOK], scope=sc1)
        sq = B.sb("sq", [128, 512], scope=sc1)
        rin = B.sb("rin", [128, 512], scope=sc1)
        B.memset(raw, 0.0)
        hc = 0
        for r in range(6):
            for ci in range(5):
                a, b = CTS[ci]
                w = b - a
                h = hh_[hc % 2]
                hc += 1
                E["loadh"](ci, h)
                pp = PA()
                for k in range(8):
                    B.mm(pp[:, :w], win[:, k, r * 128:(r + 1) * 128], h[:, k, :w], start=(k == 0), stop=(k == 7))
                off = 2 if ci == 0 else 262 + (a - 256)
                B.act(raw[:, off:off + w], pp[:, :w], AF.Copy)
            cw = lambda tap: pfm[:, l * PF_L + 64 + tap * 6 + r:l * PF_L + 64 + tap * 6 + r + 1]
            for (o0, t0, n) in ((0, 0, 256), (260, 256, 2048)):
                B.ts(cacc[:, t0:t0 + n], raw[:, o0:o0 + n], cw(0), MUL)
                for tap in range(1, 5):
                    B.stt(cacc[:, t0:t0 + n], raw[:, o0 + tap:o0 + tap + n], cw(tap), cacc[:, t0:t0 + n], MUL, ADD)
            B.act(cacc, cacc, AF.Silu)
            if r < 4:
                for ci in range(5):
                    a, b = CTS[ci]
                    w = b - a
                    B.act(sq[:, :w], cacc[:, a:b], AF.Square)
                    pq_ = PB()
                    B.mm(pq_[:, :w], CM(C_BLK), sq[:, :w])
                    B.act(rin[:, :w], pq_[:, :w], AF.Sqrt, bias=E["epsT"])
                    B.recip(rin[:, :w], rin[:, :w])
                    dst = (QT if r < 2 else KT)[:, r % 2, a:b]
                    B.stt(dst, cacc[:, a:b], 0.125 if r < 2 else 1.0, rin[:, :w], MUL, MUL)
                    if r >= 2:
                        B.tt(sq[:, :w], cacc[:, a:b], rin[:, :w], MUL)
                        for ti in range(w // 128):
                            pt = PB()
                            B.tr(pt[:, :128], sq[:, ti * 128:(ti + 1) * 128], ident)
                            B.act(Ktok[:, (a + ti * 128) // 128, (r - 2) * 2:(r - 2) * 2 + 2, :],
                                  pt[:, :128].rr("p (h d) -> p h d", d=64), AF.Copy)
            else:
                for t in range(NT):
                    pt = PB()
                    B.tr(pt[:, :128], cacc[:, t * 128:(t + 1) * 128], ident)
                    B.act(Vtok[:, t, (r - 4) * 2:(r - 4) * 2 + 2, :], pt[:, :128].rr("p (h d) -> p h d", d=64), AF.Copy)
        for ci in range(5):
            a, b = CTS[ci]
            w = b - a
            h = hh_[hc % 2]
            hc += 1
            E["loadh"](ci, h)
            def tile_gen(slot, ti, h=h, a=a):
                PS = rings1[slot]
                t0 = ti * 128
                t = (a + t0) // 128
                p2 = PS()
                for k in range(8):
                    B.mm(p2[:, :272], h[:, k, t0:t0 + 128], win[:, k, 768:1040], start=(k == 0), stop=(k == 7))
                yield
                B.act(Zg[:, t, :], p2[:, 0:256], AF.Silu)
                xa, ng = g8s[slot][0], g8s[slot][1]
                B.tt(xa, p2[:, 256:264], pbc[:, PB_DTB:PB_DTB + 8], ADD)
                yield
                B.act(xa, xa, AF.Exp)
                yield
                B.act(xa, xa, AF.Ln, bias=E["oneT"])
                yield
                B.tt(ng, xa, eal, MUL)
                B.act(betaT[:, t, :], p2[:, 264:272], AF.Sigmoid)
                yield
                pg = PS()
                B.mm(pg[:, 0:4], CM(C_CUM0), ng[:, 0:4])
                B.mm(pg[:, 4:8], CM(C_CUM1), ng[:, 4:8])
                B.mm(pg[:, 8:16], CM(C_IND0), ng)
                B.mm(pg[:, 16:24], CM(C_IND1), ng)
                yield
                B.copy(ngcT[:, t, :], pg[:, 0:8])
                B.copy(nglT[:, t], pg[:, 8:24].rr("p (c n) -> p c n", n=8))
            lockstep([(lambda slot, ti=ti: tile_gen(slot, ti)) for ti in range(w // 128)], 2)
        B.barrier()
    B.act(glT, nglT, AF.Exp, scale=-1.0)
    acc = B.sb("acc", [128, NT, 4, 64], scope=sc)
    B.memset(acc, 0.0)
    H4 = lambda p: p[:, :512].rr("p (h j) -> p h j", j=128)
    with new_scope() as sc2:
        def dirbufs():
            o = {}
            o["S32"] = B.sb("S32", [128, 2, 64], scope=sc2)
            o["Sb"] = B.sb("Sb", [128, 2, 64], BF16, scope=sc2)
            for nm in ("nglk", "rowt", "egc", "negegc", "kdf", "negb"):
                o[nm] = B.sb(nm, [128, 4], scope=sc2)
            o["ET"] = B.sb("ET", [128, 4, 128], scope=sc2)
            o["Xs"] = [B.sb("X", [128, 4, 128], scope=sc2) for _ in range(2)]
            o["XTs"] = [B.sb("XT", [128, 4, 128], scope=sc2) for _ in range(2)]
            o["Ps"] = [B.sb("P", [128, 4, 128], scope=sc2) for _ in range(2)]
            o["attnT"] = B.sb("attnT", [128, 4, 128], BF16, scope=sc2)
            o["kdec"] = B.sb("kdec", [128, 4, 64], BF16, scope=sc2)
            o["Rp"] = B.sb("Rp", [128, 4, 64], scope=sc2)
            o["vn"] = B.sb("vn", [128, 4, 64], BF16, scope=sc2)
            o["t1"] = B.sb("t1", [128, 4, 64], scope=sc2)
            o["ho"] = B.sb("ho", [128, 4, 64], scope=sc2)
            return o

        DB = [dirbufs(), dirbufs()]
        for d in range(2):
            B.memset(DB[d]["S32"], 0.0)
            B.memset(DB[d]["Sb"], 0.0)

        rings = [make_ring(E["psA"]), make_ring(E["psB"])]

        def tile_step(d, t):
            o = DB[d]
            PS = rings[d]
            S32, Sb, ET, Xs, XTs, Ps, attnT, kdec, Rp, vn, t1, ho = (o[k] for k in (
                "S32", "Sb", "ET", "Xs", "XTs", "Ps", "attnT", "kdec", "Rp", "vn", "t1", "ho"))
            nglk, rowt, egc, negegc, kdf, negb = (o[k] for k in ("nglk", "rowt", "egc", "negegc", "kdf", "negb"))
            d4 = slice(d * 4, d * 4 + 4)
            mneg = CM(C_MN0 if d == 0 else C_MN1)
            tok0 = t * 128
            ngc = ngcT[:, t, d4]
            for cl in range(2):
                hs_ = slice(cl * 64, cl * 64 + 64)
                B.copy(nglk[hs_], nglT[hs_, t, cl, d4])
            B.ts(rowt, ngc, -1.0, MUL)
            B.act(egc, ngc, AF.Exp, scale=-1.0)
            B.ts(negegc, egc, -1.0, MUL)
            B.tt(kdf, ngc, nglk, SUB)
            B.act(kdf, kdf, AF.Exp)
            B.ts(negb, betaT[:, t, d4], -1.0, MUL)
            rd = Ps[0]
            tmpA = Xs[1]
            B.tt(rd, rowt.unsq(2).bc([128, 4, 128]), ident.unsq(1).bc([128, 4, 128]), MUL)
            yield
            pe = PS()
            B.mm(pe[:, :512], ones, rd)
            for hh in range(4):
                B.stt(ET[:, hh, :], pe[:, hh * 128:(hh + 1) * 128], ngcT[:, t, d * 4 + hh:d * 4 + hh + 1], mneg, ADD, MIN)
            yield
            B.act(ET, ET, AF.Exp)
            yield
            pkk, pkq = (PS(), PS()), (PS(), PS())
            for hh in range(4):
                hb, hp = (hh % 2) * 64, hh // 2
                kt_ = KT[hb:hb + 64, hp, tok0:tok0 + 128]
                B.mm(pkk[hh % 2][:, hh * 128:(hh + 1) * 128], kt_, kt_)
                B.mm(pkq[hh % 2][:, hh * 128:(hh + 1) * 128], kt_, QT[hb:hb + 64, hp, tok0:tok0 + 128])
            yield
            for par in range(2):
                B.tt(attnT[:, par::2, :], H4(pkq[par])[:, par::2, :], ET[:, par::2, :], MUL)
                B.tt(tmpA[:, par::2, :], H4(pkk[par])[:, par::2, :], ET[:, par::2, :], MUL)
            yield
            X, XT, P = Xs[0], XTs[0], Ps[0]
            for hh in range(4):
                B.stt(X[:, hh, :], tmpA[:, hh, :], negb[:, hh:hh + 1], CM(C_OFFD), MUL, MUL)
            yield
            ptr = PS()
            for hh in range(4):
                B.tr(ptr[:, hh * 128:(hh + 1) * 128], X[:, hh, :], ident)
            yield
            B.act(XT, H4(ptr), AF.Copy)
            B.tt(P, X, ident.unsq(1).bc([128, 4, 128]), ADD)
            for lev in range(5):
                X2, XT2, P2 = Xs[(lev + 1) % 2], XTs[(lev + 1) % 2], Ps[(lev + 1) % 2]
                yield
                pXT = PS()
                for hh in range(4):
                    B.mm(pXT[:, hh * 128:(hh + 1) * 128], X[:, hh, :], XT[:, hh, :])
                yield
                B.act(XT2, H4(pXT), AF.Copy)
                if lev < 4:
                    pX = PS()
                    for hh in range(4):
                        B.mm(pX[:, hh * 128:(hh + 1) * 128], XT[:, hh, :], X[:, hh, :])
                    B.act(X2, H4(pX), AF.Copy)
                yield
                pP = PS()
                for hh in range(4):
                    B.mm(pP[:, hh * 128:(hh + 1) * 128], XT2[:, hh, :], P[:, hh, :])
                yield
                B.tt(P2, H4(pP), P, ADD)
                X, XT, P = X2, XT2, P2
            yield
            B.tt(kdec, Ktok[:, t], kdf.unsq(2).bc([128, 4, 64]), MUL)
            for cl in ((0, 1) if d == 0 else (1, 0)):
                cb = cl * 64
                cs = slice(cb, cb + 64)
                yield
                pks = (PS(), PS())
                for hh in range(4):
                    hb, hp = (hh % 2) * 64, hh // 2
                    B.mm(pks[hh % 2][cs, hh * 64:(hh + 1) * 64], KT[hb:hb + 64, hp, tok0 + cb:tok0 + cb + 64], Sb[hb:hb + 64, hp, :])
                yield
                for hh in range(4):
                    B.stt(Rp[cs, hh, :], pks[hh % 2][cs, hh * 64:(hh + 1) * 64], negegc[cs, hh:hh + 1], Vtok[cs, t, hh, :], MUL, ADD)
                yield
                pv = PS()
                for hh in range(4):
                    B.mm(pv[cs, hh * 64:(hh + 1) * 64], P[cs, hh, cb:cb + 64], Rp[cs, hh, :])
                yield
                for hh in range(4):
                    B.act(vn[cs, hh, :], pv[cs, hh * 64:(hh + 1) * 64], AF.Copy, scale=betaT[cs, t, d * 4 + hh:d * 4 + hh + 1])
                yield
                pq, pa = (PS(), PS()), PS()
                for hh in range(4):
                    hb, hp = (hh % 2) * 64, hh // 2
                    B.mm(pq[hh % 2][cs, hh * 64:(hh + 1) * 64], QT[hb:hb + 64, hp, tok0 + cb:tok0 + cb + 64], Sb[hb:hb + 64, hp, :])
                    B.mm(pa[cs, hh * 64:(hh + 1) * 64], attnT[cs, hh, cb:cb + 64], vn[cs, hh, :])
                yield
                for par in range(2):
                    B.tt(t1[cs, par::2, :], pq[par][cs, 0:256].rr("p (h e) -> p h e", e=64)[:, par::2, :],
                         egc[cs, par::2].unsq(2).bc([64, 2, 64]), MUL)
                B.tt(ho[cs], t1[cs], pa[cs, 0:256].rr("p (h e) -> p h e", e=64), ADD)
                B.tt(acc[cs, t], acc[cs, t], ho[cs], ADD, eng="pool")
                yield
                pu = PS()
                for hh in range(4):
                    hb, hp = (hh % 2) * 64, hh // 2
                    B.mm(pu[hb:hb + 64, hp * 64:(hp + 1) * 64], kdec[cs, hh, :], vn[cs, hh, :])
                yield
                for hh in range(4):
                    hb, hp = (hh % 2) * 64, hh // 2
                    B.stt(S32[hb:hb + 64, hp, :], S32[hb:hb + 64, hp, :], glT[hb:hb + 64, t, cl, d * 4 + hh:d * 4 + hh + 1],
                          pu[hb:hb + 64, hp * 64:(hp + 1) * 64], MUL, ADD)
                B.act(Sb, S32, AF.Copy)

        order = [list(range(NT)), [1, 0] + list(range(NT - 1, 1, -1))]
        def dir_gen(d):
            for t in order[d]:
                yield from tile_step(d, t)

        lockstep([lambda slot: dir_gen(0), lambda slot: dir_gen(1)], 2)
        B.barrier()
    mixT = S["mixT"] = B.sb("mixT", [128, 2, TOK], BF16, scope=sc)
    ho = B.sb("ho", [128, 4, 64], scope=sc)
    dd = B.sb("dd", [128, 4], scope=sc)
    gnw = pbc[:, PB_GNW:PB_GNW + 64].unsq(1).bc([128, 4, 64])
    hn = B.sb("hn", [128, 4, 64], scope=sc)
    for t in range(NT):
        B.tt(ho, acc[:, t], acc[:, t], MUL)
        B.red(dd, ho, ADD)
        B.act(dd, dd, AF.Sqrt, scale=1.0 / 64, bias=E["epsT"])
        B.recip(dd, dd)
        B.tt(hn, acc[:, t], dd.unsq(2).bc([128, 4, 64]), MUL)
        B.tt(hn, hn, gnw, MUL)
        B.tt(hn, hn, Zg[:, t, :].rr("p (h d) -> p h d", d=64), MUL)
        for pc_ in range(2):
            pt = PB()
            B.tr(pt[:, :128], hn[:, pc_ * 2:pc_ * 2 + 2, :].rr("p h d -> p (h d)"), ident)
            B.act(mixT[:, pc_, t * 128:(t + 1) * 128], pt[:, :128], AF.Copy)
    if "dbg" in io:
        io["dbg"][(l, "gdn")] = np.array(mixT.a, dtype=np.float32)


L_DEPTH = 4
IO_SPECS = [
    ("xT", [1024, 2048]), ("ctxT", [1024, 256]), ("cvT", [128, 8, 2]), ("pfm", [128, L_DEPTH * PF_L + 8]),
    ("pbc", [L_DEPTH, PB_L]), ("consts", [128, NCONST * 128]), ("ropec", [128, 2048]), ("ropes", [128, 2048]),
    ("gmlp_wsT", [L_DEPTH, 4, 128, 128]), ("gmlp_bsr", [L_DEPTH, 128, 2, 128]),
    ("ada_w", [L_DEPTH, 1024, 6144]), ("w_in", [L_DEPTH, 1024, 3104]), ("w_out", [L_DEPTH, 1024, 1024]),
    ("mlp_w1", [L_DEPTH, 1024, 4096]), ("mlp_w2", [L_DEPTH, 4096, 1024]),
]


def prep_shared(inp):
    f = lambda a: np.ascontiguousarray(np.asarray(a, dtype=np.float32))
    L = L_DEPTH
    pfm = np.zeros((128, L * PF_L + 8), np.float32)
    pbc = np.zeros((L, PB_L), np.float32)
    fm = lambda v: f(v).reshape(-1, 128).T
    for l in range(L):
        o = l * PF_L
        pfm[:, o:o + 8] = fm(inp["norm1_w"][l])
        pfm[:, o + 8:o + 16] = fm(inp["norm2_w"][l])
        pfm[:, o + 16:o + 64] = fm(inp["ada_b"][l])
        pfm[:, o + 64:o + 94] = f(inp["gdn_conv_w"][l]).reshape(5, 6, 128).transpose(2, 0, 1).reshape(128, 30)
        pbc[l] = np.concatenate([f(inp["gdn_a_log"][l]).ravel(), f(inp["gdn_dt_bias"][l]).ravel(),
                                 f(inp["gdn_norm_w"][l]).ravel(), f(inp["swa_sink"][l]).ravel(),
                                 f(inp["gmlp_norm_w"][l]).ravel(), f(inp["mlstm_ig_bias"][l]).ravel(),
                                 f(inp["mlstm_fg_bias"][l]).ravel(), f(inp["mlstm_norm_w"][l]).ravel()])
    pfm[:, L * PF_L:] = fm(inp["final_norm_w"])
    bs = f(inp["gmlp_b_s"])
    bsr = np.repeat(bs.reshape(L, 2, 2, 1, 128), 64, axis=3)
    bsr = bsr.transpose(0, 2, 3, 1, 4).reshape(L, 128, 2, 128)
    rc, rs = make_rope()
    return {"pfm": pfm, "pbc": pbc, "consts": make_consts(), "ropec": rc, "ropes": rs,
            "gmlp_wsT": f(np.asarray(inp["gmlp_w_s"]).transpose(0, 1, 3, 2)), "gmlp_bsr": f(bsr),
            "ada_w": f(inp["ada_w"]), "w_in": f(inp["w_in"]), "w_out": f(inp["w_out"]),
            "mlp_w1": f(inp["mlp_w1"]), "mlp_w2": f(inp["mlp_w2"])}


def prep_core(inp, b):
    f = lambda a: np.ascontiguousarray(np.asarray(a, dtype=np.float32))
    cv = np.stack([np.asarray(inp["c"])[b], np.asarray(inp["c_ctx"])], axis=-1)
    return {"xT": f(np.asarray(inp["x"])[b].T), "ctxT": f(np.asarray(inp["ctx"])[b].T),
            "cvT": f(cv.reshape(8, 128, 2).transpose(1, 0, 2))}


def build_nc(**kw):
    nc = bass.Bass("TRN2", target_bir_lowering=False)
    with ExitStack() as es:
        B = BassBackend(nc, es)
        io = {}
        for name, shape in IO_SPECS:
            io[name] = B.dram(name, shape, F32, "ExternalInput")
        io["outT"] = B.dram("outT", [1024, 2048], F32, "ExternalOutput")
        io["hd"] = B.dram("hd_scratch", [128, 8, TOK], BF16, "Internal")
        build_model(B, io, **kw)
        print("instructions:", B.n_ins, {e: B.seq[e] for e in B.seq}, flush=True)
    return nc


def kernel(**inputs):
    shared = prep_shared(inputs)
    nc = build_nc()
    in_maps = []
    for b in range(8):
        m = dict(shared)
        m.update(prep_core(inputs, b))
        in_maps.append(m)
    res = run_bass_kernel_spmd(nc, in_maps, core_ids=list(range(8)))
    out = np.stack([np.asarray(r["outT"]).T for r in res.results], axis=0)
    return np.ascontiguousarray(out.astype(np.float32))
```

```python
import math
import numpy as np
from contextlib import ExitStack
import concourse.bass as bass
import concourse.mybir as mybir
from concourse.bass_utils import run_bass_kernel_spmd

F32 = mybir.dt.float32
BF16 = mybir.dt.bfloat16
AF = mybir.ActivationFunctionType
ALU = mybir.AluOpType
AX = mybir.AxisListType

NEG = -30000.0


class Res:
    __slots__ = ("w", "rs", "excl")

    def __init__(self, excl=False):
        self.w = None
        self.rs = []
        self.excl = excl


class V:
    __slots__ = ("ap", "res")

    def __init__(self, ap, res):
        self.ap = ap
        self.res = res

    def __getitem__(self, k):
        return V(self.ap[k], self.res)

    def bc(self, shape):
        return V(self.ap.to_broadcast(list(shape)), self.res)

    def unsq(self, ax):
        return V(self.ap.unsqueeze(ax), self.res)

    def bitcast(self, dt):
        return V(self.ap.bitcast(dt), self.res)

    def rr(self, pat, **kw):
        return V(self.ap.rearrange(pat, **kw), self.res)

    def pbc(self, n):
        return V(self.ap.partition_broadcast(n), self.res)

    @property
    def shape(self):
        return tuple(self.ap.shape)


class BassBackend:
    LIM = 20000
    NSLOT = 12

    def __init__(self, nc, es):
        self.nc = nc
        self.es = es
        self.eng = {"pe": nc.tensor, "act": nc.scalar, "dve": nc.vector, "pool": nc.gpsimd, "sp": nc.sync}
        names = list(self.eng)
        self.sems = {e: [] for e in names}
        self.seq = {e: 0 for e in names}
        self.seen = {e: {f: 0 for f in names} for e in names}
        self.hist = {e: [] for e in names}
        self.dq = {}
        for q in ("sp", "pool", "act"):
            self.dq[q] = {"sems": [es.enter_context(nc.semaphore("d%s%d" % (q, i))) for i in range(self.NSLOT)],
                          "cnt": [0] * self.NSLOT, "next": 0}
        self.dseen = {e: {} for e in names}
        self.n_ins = 0
        self._uid = 0

    def sb(self, name, shape, dtype=F32, scope=None):
        self._uid += 1
        t = (scope or self.es).enter_context(self.nc.sbuf_tensor("%s_%d" % (name, self._uid), list(shape), dtype))
        return V(t[:] if len(shape) == 2 else t[tuple([slice(None)] * len(shape))], (Res(),))

    def ps(self, name, shape, dtype=F32):
        t = self.es.enter_context(self.nc.psum_tensor(name, list(shape), dtype))
        return V(t[tuple([slice(None)] * len(shape))], (Res(excl=True),))

    def dram(self, name, shape, dtype, kind):
        t = self.nc.dram_tensor(name, list(shape), dtype, kind=kind)
        return V(t.ap(), ())

    def _sem(self, e, sq):
        i = (sq - 1) // self.LIM
        while len(self.sems[e]) <= i:
            self.sems[e].append(self.es.enter_context(self.nc.semaphore("s%s%d" % (e, len(self.sems[e])))))
        return self.sems[e][i], (sq - 1) % self.LIM + 1

    def _merge(self, e, snap):
        se = self.seen[e]
        for f, v in snap[0].items():
            if v > se[f]:
                se[f] = v
        de = self.dseen[e]
        for k, v in snap[1].items():
            if v > de.get(k, 0):
                de[k] = v

    def _wait_tok(self, e, tok, skip_same):
        if tok[0] == "e":
            _, f, sq = tok
            if f == e and skip_same:
                return
            if self.seen[e][f] >= sq:
                return
            sem, val = self._sem(f, sq)
            self.eng[e].wait_ge(sem, val)
            self.seen[e][f] = sq
            self._merge(e, self.hist[f][sq - 1])
        else:
            _, q, slot, cnt, snap = tok
            if self.dseen[e].get((q, slot), 0) >= cnt:
                return
            self.eng[e].wait_ge(self.dq[q]["sems"][slot], cnt)
            self.dseen[e][(q, slot)] = cnt
            self._merge(e, snap)

    def _sync(self, e, reads, writes, skip_same=False):
        for r in reads:
            for res in r.res:
                if res.w is not None:
                    self._wait_tok(e, res.w, skip_same)
                if res.excl:
                    for t in res.rs:
                        if t[1] != e:
                            self._wait_tok(e, t, skip_same)
        for w in writes:
            for res in w.res:
                if res.w is not None:
                    self._wait_tok(e, res.w, skip_same)
                for t in res.rs:
                    self._wait_tok(e, t, skip_same)

    def _snap(self, e):
        return (dict(self.seen[e]), dict(self.dseen[e]))

    def _mark(self, tok, e, reads, writes):
        for r in reads:
            for res in r.res:
                res.rs = [t for t in res.rs if not (t[0] == "e" and t[1] == e)] + [tok]
        for w in writes:
            for res in w.res:
                res.w = tok
                res.rs = []

    def _commit(self, e, ins, reads, writes):
        self.seq[e] += 1
        sq = self.seq[e]
        sem, val = self._sem(e, sq)
        ins.then_inc(sem, 1)
        self.hist[e].append(self._snap(e))
        self._mark(("e", e, sq), e, reads, writes)
        self.n_ins += 1

    def dma(self, out, in_, q="sp"):
        e = q
        self._sync(e, [in_], [out])
        d = self.dq[q]
        slot = d["next"]
        d["next"] = (slot + 1) % self.NSLOT
        if d["cnt"][slot] > self.dseen[e].get((q, slot), 0):
            self.eng[e].wait_ge(d["sems"][slot], d["cnt"][slot])
            self.dseen[e][(q, slot)] = d["cnt"][slot]
        ins = self.eng[e].dma_start(out=out.ap, in_=in_.ap)
        d["cnt"][slot] += 16
        ins.then_inc(d["sems"][slot], 16)
        tok = ("d", q, slot, d["cnt"][slot], self._snap(e))
        self._mark(tok, e, [in_], [out])
        self.n_ins += 1

    def mm(self, out, lhsT, rhs, start=True, stop=True):
        self._sync("pe", [lhsT, rhs], [out], skip_same=True)
        ins = self.nc.tensor.matmul(out.ap, lhsT=lhsT.ap, rhs=rhs.ap, start=start, stop=stop)
        self._commit("pe", ins, [lhsT, rhs], [out])

    def tr(self, out, in_, ident):
        self._sync("pe", [in_, ident], [out], skip_same=True)
        ins = self.nc.tensor.transpose(out.ap, in_.ap, ident.ap)
        self._commit("pe", ins, [in_, ident], [out])

    def act(self, out, in_, func, bias=None, scale=1.0, accum=None):
        rd = [in_] + [x for x in (bias, scale) if isinstance(x, V)]
        wr = [out] + ([accum] if accum is not None else [])
        self._sync("act", rd, wr)
        kw = {}
        if bias is not None:
            kw["bias"] = bias.ap if isinstance(bias, V) else float(bias)
        if accum is not None:
            kw["accum_out"] = accum.ap
        ins = self.nc.scalar.activation(out=out.ap, in_=in_.ap, func=func,
                                        scale=(scale.ap if isinstance(scale, V) else float(scale)), **kw)
        self._commit("act", ins, rd, wr)

    def tt(self, out, a, b, op, eng="dve"):
        self._sync(eng, [a, b], [out])
        ins = self.eng[eng].tensor_tensor(out=out.ap, in0=a.ap, in1=b.ap, op=op)
        self._commit(eng, ins, [a, b], [out])

    def ts(self, out, a, s1, op0, s2=None, op1=None, eng="dve"):
        rd = [a] + [x for x in (s1, s2) if isinstance(x, V)]
        self._sync(eng, rd, [out])
        f = lambda x: x.ap if isinstance(x, V) else (None if x is None else float(x))
        if op1 is None:
            ins = self.eng[eng].tensor_scalar(out=out.ap, in0=a.ap, scalar1=f(s1), scalar2=None, op0=op0)
        else:
            ins = self.eng[eng].tensor_scalar(out=out.ap, in0=a.ap, scalar1=f(s1), scalar2=f(s2), op0=op0, op1=op1)
        self._commit(eng, ins, rd, [out])

    def stt(self, out, a, s, b, op0, op1, eng="dve"):
        rd = [a, b] + ([s] if isinstance(s, V) else [])
        self._sync(eng, rd, [out])
        ins = self.eng[eng].scalar_tensor_tensor(out=out.ap, in0=a.ap, scalar=(s.ap if isinstance(s, V) else float(s)),
                                                 in1=b.ap, op0=op0, op1=op1)
        self._commit(eng, ins, rd, [out])

    def red(self, out, in_, op, eng="dve"):
        self._sync(eng, [in_], [out])
        ins = self.eng[eng].tensor_reduce(out=out.ap, in_=in_.ap, axis=AX.X, op=op)
        self._commit(eng, ins, [in_], [out])

    def copy(self, out, in_, eng="dve"):
        if eng == "act":
            return self.act(out, in_, AF.Copy)
        self._sync(eng, [in_], [out])
        ins = self.eng[eng].tensor_copy(out=out.ap, in_=in_.ap)
        self._commit(eng, ins, [in_], [out])

    def memset(self, out, val, eng="dve"):
        self._sync(eng, [], [out])
        ins = self.eng[eng].memset(out.ap, float(val))
        self._commit(eng, ins, [], [out])

    def recip(self, out, in_):
        self._sync("dve", [in_], [out])
        ins = self.nc.vector.reciprocal(out=out.ap, in_=in_.ap)
        self._commit("dve", ins, [in_], [out])

    def finish(self):
        for q, d in self.dq.items():
            for slot in range(self.NSLOT):
                if d["cnt"][slot] > self.dseen["sp"].get((q, slot), 0):
                    self.nc.sync.wait_ge(d["sems"][slot], d["cnt"][slot])


    def barrier(self):
        names = list(self.eng)
        toks = [("e", f, self.seq[f]) for f in names if self.seq[f] > 0]
        for q, d in self.dq.items():
            for slot in range(self.NSLOT):
                if d["cnt"][slot] > 0:
                    toks.append(("d", q, slot, d["cnt"][slot], ({}, {})))
        for e in ("pe", "act", "dve", "pool", "sp"):
            for t in toks:
                self._wait_tok(e, t, False)


def make_ring(tiles):
    ctr = [0]

    def nxt():
        ctr[0] += 1
        return tiles[ctr[0] % len(tiles)]
    return nxt


def lockstep(factories, width):
    it = iter(factories)
    active = []
    free = list(range(width))
    exhausted = False
    while True:
        while free and not exhausted:
            f = next(it, None)
            if f is None:
                exhausted = True
                break
            slot = free.pop(0)
            active.append((slot, f(slot)))
        if not active:
            break
        for item in list(active):
            try:
                next(item[1])
            except StopIteration:
                active.remove(item)
                free.append(item[0])


D = 1024
NT = 18
TOK = 2304
NCTX = 256
CTS = [(0, 256), (256, 768), (768, 1280), (1280, 1792), (1792, 2304)]
C_ID, C_ONES, C_BLK, C_CUM0, C_CUM1, C_IND0, C_IND1, C_S63, C_S127, C_S0, C_S64, C_MN0, C_MN1, C_OFFD, C_RMT, C_BAND = range(16)
NCONST = 18
PF_L = 94
PB_ALOG, PB_DTB, PB_GNW, PB_SINK, PB_MNW, PB_IGB, PB_FGB, PB_LNW = 0, 8, 16, 80, 84, 340, 348, 356
PB_L = 612


def make_consts():
    c = np.zeros((NCONST, 128, 128), np.float32)
    k = np.arange(128)[:, None]
    i = np.arange(128)[None, :]
    same = (k // 64) == (i // 64)
    c[C_ID] = (k == i)
    c[C_ONES] = 1.0
    c[C_BLK] = same
    c[C_CUM0] = same & (k <= i)
    c[C_CUM1] = same & (k >= i)
    c[C_IND0] = (k < 64) & (i >= 0)
    c[C_IND1] = (k >= 64) & (i >= 0)
    c[C_S63] = (k == 63) & (i >= 0)
    c[C_S127] = (k == 127) & (i >= 0)
    c[C_S0] = (k == 0) & (i >= 0)
    c[C_S64] = (k == 64) & (i >= 0)
    c[C_MN0] = np.where(same & (k <= i), 0.0, NEG)
    c[C_MN1] = np.where(same & (k >= i), 0.0, NEG)
    c[C_OFFD] = (k != i)
    rmt = np.zeros((128, 128), np.float32)
    for hb in (0, 64):
        for d in range(32):
            rmt[hb + d + 32, hb + d] = -1.0
            rmt[hb + d, hb + d + 32] = 1.0
    c[C_RMT] = rmt
    qi = np.arange(128)[:, None]
    kj = np.arange(384)[None, :]
    band = np.where(np.abs(kj - 128 - qi) <= 128, 0.0, NEG).astype(np.float32)
    for t in range(3):
        c[C_BAND + t] = band[:, t * 128:(t + 1) * 128]
    return np.ascontiguousarray(c.transpose(1, 0, 2).reshape(128, NCONST * 128))


def make_rope():
    rows = 2048 // 64
    row = np.repeat(np.arange(rows), 64).astype(np.float32)
    col = (np.arange(2048) % 64).astype(np.float32)
    inv = np.power(np.float32(10000.0), -np.arange(16, dtype=np.float32) / 16).astype(np.float32)
    ang = np.concatenate([row[:, None] * inv, col[:, None] * inv], axis=-1).astype(np.float32)
    cos = np.cos(ang).astype(np.float32).T
    sin = np.sin(ang).astype(np.float32).T
    cc = np.concatenate([cos, cos, cos, cos], axis=0)
    ss = np.concatenate([sin, sin, sin, sin], axis=0)
    return np.ascontiguousarray(cc), np.ascontiguousarray(ss)


def build_model(B, io, L=4, mixers=("gmlp", "swa", "mlstm", "gdn"), do_mlp=True, new_scope=ExitStack):
    MUL, ADD, SUB, MAX, MIN = ALU.mult, ALU.add, ALU.subtract, ALU.max, ALU.min
    cst = B.sb("cst", [128, NCONST, 128])
    B.dma(cst, io["consts"].rr("p (n c) -> p n c", c=128))
    CM = lambda i: cst[:, i, :]
    ident, ones = CM(C_ID), CM(C_ONES)
    identb = B.sb("identb", [128, 128], BF16)
    B.copy(identb, ident)
    xT = B.sb("xT", [128, 8, TOK])
    xres = [[Res() for _ in CTS] for _ in range(8)]

    def X(c, ci):
        a, b = CTS[ci]
        v = xT[:, c, a:b]
        v.res = (xres[c][ci],)
        return v

    for c in range(8):
        for ci in range(5):
            a, b = CTS[ci]
            if ci == 0:
                B.dma(X(c, 0), io["ctxT"][c * 128:(c + 1) * 128, :])
            else:
                B.dma(X(c, ci), io["xT"][c * 128:(c + 1) * 128, a - 256:b - 256])
    pfm = B.sb("pfm", [128, L_DEPTH * PF_L + 8])
    B.dma(pfm, io["pfm"])
    epsT = B.sb("epsT", [128, 1])
    B.memset(epsT, 1e-6)
    oneT = B.sb("oneT", [128, 1])
    B.memset(oneT, 1.0)
    psA = [B.ps("psA%d" % i, [128, 512]) for i in range(4)]
    psB = [B.ps("psB%d" % i, [128, 512]) for i in range(4)]
    pctr = [0, 0]

    def PA():
        pctr[0] += 1
        return psA[pctr[0] % 4]

    def PB():
        pctr[1] += 1
        return psB[pctr[1] % 4]

    mod = B.sb("mod", [128, L, 48, 2])
    ns1 = B.sb("ns1", [128, L, 8, 2])
    ns2 = B.sb("ns2", [128, L, 8, 2])
    with new_scope() as sc0:
        scv = B.sb("scv", [128, 8, 2], scope=sc0)
        B.dma(scv, io["cvT"])
        B.act(scv, scv, AF.Silu)
        awb = [B.sb("awb", [128, 8, 768], scope=sc0) for _ in range(2)]
        for l in range(L):
            pm = PA()
            for s in range(8):
                aw = awb[s % 2]
                B.dma(aw, io["ada_w"][l].rr("(k p) n -> p k n", p=128)[:, :, s * 768:(s + 1) * 768])
                for nn in range(6):
                    n = s * 6 + nn
                    for k in range(8):
                        B.mm(pm[:, n * 2:n * 2 + 2], aw[:, k, nn * 128:(nn + 1) * 128], scv[:, k, :],
                             start=(k == 0), stop=(k == 7))
            adab = pfm[:, l * PF_L + 16:l * PF_L + 64]
            B.tt(mod[:, l], pm[:, 0:96].rr("p (a b) -> p a b", b=2), adab.unsq(2).bc([128, 48, 2]), ADD)
            for (ns, so, no) in ((ns1, 8, 0), (ns2, 32, 8)):
                B.ts(ns[:, l], mod[:, l, so:so + 8, :], 1.0, ADD)
                B.tt(ns[:, l], ns[:, l], pfm[:, l * PF_L + no:l * PF_L + no + 8].unsq(2).bc([128, 8, 2]), MUL)
        B.barrier()

    def normmod(ci, ns, sh, hout, S):
        a, b = CTS[ci]
        w = b - a
        s = 1 if ci == 0 else 0
        pss = PA()
        for c in range(8):
            sq = S["sq"][c % 2]
            B.act(sq[:, :w], X(c, ci), AF.Square)
            B.mm(pss[:, :w], ones, sq[:, :w], start=(c == 0), stop=(c == 7))
        rstd = S["rstd"]
        B.act(rstd[:, :w], pss[:, :w], AF.Sqrt, scale=1.0 / D, bias=epsT)
        B.recip(rstd[:, :w], rstd[:, :w])
        for c in range(8):
            tmp = S["tmp"][c % 2]
            B.tt(tmp[:, :w], X(c, ci), rstd[:, :w], MUL)
            B.act(hout[:, c, :w], tmp[:, :w], AF.Identity, scale=ns[:, c, s:s + 1], bias=sh[:, c, s:s + 1])

    def norm_scratch(scope):
        return {"sq": [B.sb("sq", [128, 512], scope=scope) for _ in range(2)],
                "tmp": [B.sb("tmpn", [128, 512], scope=scope) for _ in range(2)],
                "rstd": B.sb("rstd", [128, 512], scope=scope)}

    def load_w(dst, src3, q="pool"):
        for k in range(8):
            B.dma(dst[:, k, :], src3[:, k, :], q=q)

    def residual(l, mixT, wo, kc, ci_list, gate_off):
        for ci in ci_list:
            a, b = CTS[ci]
            w = b - a
            s = 1 if ci == 0 else 0
            for n in range(8):
                pp = PB()
                for k in range(kc):
                    B.mm(pp[:, :w], wo[:, k, n * 128:(n + 1) * 128], mixT[:, k, a:b], start=(k == 0), stop=(k == kc - 1))
                B.stt(X(n, ci), pp[:, :w], mod[:, l, gate_off + n, s:s + 1], X(n, ci), MUL, ADD)

    ctx_needed = lambda l: l < L - 1
    hres = [Res() for _ in CTS]

    def HD(ci):
        a, b = CTS[ci]
        v = io["hd"][:, :, a:b]
        v.res = (hres[ci],)
        return v

    for l in range(L):
        cis = [0, 1, 2, 3, 4] if ctx_needed(l) else [1, 2, 3, 4]
        sh1 = mod[:, l, 0:8, :]
        sh2 = mod[:, l, 24:32, :]
        with new_scope() as sc:
            S = norm_scratch(sc)
            hb = [B.sb("hb", [128, 8, 512], BF16, scope=sc) for _ in range(2)]
            for ci in range(5):
                a, b = CTS[ci]
                normmod(ci, ns1[:, l], sh1, hb[ci % 2], S)
                B.dma(HD(ci), hb[ci % 2][:, :, :b - a])
            B.barrier()

        def loadh(ci, h):
            a, b = CTS[ci]
            B.dma(h[:, :, :b - a], HD(ci))

        for mi, mx in enumerate(("gmlp", "swa", "mlstm", "gdn")):
            if mx not in mixers:
                continue
            with new_scope() as sc:
                S = {}
                S["sc"] = sc
                S["cst"] = cst
                if mx in ("gmlp", "swa"):
                    S["h"] = [B.sb("hct", [128, 8, 512], BF16, scope=sc) for _ in range(2)]
                S["wout"] = B.sb("wout", [128, 2, 1024], BF16, scope=sc)
                S["pbc"] = B.sb("pbc", [128, PB_L], scope=sc)
                B.dma(S["pbc"], io["pbc"][l].pbc(128))
                grp = {"gdn": 0, "swa": 1, "gmlp": 2, "mlstm": 3}[mx]
                for k in range(2):
                    B.dma(S["wout"][:, k, :], io["w_out"][l, grp * 256 + k * 128:grp * 256 + (k + 1) * 128, :], q="pool")
                env = dict(B=B, io=io, l=l, S=S, CM=CM, PA=PA, PB=PB, normmod=normmod, ns1=ns1[:, l], sh1=sh1,
                           load_w=load_w, loadh=loadh, new_scope=new_scope, psA=psA, psB=psB, ident=ident, identb=identb, ones=ones, pfm=pfm, epsT=epsT, oneT=oneT)
                {"gmlp": mixer_gmlp, "swa": mixer_swa, "mlstm": mixer_mlstm, "gdn": mixer_gdn}[mx](env)
                residual(l, S["mixT"], S["wout"], 2, cis, 16)
                B.barrier()
        if do_mlp:
            with new_scope() as sc:
                hT = B.sb("hT", [128, 8, TOK], BF16, scope=sc)
                with new_scope() as scn:
                    S = norm_scratch(scn)
                    for ci in cis:
                        a, b = CTS[ci]
                        normmod(ci, ns2[:, l], sh2, hT[:, :, a:b], S)
                    B.barrier()
                w1s = [B.sb("w1", [128, 8, 1024], BF16, scope=sc) for _ in range(2)]
                w2s = [B.sb("w2", [128, 8, 1024], BF16, scope=sc) for _ in range(2)]
                hid = [B.sb("hid", [128, 8, 512], BF16, scope=sc) for _ in range(2)]
                rl = [B.sb("rl", [128, 512], scope=sc) for _ in range(2)]

                def ldq(qq):
                    load_w(w1s[qq % 2], io["mlp_w1"][l].rr("(k p) n -> p k n", p=128)[:, :, qq * 1024:(qq + 1) * 1024])
                    load_w(w2s[qq % 2], io["mlp_w2"][l, qq * 1024:(qq + 1) * 1024, :].rr("(k p) n -> p k n", p=128))

                ldq(0)
                for qq in range(4):
                    if qq < 3:
                        ldq(qq + 1)
                    w1, w2 = w1s[qq % 2], w2s[qq % 2]
                    for ci in cis:
                        a, b = CTS[ci]
                        w = b - a
                        s = 1 if ci == 0 else 0
                        hd = hid[ci % 2]
                        for f in range(8):
                            pp = PA()
                            for k in range(8):
                                B.mm(pp[:, :w], w1[:, k, f * 128:(f + 1) * 128], hT[:, k, a:b], start=(k == 0), stop=(k == 7))
                            r = rl[f % 2]
                            B.act(r[:, :w], pp[:, :w], AF.Relu)
                            B.tt(hd[:, f, :w], r[:, :w], r[:, :w], MUL, eng="pool")
                        for n in range(8):
                            pp = PB()
                            for f in range(8):
                                B.mm(pp[:, :w], w2[:, f, n * 128:(n + 1) * 128], hd[:, f, :w], start=(f == 0), stop=(f == 7))
                            B.stt(X(n, ci), pp[:, :w], mod[:, l, 40 + n, s:s + 1], X(n, ci), MUL, ADD)
                B.barrier()

    with new_scope() as sc:
        S = norm_scratch(sc)
        fo = [B.sb("fo", [128, 512], scope=sc) for _ in range(2)]
        fw = pfm[:, L_DEPTH * PF_L:L_DEPTH * PF_L + 8]
        for ci in range(1, 5):
            a, b = CTS[ci]
            w = b - a
            pss = PA()
            for c in range(8):
                sq = S["sq"][c % 2]
                B.act(sq[:, :w], X(c, ci), AF.Square)
                B.mm(pss[:, :w], ones, sq[:, :w], start=(c == 0), stop=(c == 7))
            rstd = S["rstd"]
            B.act(rstd[:, :w], pss[:, :w], AF.Sqrt, scale=1.0 / D, bias=epsT)
            B.recip(rstd[:, :w], rstd[:, :w])
            for c in range(8):
                o = fo[c % 2]
                B.stt(o[:, :w], X(c, ci), fw[:, c:c + 1], rstd[:, :w], MUL, MUL)
                B.dma(io["outT"][c * 128:(c + 1) * 128, a - 256:b - 256], o[:, :w])
    B.finish()


def mixer_gmlp(E):
    B, io, l, S, PA, PB = E["B"], E["io"], E["l"], E["S"], E["PA"], E["PB"]
    sc = S["sc"]
    MUL, ADD = ALU.mult, ALU.add
    win = B.sb("win", [128, 8, 512], BF16, scope=sc)
    E["load_w"](win, io["w_in"][l].rr("(k p) n -> p k n", p=128)[:, :, 1552:2064])
    wsT = B.sb("wsT", [128, 4, 128], BF16, scope=sc)
    B.dma(wsT, io["gmlp_wsT"][l].rr("g q p -> q g p"), q="pool")
    bs = B.sb("bs", [128, 2, 128], scope=sc)
    B.dma(bs, io["gmlp_bsr"][l])
    nw = S["pbc"][:, PB_MNW:PB_MNW + 256]
    uT = B.sb("uT", [128, 2, 512], scope=sc)
    vg = B.sb("vg", [128, 256], scope=sc)
    junk = B.sb("junk", [128, 256], scope=sc)
    vn = B.sb("vn", [128, 256], BF16, scope=sc)
    ssq = B.sb("ssq", [128, 1], scope=sc)
    tmpm = B.sb("tmpm", [128, 128], scope=sc)
    mixT = S["mixT"] = B.sb("mixT", [128, 2, TOK], BF16, scope=sc)
    E["loadh"](0, S["h"][0])
    for ci in range(5):
        a, b = CTS[ci]
        w = b - a
        h = S["h"][ci % 2]
        if ci < 4:
            E["loadh"](ci + 1, S["h"][(ci + 1) % 2])
        for c in range(2):
            pp = PA()
            for k in range(8):
                B.mm(pp[:, :w], win[:, k, c * 128:(c + 1) * 128], h[:, k, :w], start=(k == 0), stop=(k == 7))
            B.act(uT[:, c, :w], pp[:, :w], AF.Gelu_apprx_tanh)
        for ti in range(w // 128):
            t0 = ti * 128
            pv = PA()
            for k in range(8):
                B.mm(pv[:, :256], h[:, k, t0:t0 + 128], win[:, k, 256:512], start=(k == 0), stop=(k == 7))
            B.act(vg, pv[:, :256], AF.Gelu_apprx_tanh)
            B.memset(ssq, 0.0)
            B.act(junk, vg, AF.Square, accum=ssq)
            B.act(ssq, ssq, AF.Sqrt, scale=1.0 / 256, bias=E["epsT"])
            B.recip(ssq, ssq)
            B.stt(vn, vg, ssq, nw, MUL, MUL)
            for pc in range(2):
                pm_ = PB()
                for gi in range(2):
                    g = pc * 2 + gi
                    B.mm(pm_[gi * 64:(gi + 1) * 64, :128], vn[:, g * 64:(g + 1) * 64], wsT[:, g, :])
                B.tt(tmpm, pm_[:, :128], bs[:, pc, :], ADD)
                B.tt(mixT[:, pc, a + t0:a + t0 + 128], tmpm, uT[:, pc, t0:t0 + 128], MUL, eng="pool")
    if "dbg" in io:
        io["dbg"][(l, "gmlp")] = np.array(mixT.a, dtype=np.float32)


def mixer_swa(E):
    B, io, l, S, PA, PB, CM = E["B"], E["io"], E["l"], E["S"], E["PA"], E["PB"], E["CM"]
    sc = S["sc"]
    MUL, ADD, MAX = ALU.mult, ALU.add, ALU.max
    ident = E["ident"]
    win = B.sb("win", [128, 8, 512], BF16, scope=sc)
    E["load_w"](win, io["w_in"][l].rr("(k p) n -> p k n", p=128)[:, :, 1040:1552])
    qT = B.sb("qT", [128, 2, TOK], BF16, scope=sc)
    kT = B.sb("kT", [128, TOK], BF16, scope=sc)
    vtok = B.sb("vtok", [128, NT, 128], BF16, scope=sc)
    rc_l = [B.sb("rc", [128, 512], scope=sc) for _ in range(2)]
    rs_l = [B.sb("rs", [128, 512], scope=sc) for _ in range(2)]
    raw_l = [B.sb("raw", [128, 512], scope=sc) for _ in range(2)]
    t1_l = [B.sb("t1", [128, 512], scope=sc) for _ in range(2)]
    t2_l = [B.sb("t2", [128, 512], scope=sc) for _ in range(2)]
    fctr = 0
    sink = S["pbc"][:, PB_SINK:PB_SINK + 4]
    mixT = S["mixT"] = B.sb("mixT", [128, 2, TOK], BF16, scope=sc)
    E["loadh"](0, S["h"][0])
    for ci in range(5):
        a, b = CTS[ci]
        w = b - a
        h = S["h"][ci % 2]
        if ci < 4:
            E["loadh"](ci + 1, S["h"][(ci + 1) % 2])
        rc, rs_ = rc_l[ci % 2], rs_l[ci % 2]
        if ci > 0:
            B.dma(rc, io["ropec"][:, a - 256:b - 256])
            B.dma(rs_, io["ropes"][:, a - 256:b - 256])
        for r in range(3):
            raw, t1, t2 = raw_l[fctr % 2], t1_l[fctr % 2], t2_l[fctr % 2]
            fctr += 1
            pp = PA()
            if r < 2:
                for j, hq in enumerate((r, r + 2)):
                    for k in range(8):
                        B.mm(pp[j * 64:(j + 1) * 64, :w], win[:, k, hq * 64:(hq + 1) * 64], h[:, k, :w],
                             start=(k == 0), stop=(k == 7))
            else:
                for k in range(8):
                    B.mm(pp[:, :w], win[:, k, 256:384], h[:, k, :w], start=(k == 0), stop=(k == 7))
            dst = qT[:, r, a:b] if r < 2 else kT[:, a:b]
            if ci == 0:
                B.act(dst, pp[:, :w], AF.Copy)
            else:
                B.act(raw[:, :w], pp[:, :w], AF.Copy)
                pr = PB()
                B.mm(pr[:, :w], CM(C_RMT), raw[:, :w])
                B.tt(t1[:, :w], raw[:, :w], rc[:, :w], MUL)
                B.tt(t2[:, :w], pr[:, :w], rs_[:, :w], MUL)
                B.tt(dst, t1[:, :w], t2[:, :w], ADD, eng="pool")
        for ti in range(w // 128):
            t0 = ti * 128
            pv = PA()
            for k in range(8):
                B.mm(pv[:, :128], h[:, k, t0:t0 + 128], win[:, k, 384:512], start=(k == 0), stop=(k == 7))
            B.act(vtok[:, (a + t0) // 128, :], pv[:, :128], AF.Copy)
    if "dbg" in io:
        io["dbg"]["qT"] = np.array(qT.a, dtype=np.float32)
        io["dbg"]["kT"] = np.array(kT.a, dtype=np.float32)
    bandm = E["B"].sb("bandm", [128, 384], scope=sc)
    B.copy(bandm.rr("p (a b) -> p a b", b=128), E["S"]["cst"][:, C_BAND:C_BAND + 3, :], eng="pool")
    NBUF = 4
    s_l = [B.sb("s", [128, 640], scope=sc) for _ in range(NBUF)]
    p_l = [B.sb("p", [128, 640], scope=sc) for _ in range(NBUF)]
    pT_l = [B.sb("pT", [128, 640], BF16, scope=sc) for _ in range(NBUF)]
    st_l = [B.sb("st", [128, 8], scope=sc) for _ in range(NBUF)]
    rings = [make_ring(E["psA"][0:2]), make_ring(E["psA"][2:4]), make_ring(E["psB"][0:2]), make_ring(E["psB"][2:4])]

    def unit_gen(slot, qb, hq):
        PS = rings[slot]
        s, p, pT, st = s_l[slot], p_l[slot], pT_l[slot], st_l[slot]
        mxv, negm, rsum, es, den = (st[:, i:i + 1] for i in range(5))
        q0 = qb * 128
        if qb >= 2:
            n = qb - 2
            lo, hi = max(n - 1, 0), min(n + 1, 15)
            kl0, kl1 = 256 + lo * 128, 256 + (hi + 1) * 128
            wl = kl1 - kl0
            moff = (lo - (n - 1)) * 128
        else:
            wl = 0
        W = wl + 256
        nb = W // 128
        hk = hq // 2
        r = hq % 2
        base = hk * 64
        qv = qT[base:base + 64, r, q0:q0 + 128]
        if wl:
            pa = PS()
            B.mm(pa[:, :wl], qv, kT[base:base + 64, kl0:kl1])
        pb = PS()
        B.mm(pb[:, :256], qv, kT[base:base + 64, 0:256])
        yield
        if wl:
            B.stt(s[:, :wl], pa[:, :wl], 0.125, bandm[:, moff:moff + wl], MUL, ADD)
        B.act(s[:, wl:W], pb[:, :256], AF.Copy, scale=0.125)
        yield
        B.red(mxv, s[:, :W], MAX)
        yield
        B.ts(negm, mxv, sink[:, hq:hq + 1], MAX, -1.0, MUL)
        B.memset(rsum, 0.0)
        yield
        B.act(p[:, :W], s[:, :W], AF.Exp, bias=negm, accum=rsum)
        B.act(es, sink[:, hq:hq + 1], AF.Exp, bias=negm)
        yield
        B.tt(den, rsum, es, ADD)
        yield
        B.recip(den, den)
        yield
        B.ts(p[:, :W], p[:, :W], den, MUL)
        yield
        pt1, pt2 = PS(), PS()
        for j_ in range(nb):
            B.tr((pt1 if j_ < 4 else pt2)[:, (j_ % 4) * 128:(j_ % 4 + 1) * 128], p[:, j_ * 128:(j_ + 1) * 128], ident)
        yield
        B.act(pT[:, 0:min(nb, 4) * 128], pt1[:, 0:min(nb, 4) * 128], AF.Copy)
        if nb > 4:
            B.copy(pT[:, 512:640], pt2[:, 0:128])
        yield
        po = PS()
        ob = (hq % 2) * 64
        for j_ in range(nb):
            kt = (kl0 // 128 + j_) if j_ < wl // 128 else (j_ - wl // 128)
            B.mm(po[ob:ob + 64, :128], vtok[:, kt, hk * 64:(hk + 1) * 64], pT[:, j_ * 128:(j_ + 1) * 128],
                 start=(j_ == 0), stop=(j_ == nb - 1))
        yield
        B.act(mixT[ob:ob + 64, hq // 2, q0:q0 + 128], po[ob:ob + 64, :128], AF.Copy)

    lockstep([(lambda slot, qb=qb, hq=hq: unit_gen(slot, qb, hq)) for qb in range(NT) for hq in range(4)], NBUF)
    if "dbg" in io:
        io["dbg"][(l, "swa")] = np.array(mixT.a, dtype=np.float32)


def mixer_mlstm(E):
    B, io, l, S, PA, PB, CM = E["B"], E["io"], E["l"], E["S"], E["PA"], E["PB"], E["CM"]
    sc = S["sc"]
    MUL, ADD, SUB, MAX, MIN = ALU.mult, ALU.add, ALU.subtract, ALU.max, ALU.min
    ident, ones = E["ident"], E["ones"]
    new_scope = E["new_scope"]
    mixT = S["mixT"] = B.sb("mixT", [128, 2, TOK], BF16, scope=sc)
    pbc = S["pbc"]
    QT = B.sb("QT", [128, 2, TOK], BF16, scope=sc)
    KT = B.sb("KT", [128, 2, TOK], BF16, scope=sc)
    Ktok = B.sb("Ktok", [128, NT, 4, 64], BF16, scope=sc)
    V1 = B.sb("V1", [128, NT, 4, 66], BF16, scope=sc)
    Og = B.sb("Og", [128, NT, 256], BF16, scope=sc)
    nbT = B.sb("nbT", [128, NT, 8], scope=sc)
    colT = B.sb("colT", [128, NT, 8], scope=sc)
    cmT = B.sb("cmT", [128, NT, 8], scope=sc)
    nblT = B.sb("nblT", [128, NT, 2, 8], scope=sc)
    cmlT = B.sb("cmlT", [128, NT, 2, 8], scope=sc)
    mnew = B.sb("mnew", [128, 36, 8], scope=sc)
    carry = B.sb("carry", [128, 36, 8], scope=sc)
    zer = B.sb("zer", [128, 8], scope=sc)
    B.memset(zer, 0.0)
    B.memset(V1, 1.0)
    g8 = [B.sb("g8", [128, 8], scope=sc) for _ in range(6)]
    rd_l = [B.sb("rd", [128, 8, 128], scope=sc) for _ in range(2)]
    big_l = [B.sb("big", [128, 4, 128], scope=sc) for _ in range(2)]
    with new_scope() as sc1:
        win = B.sb("win", [128, 8, 1040], BF16, scope=sc1)
        E["load_w"](win, io["w_in"][l].rr("(k p) n -> p k n", p=128)[:, :, 2064:3104])
        rings1 = [make_ring(E["psA"]), make_ring(E["psB"])]
        hs2 = [B.sb("hct", [128, 8, 512], BF16, scope=sc1) for _ in range(2)]
        E["loadh"](0, hs2[0])
        for ci in range(5):
            a, b = CTS[ci]
            w = b - a
            h = hs2[ci % 2]
            if ci < 4:
                E["loadh"](ci + 1, hs2[(ci + 1) % 2])
            for r in range(4):
                pp = PA()
                for k in range(8):
                    B.mm(pp[:, :w], win[:, k, r * 128:(r + 1) * 128], h[:, k, :w], start=(k == 0), stop=(k == 7))
                if r < 2:
                    B.act(QT[:, r, a:b], pp[:, :w], AF.Copy, scale=0.125)
                else:
                    B.act(KT[:, r - 2, a:b], pp[:, :w], AF.Copy)
            def tile_gen(slot, ti, h=h, a=a):
                PS = rings1[slot]
                t0 = ti * 128
                t = (a + t0) // 128
                p1, p2 = PS(), PS()
                for k in range(8):
                    B.mm(p1[:, :512], h[:, k, t0:t0 + 128], win[:, k, 256:768], start=(k == 0), stop=(k == 7))
                for k in range(8):
                    B.mm(p2[:, :272], h[:, k, t0:t0 + 128], win[:, k, 768:1040], start=(k == 0), stop=(k == 7))
                yield
                B.act(Ktok[:, t], p1[:, 0:256].rr("p (h d) -> p h d", d=64), AF.Copy)
                B.act(V1[:, t, :, 0:64], p1[:, 256:512].rr("p (h d) -> p h d", d=64), AF.Copy)
                B.act(Og[:, t, :], p2[:, 0:256], AF.Sigmoid)
                ig, fx, lf = g8[slot * 3], g8[slot * 3 + 1], g8[slot * 3 + 2]
                rd = rd_l[slot]
                B.tt(ig, p2[:, 256:264], pbc[:, PB_IGB:PB_IGB + 8], ADD)
                B.tt(fx, p2[:, 264:272], pbc[:, PB_FGB:PB_FGB + 8], ADD)
                yield
                B.act(fx, fx, AF.Exp, scale=-1.0)
                yield
                B.act(lf, fx, AF.Ln, bias=E["oneT"])
                yield
                pg = PS()
                B.mm(pg[:, 0:4], CM(C_CUM0), lf[:, 0:4])
                B.mm(pg[:, 4:8], CM(C_CUM1), lf[:, 4:8])
                B.mm(pg[:, 8:16], CM(C_IND0), lf)
                B.mm(pg[:, 16:24], CM(C_IND1), lf)
                yield
                B.copy(nbT[:, t, :], pg[:, 0:8])
                B.copy(nblT[:, t], pg[:, 8:24].rr("p (c n) -> p c n", n=8))
                B.tt(colT[:, t, :], ig, pg[:, 0:8], ADD)
                yield
                B.tt(rd, colT[:, t, :].unsq(2).bc([128, 8, 128]), ident.unsq(1).bc([128, 8, 128]), MUL)
                yield
                for d in range(2):
                    big = big_l[slot]
                    pc = PS()
                    B.mm(pc[:, :512], ones, rd[:, d * 4:(d + 1) * 4, :])
                    B.tt(big, pc[:, :512].rr("p (h j) -> p h j", j=128),
                         CM(C_MN1 if d == 0 else C_MN0).unsq(1).bc([128, 4, 128]), ADD)
                    B.red(cmT[:, t, d * 4:(d + 1) * 4], big, MAX)
                yield
                pl = PS()
                for cl in range(2):
                    for d in range(2):
                        sel = (C_S63, C_S127)[cl] if d == 0 else (C_S0, C_S64)[cl]
                        B.mm(pl[:, cl * 8 + d * 4:cl * 8 + d * 4 + 4], CM(sel), cmT[:, t, d * 4:(d + 1) * 4])
                yield
                B.copy(cmlT[:, t], pl[:, 0:16].rr("p (c n) -> p c n", n=8))
            lockstep([(lambda slot, ti=ti: tile_gen(slot, ti)) for ti in range(w // 128)], 2)
        B.barrier()
    acc = B.sb("acc", [128, NT, 4, 64], scope=sc)
    tmp4 = B.sb("tmp4", [128, 4], scope=sc)
    mst = {}
    for d in range(2):
        d4 = slice(d * 4, d * 4 + 4)
        order = list(range(36)) if d == 0 else [3, 2, 1, 0] + list(range(35, 3, -1))
        prev = zer[:, d4]
        for c in order:
            t, cl = divmod(c, 2)
            mst[(c, d)] = prev
            B.tt(tmp4, prev, cmlT[:, t, cl, d4], MAX)
            B.tt(mnew[:, c, d4], tmp4, nblT[:, t, cl, d4], SUB)
            B.tt(tmp4, prev, nblT[:, t, cl, d4], SUB)
            B.tt(tmp4, tmp4, mnew[:, c, d4], SUB)
            B.act(carry[:, c, d4], tmp4, AF.Exp)
            prev = mnew[:, c, d4]
    def dirbufs():
        o = {}
        o["St32"] = B.sb("St32", [128, 2, 66], scope=sc)
        o["Stb"] = B.sb("Stb", [128, 2, 66], BF16, scope=sc)
        for nm in ("mstk", "mnwk", "nblk", "inter", "mt", "rowt", "wint", "emt", "kwf", "dd"):
            o[nm] = B.sb(nm, [128, 4], scope=sc)
        o["ET"] = B.sb("ET", [128, 4, 128], scope=sc)
        o["SmT"] = B.sb("SmT", [128, 4, 128], BF16, scope=sc)
        o["kw"] = B.sb("kw", [128, 4, 64], BF16, scope=sc)
        o["t1"] = B.sb("t1", [128, 4, 66], scope=sc)
        o["nd"] = B.sb("nd", [128, 4, 66], scope=sc)
        o["ho"] = B.sb("ho", [128, 4, 64], scope=sc)
        o["rdd"] = B.sb("rdd", [128, 4, 128], scope=sc)
        return o

    DB = [dirbufs(), dirbufs()]
    B.memset(acc, 0.0)
    for d in range(2):
        B.memset(DB[d]["St32"], 0.0)
        B.memset(DB[d]["Stb"], 0.0)

    rings = [make_ring(E["psA"]), make_ring(E["psB"])]

    def tile_step(d, t):
        o = DB[d]
        PS = rings[d]
        St32, Stb, ET, SmT, kw, t1, nd, ho = (o[k] for k in ("St32", "Stb", "ET", "SmT", "kw", "t1", "nd", "ho"))
        mstk, mnwk, nblk, inter, mt, rowt, wint, emt, kwf, dd = (o[k] for k in (
            "mstk", "mnwk", "nblk", "inter", "mt", "rowt", "wint", "emt", "kwf", "dd"))
        d4 = slice(d * 4, d * 4 + 4)
        mneg = CM(C_MN0 if d == 0 else C_MN1)
        tok0 = t * 128
        for cl in range(2):
            c = t * 2 + cl
            hs_ = slice(cl * 64, cl * 64 + 64)
            B.copy(mstk[hs_], mst[(c, d)][hs_])
            B.copy(mnwk[hs_], mnew[hs_, c, d4])
            B.copy(nblk[hs_], nblT[hs_, t, cl, d4])
        nb = nbT[:, t, d4]
        colt = colT[:, t, d4]
        B.tt(inter, mstk, nb, SUB)
        B.tt(mt, cmT[:, t, d4], nb, SUB)
        B.tt(mt, mt, inter, MAX)
        B.stt(rowt, nb, -1.0, mt, MUL, SUB)
        B.tt(wint, inter, mt, SUB)
        B.act(wint, wint, AF.Exp)
        B.act(emt, mt, AF.Exp, scale=-1.0)
        B.tt(kwf, colt, nblk, SUB)
        B.tt(kwf, kwf, mnwk, SUB)
        B.act(kwf, kwf, AF.Exp)
        rdd = o["rdd"]
        B.tt(rdd, rowt.unsq(2).bc([128, 4, 128]), ident.unsq(1).bc([128, 4, 128]), MUL)
        yield
        pe = PS()
        B.mm(pe[:, :512], ones, rdd)
        for hh in range(4):
            B.stt(ET[:, hh, :], pe[:, hh * 128:(hh + 1) * 128], colT[:, t, d * 4 + hh:d * 4 + hh + 1], mneg, ADD, MIN)
        yield
        B.act(ET, ET, AF.Exp)
        yield
        pkp = (PS(), PS())
        for hh in range(4):
            hb = (hh % 2) * 64
            B.mm(pkp[hh % 2][:, hh * 128:(hh + 1) * 128], KT[hb:hb + 64, hh // 2, tok0:tok0 + 128],
                 QT[hb:hb + 64, hh // 2, tok0:tok0 + 128])
        yield
        for par in range(2):
            B.tt(SmT[:, par::2, :], pkp[par][:, :512].rr("p (h j) -> p h j", j=128)[:, par::2, :], ET[:, par::2, :], MUL)
        yield
        pi = PS()
        for hh in range(4):
            B.mm(pi[:, hh * 66:hh * 66 + 65], SmT[:, hh, :], V1[:, t, hh, 0:65])
        yield
        B.tt(kw, Ktok[:, t], kwf.unsq(2).bc([128, 4, 64]), MUL)
        yield
        pinp = (PS(), PS())
        pu = PS()
        for cl in ((0, 1) if d == 0 else (1, 0)):
            c = t * 2 + cl
            cb = cl * 64
            for hh in range(4):
                hb, hp = (hh % 2) * 64, hh // 2
                B.mm(pinp[hh % 2][cb:cb + 64, hh * 66:hh * 66 + 65], QT[hb:hb + 64, hp, tok0 + cb:tok0 + cb + 64],
                     Stb[hb:hb + 64, hp, 0:65])
            yield
            for hh in range(4):
                hb, hp = (hh % 2) * 64, hh // 2
                B.mm(pu[hb:hb + 64, hp * 66:hp * 66 + 65], kw[cb:cb + 64, hh, :], V1[cb:cb + 64, t, hh, 0:65])
            yield
            for hh in range(4):
                hb, hp = (hh % 2) * 64, hh // 2
                B.stt(St32[hb:hb + 64, hp, 0:65], St32[hb:hb + 64, hp, 0:65], carry[hb:hb + 64, c, d * 4 + hh:d * 4 + hh + 1],
                      pu[hb:hb + 64, hp * 66:hp * 66 + 65], MUL, ADD)
            B.act(Stb, St32, AF.Copy)
        yield
        for par in range(2):
            B.tt(t1[:, par::2, 0:65], pinp[par][:, 0:264].rr("p (h e) -> p h e", e=66)[:, par::2, 0:65],
                 wint[:, par::2].unsq(2).bc([128, 2, 65]), MUL)
        B.tt(nd[:, :, 0:65], t1[:, :, 0:65], pi[:, 0:264].rr("p (h e) -> p h e", e=66)[:, :, 0:65], ADD)
        B.stt(dd, nd[:, :, 64], -1.0, nd[:, :, 64], MUL, MAX)
        B.tt(dd, dd, emt, MAX)
        B.recip(dd, dd)
        B.tt(ho, nd[:, :, 0:64], dd.unsq(2).bc([128, 4, 64]), MUL)
        B.tt(acc[:, t], acc[:, t], ho, ADD, eng="pool")

    order = [list(range(NT)), [1, 0] + list(range(NT - 1, 1, -1))]
    def dir_gen(d):
        for t in order[d]:
            yield from tile_step(d, t)

    lockstep([lambda slot: dir_gen(0), lambda slot: dir_gen(1)], 2)
    ho = DB[0]["ho"]
    dd = DB[0]["dd"]
    lnw = pbc[:, PB_LNW:PB_LNW + 256].rr("p (h d) -> p h d", d=64)
    hn = B.sb("hn", [128, 4, 64], scope=sc)
    for t in range(NT):
        B.tt(ho, acc[:, t], acc[:, t], MUL)
        B.red(dd, ho, ADD)
        B.act(dd, dd, AF.Sqrt, scale=1.0 / 64, bias=E["epsT"])
        B.recip(dd, dd)
        B.tt(hn, acc[:, t], dd.unsq(2).bc([128, 4, 64]), MUL)
        B.tt(hn, hn, lnw, MUL)
        B.tt(hn, hn, Og[:, t, :].rr("p (h d) -> p h d", d=64), MUL)
        for pc_ in range(2):
            pt = PB()
            B.tr(pt[:, :128], hn[:, pc_ * 2:pc_ * 2 + 2, :].rr("p h d -> p (h d)"), ident)
            B.act(mixT[:, pc_, t * 128:(t + 1) * 128], pt[:, :128], AF.Copy)
    if "dbg" in io:
        io["dbg"][(l, "mlstm")] = np.array(mixT.a, dtype=np.float32)


def mixer_gdn(E):
    B, io, l, S, PA, PB, CM = E["B"], E["io"], E["l"], E["S"], E["PA"], E["PB"], E["CM"]
    sc = S["sc"]
    MUL, ADD, SUB, MAX, MIN = ALU.mult, ALU.add, ALU.subtract, ALU.max, ALU.min
    ident, ones, pfm = E["ident"], E["ones"], E["pfm"]
    new_scope = E["new_scope"]
    pbc = S["pbc"]
    QT = B.sb("QT", [128, 2, TOK], BF16, scope=sc)
    KT = B.sb("KT", [128, 2, TOK], BF16, scope=sc)
    Ktok = B.sb("Ktok", [128, NT, 4, 64], BF16, scope=sc)
    Vtok = B.sb("Vtok", [128, NT, 4, 64], BF16, scope=sc)
    Zg = B.sb("Zg", [128, NT, 256], BF16, scope=sc)
    ngcT = B.sb("ngcT", [128, NT, 8], scope=sc)
    betaT = B.sb("betaT", [128, NT, 8], scope=sc)
    nglT = B.sb("nglT", [128, NT, 2, 8], scope=sc)
    glT = B.sb("glT", [128, NT, 2, 8], scope=sc)
    g8s = [[B.sb("g8", [128, 8], scope=sc) for _ in range(2)] for _ in range(2)]
    rings1 = [make_ring(E["psA"]), make_ring(E["psB"])]
    eal = B.sb("eal", [128, 8], scope=sc)
    B.act(eal, pbc[:, PB_ALOG:PB_ALOG + 8], AF.Exp)
    with new_scope() as sc1:
        win = B.sb("win", [128, 8, 1040], BF16, scope=sc1)
        E["load_w"](win, io["w_in"][l].rr("(k p) n -> p k n", p=128)[:, :, 0:1040])
        hh_ = [B.sb("hct", [128, 8, 512], BF16, scope=sc1) for _ in range(2)]
        raw = B.sb("raw", [128, 2312], scope=sc1)
        cacc = B.sb("cacc", [128, TOK], scope=sc1)
        sq_l = [B.sb("sq", [128, 512], scope=sc1) for _ in range(2)]
        rin_l = [B.sb("rin", [128, 512], scope=sc1) for _ in range(2)]
        nrings = [make_ring(E["psB"][0:2]), make_ring(E["psB"][2:4])]
        B.memset(raw, 0.0)
        hc = 0
        for r in range(6):
            for ci in range(5):
                a, b = CTS[ci]
                w = b - a
                h = hh_[hc % 2]
                hc += 1
                E["loadh"](ci, h)
                pp = PA()
                for k in range(8):
                    B.mm(pp[:, :w], win[:, k, r * 128:(r + 1) * 128], h[:, k, :w], start=(k == 0), stop=(k == 7))
                off = 2 if ci == 0 else 262 + (a - 256)
                B.act(raw[:, off:off + w], pp[:, :w], AF.Copy)
            cw = lambda tap: pfm[:, l * PF_L + 64 + tap * 6 + r:l * PF_L + 64 + tap * 6 + r + 1]
            for (o0, t0, n) in ((0, 0, 256), (260, 256, 2048)):
                B.ts(cacc[:, t0:t0 + n], raw[:, o0:o0 + n], cw(0), MUL)
                for tap in range(1, 5):
                    B.stt(cacc[:, t0:t0 + n], raw[:, o0 + tap:o0 + tap + n], cw(tap), cacc[:, t0:t0 + n], MUL, ADD)
            B.act(cacc, cacc, AF.Silu)
            if r < 4:
                def norm_gen(slot, ci, r=r):
                    sq, rin = sq_l[slot], rin_l[slot]
                    PS = nrings[slot]
                    a, b = CTS[ci]
                    w = b - a
                    B.act(sq[:, :w], cacc[:, a:b], AF.Square)
                    yield
                    pq_ = PS()
                    B.mm(pq_[:, :w], CM(C_BLK), sq[:, :w])
                    yield
                    B.act(rin[:, :w], pq_[:, :w], AF.Sqrt, bias=E["epsT"])
                    yield
                    B.recip(rin[:, :w], rin[:, :w])
                    yield
                    dst = (QT if r < 2 else KT)[:, r % 2, a:b]
                    B.stt(dst, cacc[:, a:b], 0.125 if r < 2 else 1.0, rin[:, :w], MUL, MUL)
                    if r >= 2:
                        B.tt(sq[:, :w], cacc[:, a:b], rin[:, :w], MUL)
                        yield
                        for ti in range(w // 128):
                            pt = PS()
                            B.tr(pt[:, :128], sq[:, ti * 128:(ti + 1) * 128], ident)
                            yield
                            B.act(Ktok[:, (a + ti * 128) // 128, (r - 2) * 2:(r - 2) * 2 + 2, :],
                                  pt[:, :128].rr("p (h d) -> p h d", d=64), AF.Copy)
                            yield

                lockstep([(lambda slot, ci=ci: norm_gen(slot, ci)) for ci in range(5)], 2)
            else:
                for t in range(NT):
                    pt = PB()
                    B.tr(pt[:, :128], cacc[:, t * 128:(t + 1) * 128], ident)
                    B.act(Vtok[:, t, (r - 4) * 2:(r - 4) * 2 + 2, :], pt[:, :128].rr("p (h d) -> p h d", d=64), AF.Copy)
        for ci in range(5):
            a, b = CTS[ci]
            w = b - a
            h = hh_[hc % 2]
            hc += 1
            E["loadh"](ci, h)
            def tile_gen(slot, ti, h=h, a=a):
                PS = rings1[slot]
                t0 = ti * 128
                t = (a + t0) // 128
                p2 = PS()
                for k in range(8):
                    B.mm(p2[:, :272], h[:, k, t0:t0 + 128], win[:, k, 768:1040], start=(k == 0), stop=(k == 7))
                yield
                B.act(Zg[:, t, :], p2[:, 0:256], AF.Silu)
                xa, ng = g8s[slot][0], g8s[slot][1]
                B.tt(xa, p2[:, 256:264], pbc[:, PB_DTB:PB_DTB + 8], ADD)
                yield
                B.act(xa, xa, AF.Exp)
                yield
                B.act(xa, xa, AF.Ln, bias=E["oneT"])
                yield
                B.tt(ng, xa, eal, MUL)
                B.act(betaT[:, t, :], p2[:, 264:272], AF.Sigmoid)
                yield
                pg = PS()
                B.mm(pg[:, 0:4], CM(C_CUM0), ng[:, 0:4])
                B.mm(pg[:, 4:8], CM(C_CUM1), ng[:, 4:8])
                B.mm(pg[:, 8:16], CM(C_IND0), ng)
                B.mm(pg[:, 16:24], CM(C_IND1), ng)
                yield
                B.copy(ngcT[:, t, :], pg[:, 0:8])
                B.copy(nglT[:, t], pg[:, 8:24].rr("p (c n) -> p c n", n=8))
            lockstep([(lambda slot, ti=ti: tile_gen(slot, ti)) for ti in range(w // 128)], 2)
        B.barrier()
    B.act(glT, nglT, AF.Exp, scale=-1.0)
    acc = B.sb("acc", [128, NT, 4, 64], scope=sc)
    B.memset(acc, 0.0)
    H4 = lambda p: p[:, :512].rr("p (h j) -> p h j", j=128)
    with new_scope() as sc2:
        def dirbufs():
            o = {}
            o["S32"] = B.sb("S32", [128, 2, 64], scope=sc2)
            o["Sb"] = B.sb("Sb", [128, 2, 64], BF16, scope=sc2)
            for nm in ("nglk", "rowt", "egc", "negegc", "kdf", "negb"):
                o[nm] = B.sb(nm, [128, 4], scope=sc2)
            o["ET"] = B.sb("ET", [128, 4, 128], scope=sc2)
            o["Xs"] = [B.sb("X", [128, 4, 128], scope=sc2) for _ in range(2)]
            o["XTs"] = [B.sb("XT", [128, 4, 128], scope=sc2) for _ in range(2)]
            o["Ps"] = [B.sb("P", [128, 4, 128], scope=sc2) for _ in range(2)]
            o["attnT"] = B.sb("attnT", [128, 4, 128], BF16, scope=sc2)
            o["kdec"] = B.sb("kdec", [128, 4, 64], BF16, scope=sc2)
            o["Rp"] = B.sb("Rp", [128, 4, 64], scope=sc2)
            o["vn"] = B.sb("vn", [128, 4, 64], BF16, scope=sc2)
            o["t1"] = B.sb("t1", [128, 4, 64], scope=sc2)
            o["ho"] = B.sb("ho", [128, 4, 64], scope=sc2)
            return o

        DB = [dirbufs(), dirbufs()]
        for d in range(2):
            B.memset(DB[d]["S32"], 0.0)
            B.memset(DB[d]["Sb"], 0.0)

        rings = [make_ring(E["psA"]), make_ring(E["psB"])]

        def tile_step(d, t):
            o = DB[d]
            PS = rings[d]
            S32, Sb, ET, Xs, XTs, Ps, attnT, kdec, Rp, vn, t1, ho = (o[k] for k in (
                "S32", "Sb", "ET", "Xs", "XTs", "Ps", "attnT", "kdec", "Rp", "vn", "t1", "ho"))
            nglk, rowt, egc, negegc, kdf, negb = (o[k] for k in ("nglk", "rowt", "egc", "negegc", "kdf", "negb"))
            d4 = slice(d * 4, d * 4 + 4)
            mneg = CM(C_MN0 if d == 0 else C_MN1)
            tok0 = t * 128
            ngc = ngcT[:, t, d4]
            for cl in range(2):
                hs_ = slice(cl * 64, cl * 64 + 64)
                B.copy(nglk[hs_], nglT[hs_, t, cl, d4])
            B.ts(rowt, ngc, -1.0, MUL)
            B.act(egc, ngc, AF.Exp, scale=-1.0)
            B.ts(negegc, egc, -1.0, MUL)
            B.tt(kdf, ngc, nglk, SUB)
            B.act(kdf, kdf, AF.Exp)
            B.ts(negb, betaT[:, t, d4], -1.0, MUL)
            rd = Ps[0]
            tmpA = Xs[1]
            B.tt(rd, rowt.unsq(2).bc([128, 4, 128]), ident.unsq(1).bc([128, 4, 128]), MUL)
            yield
            pe = PS()
            B.mm(pe[:, :512], ones, rd)
            for hh in range(4):
                B.stt(ET[:, hh, :], pe[:, hh * 128:(hh + 1) * 128], ngcT[:, t, d * 4 + hh:d * 4 + hh + 1], mneg, ADD, MIN)
            yield
            B.act(ET, ET, AF.Exp)
            yield
            pkk, pkq = (PS(), PS()), (PS(), PS())
            for hh in range(4):
                hb, hp = (hh % 2) * 64, hh // 2
                kt_ = KT[hb:hb + 64, hp, tok0:tok0 + 128]
                B.mm(pkk[hh % 2][:, hh * 128:(hh + 1) * 128], kt_, kt_)
                B.mm(pkq[hh % 2][:, hh * 128:(hh + 1) * 128], kt_, QT[hb:hb + 64, hp, tok0:tok0 + 128])
            yield
            for par in range(2):
                B.tt(attnT[:, par::2, :], H4(pkq[par])[:, par::2, :], ET[:, par::2, :], MUL)
                B.tt(tmpA[:, par::2, :], H4(pkk[par])[:, par::2, :], ET[:, par::2, :], MUL)
            yield
            X, XT, P = Xs[0], XTs[0], Ps[0]
            for hh in range(4):
                B.stt(X[:, hh, :], tmpA[:, hh, :], negb[:, hh:hh + 1], CM(C_OFFD), MUL, MUL)
            yield
            ptr = PS()
            for hh in range(4):
                B.tr(ptr[:, hh * 128:(hh + 1) * 128], X[:, hh, :], ident)
            yield
            B.act(XT, H4(ptr), AF.Copy)
            B.tt(P, X, ident.unsq(1).bc([128, 4, 128]), ADD)
            for lev in range(5):
                X2, XT2, P2 = Xs[(lev + 1) % 2], XTs[(lev + 1) % 2], Ps[(lev + 1) % 2]
                yield
                pXT = PS()
                for hh in range(4):
                    B.mm(pXT[:, hh * 128:(hh + 1) * 128], X[:, hh, :], XT[:, hh, :])
                yield
                B.act(XT2, H4(pXT), AF.Copy)
                if lev < 4:
                    pX = PS()
                    for hh in range(4):
                        B.mm(pX[:, hh * 128:(hh + 1) * 128], XT[:, hh, :], X[:, hh, :])
                    B.act(X2, H4(pX), AF.Copy)
                yield
                pP = PS()
                for hh in range(4):
                    B.mm(pP[:, hh * 128:(hh + 1) * 128], XT2[:, hh, :], P[:, hh, :])
                yield
                B.tt(P2, H4(pP), P, ADD)
                X, XT, P = X2, XT2, P2
            yield
            B.tt(kdec, Ktok[:, t], kdf.unsq(2).bc([128, 4, 64]), MUL)
            for cl in ((0, 1) if d == 0 else (1, 0)):
                cb = cl * 64
                cs = slice(cb, cb + 64)
                yield
                pks = (PS(), PS())
                for hh in range(4):
                    hb, hp = (hh % 2) * 64, hh // 2
                    B.mm(pks[hh % 2][cs, hh * 64:(hh + 1) * 64], KT[hb:hb + 64, hp, tok0 + cb:tok0 + cb + 64], Sb[hb:hb + 64, hp, :])
                yield
                for hh in range(4):
                    B.stt(Rp[cs, hh, :], pks[hh % 2][cs, hh * 64:(hh + 1) * 64], negegc[cs, hh:hh + 1], Vtok[cs, t, hh, :], MUL, ADD)
                yield
                pv = PS()
                for hh in range(4):
                    B.mm(pv[cs, hh * 64:(hh + 1) * 64], P[cs, hh, cb:cb + 64], Rp[cs, hh, :])
                yield
                for hh in range(4):
                    B.act(vn[cs, hh, :], pv[cs, hh * 64:(hh + 1) * 64], AF.Copy, scale=betaT[cs, t, d * 4 + hh:d * 4 + hh + 1])
                yield
                pq, pa = (PS(), PS()), PS()
                for hh in range(4):
                    hb, hp = (hh % 2) * 64, hh // 2
                    B.mm(pq[hh % 2][cs, hh * 64:(hh + 1) * 64], QT[hb:hb + 64, hp, tok0 + cb:tok0 + cb + 64], Sb[hb:hb + 64, hp, :])
                    B.mm(pa[cs, hh * 64:(hh + 1) * 64], attnT[cs, hh, cb:cb + 64], vn[cs, hh, :])
                yield
                for par in range(2):
                    B.tt(t1[cs, par::2, :], pq[par][cs, 0:256].rr("p (h e) -> p h e", e=64)[:, par::2, :],
                         egc[cs, par::2].unsq(2).bc([64, 2, 64]), MUL)
                B.tt(ho[cs], t1[cs], pa[cs, 0:256].rr("p (h e) -> p h e", e=64), ADD)
                B.tt(acc[cs, t], acc[cs, t], ho[cs], ADD, eng="pool")
                yield
                pu = PS()
                for hh in range(4):
                    hb, hp = (hh % 2) * 64, hh // 2
                    B.mm(pu[hb:hb + 64, hp * 64:(hp + 1) * 64], kdec[cs, hh, :], vn[cs, hh, :])
                yield
                for hh in range(4):
                    hb, hp = (hh % 2) * 64, hh // 2
                    B.stt(S32[hb:hb + 64, hp, :], S32[hb:hb + 64, hp, :], glT[hb:hb + 64, t, cl, d * 4 + hh:d * 4 + hh + 1],
                          pu[hb:hb + 64, hp * 64:(hp + 1) * 64], MUL, ADD)
                B.act(Sb, S32, AF.Copy)

        order = [list(range(NT)), [1, 0] + list(range(NT - 1, 1, -1))]
        def dir_gen(d):
            for t in order[d]:
                yield from tile_step(d, t)

        lockstep([lambda slot: dir_gen(0), lambda slot: dir_gen(1)], 2)
        B.barrier()
    mixT = S["mixT"] = B.sb("mixT", [128, 2, TOK], BF16, scope=sc)
    ho = B.sb("ho", [128, 4, 64], scope=sc)
    dd = B.sb("dd", [128, 4], scope=sc)
    gnw = pbc[:, PB_GNW:PB_GNW + 64].unsq(1).bc([128, 4, 64])
    hn = B.sb("hn", [128, 4, 64], scope=sc)
    for t in range(NT):
        B.tt(ho, acc[:, t], acc[:, t], MUL)
        B.red(dd, ho, ADD)
        B.act(dd, dd, AF.Sqrt, scale=1.0 / 64, bias=E["epsT"])
        B.recip(dd, dd)
        B.tt(hn, acc[:, t], dd.unsq(2).bc([128, 4, 64]), MUL)
        B.tt(hn, hn, gnw, MUL)
        B.tt(hn, hn, Zg[:, t, :].rr("p (h d) -> p h d", d=64), MUL)
        for pc_ in range(2):
            pt = PB()
            B.tr(pt[:, :128], hn[:, pc_ * 2:pc_ * 2 + 2, :].rr("p h d -> p (h d)"), ident)
            B.act(mixT[:, pc_, t * 128:(t + 1) * 128], pt[:, :128], AF.Copy)
    if "dbg" in io:
        io["dbg"][(l, "gdn")] = np.array(mixT.a, dtype=np.float32)


L_DEPTH = 4
IO_SPECS = [
    ("xT", [1024, 2048]), ("ctxT", [1024, 256]), ("cvT", [128, 8, 2]), ("pfm", [128, L_DEPTH * PF_L + 8]),
    ("pbc", [L_DEPTH, PB_L]), ("consts", [128, NCONST * 128]), ("ropec", [128, 2048]), ("ropes", [128, 2048]),
    ("gmlp_wsT", [L_DEPTH, 4, 128, 128]), ("gmlp_bsr", [L_DEPTH, 128, 2, 128]),
    ("ada_w", [L_DEPTH, 1024, 6144]), ("w_in", [L_DEPTH, 1024, 3104]), ("w_out", [L_DEPTH, 1024, 1024]),
    ("mlp_w1", [L_DEPTH, 1024, 4096]), ("mlp_w2", [L_DEPTH, 4096, 1024]),
]


def prep_shared(inp):
    f = lambda a: np.ascontiguousarray(np.asarray(a, dtype=np.float32))
    L = L_DEPTH
    pfm = np.zeros((128, L * PF_L + 8), np.float32)
    pbc = np.zeros((L, PB_L), np.float32)
    fm = lambda v: f(v).reshape(-1, 128).T
    for l in range(L):
        o = l * PF_L
        pfm[:, o:o + 8] = fm(inp["norm1_w"][l])
        pfm[:, o + 8:o + 16] = fm(inp["norm2_w"][l])
        pfm[:, o + 16:o + 64] = fm(inp["ada_b"][l])
        pfm[:, o + 64:o + 94] = f(inp["gdn_conv_w"][l]).reshape(5, 6, 128).transpose(2, 0, 1).reshape(128, 30)
        pbc[l] = np.concatenate([f(inp["gdn_a_log"][l]).ravel(), f(inp["gdn_dt_bias"][l]).ravel(),
                                 f(inp["gdn_norm_w"][l]).ravel(), f(inp["swa_sink"][l]).ravel(),
                                 f(inp["gmlp_norm_w"][l]).ravel(), f(inp["mlstm_ig_bias"][l]).ravel(),
                                 f(inp["mlstm_fg_bias"][l]).ravel(), f(inp["mlstm_norm_w"][l]).ravel()])
    pfm[:, L * PF_L:] = fm(inp["final_norm_w"])
    bs = f(inp["gmlp_b_s"])
    bsr = np.repeat(bs.reshape(L, 2, 2, 1, 128), 64, axis=3)
    bsr = bsr.transpose(0, 2, 3, 1, 4).reshape(L, 128, 2, 128)
    rc, rs = make_rope()
    return {"pfm": pfm, "pbc": pbc, "consts": make_consts(), "ropec": rc, "ropes": rs,
            "gmlp_wsT": f(np.asarray(inp["gmlp_w_s"]).transpose(0, 1, 3, 2)), "gmlp_bsr": f(bsr),
            "ada_w": f(inp["ada_w"]), "w_in": f(inp["w_in"]), "w_out": f(inp["w_out"]),
            "mlp_w1": f(inp["mlp_w1"]), "mlp_w2": f(inp["mlp_w2"])}


def prep_core(inp, b):
    f = lambda a: np.ascontiguousarray(np.asarray(a, dtype=np.float32))
    cv = np.stack([np.asarray(inp["c"])[b], np.asarray(inp["c_ctx"])], axis=-1)
    return {"xT": f(np.asarray(inp["x"])[b].T), "ctxT": f(np.asarray(inp["ctx"])[b].T),
            "cvT": f(cv.reshape(8, 128, 2).transpose(1, 0, 2))}


def build_nc(**kw):
    nc = bass.Bass("TRN2", target_bir_lowering=False)
    with ExitStack() as es:
        B = BassBackend(nc, es)
        io = {}
        for name, shape in IO_SPECS:
            io[name] = B.dram(name, shape, F32, "ExternalInput")
        io["outT"] = B.dram("outT", [1024, 2048], F32, "ExternalOutput")
        io["hd"] = B.dram("hd_scratch", [128, 8, TOK], BF16, "Internal")
        build_model(B, io, **kw)
        print("instructions:", B.n_ins, {e: B.seq[e] for e in B.seq}, flush=True)
    return nc


def kernel(**inputs):
    shared = prep_shared(inputs)
    nc = build_nc()
    in_maps = []
    for b in range(8):
        m = dict(shared)
        m.update(prep_core(inputs, b))
        in_maps.append(m)
    res = run_bass_kernel_spmd(nc, in_maps, core_ids=list(range(8)))
    out = np.stack([np.asarray(r["outT"]).T for r in res.results], axis=0)
    return np.ascontiguousarray(out.astype(np.float32))
```
